# Optimizing a Trainium2 kernel written in Bass

```python
import jax, jax.numpy as jnp
from jax import lax
import numpy as np

D_MODEL = 2048
BATCH = 4
SEQ = 8192
DEPTH = 1
DEC_BATCH = 32
DEC_SEQ = 64
PAST_LEN = 1024

CHUNK = 64
D_MIX = D_MODEL
D_A = D_MIX // 2
D_B = D_MIX - D_A
SGU_CHUNK = 128
A_HEADS = 8
A_HEAD_DIM = D_A // A_HEADS
B_HEAD_DIM = 64
B_HEADS = D_B // B_HEAD_DIM
LORA_W = 64
LORA_A = 64
N_SHIFT = 3 * D_B + LORA_W + LORA_A
D_PROJ = 3 * D_A + N_SHIFT + D_B
NORM_EPS = 1e-6
LN_EPS = 1e-5
GN_EPS = 64e-5
W_OFFSET = 0.5

kernel_name = "hybrid_gmlp_rwkv7_stream_step"


def rms_norm(x, g):
    xf = x.astype(jnp.float32)
    y = xf * lax.rsqrt(jnp.mean(xf * xf, axis=-1, keepdims=True) + NORM_EPS)
    return (y * g.astype(jnp.float32)).astype(x.dtype)


def layer_norm(x, g, b):
    xf = x.astype(jnp.float32)
    mu = jnp.mean(xf, axis=-1, keepdims=True)
    var = jnp.mean(jnp.square(xf - mu), axis=-1, keepdims=True)
    y = (xf - mu) * lax.rsqrt(var + LN_EPS)
    return (y * g.astype(jnp.float32) + b.astype(jnp.float32)).astype(x.dtype)


def token_shift(z, prev, mu):
    z_prev = jnp.concatenate([prev.astype(z.dtype), z[:, :-1]], axis=1)
    return z + mu * (z_prev - z)


def sgu_mix(vn, w_s, b_s):
    L = vn.shape[2]
    mask = jnp.tril(jnp.ones((L, L), dtype=bool))
    w = jnp.where(mask[None], w_s[:, :L, :L], jnp.zeros((), w_s.dtype)).astype(vn.dtype)
    f = jnp.einsum('hij,bcjhd->bcihd', w, vn)
    return f + b_s[:, :L].T[None, None, :, :, None].astype(vn.dtype)


def wkv7_scan(S0, r, w, k, v, kk, b):
    def step(S, inp):
        r_t, w_t, k_t, v_t, kk_t, b_t = inp
        sa = -jnp.einsum('bhvk,bhk->bhv', S, kk_t)
        S = (S * w_t[:, :, None, :] + sa[..., None] * b_t[:, :, None, :]
             + v_t[..., None] * k_t[:, :, None, :])
        y = jnp.einsum('bhvk,bhk->bhv', S, r_t)
        return S, y
    xs = tuple(jnp.swapaxes(t, 0, 1) for t in (r, w, k, v, kk, b))
    S, ys = lax.scan(step, S0, xs)
    return jnp.swapaxes(ys, 0, 1), S


def hybrid_layer(x, wkv0, shift0, norm_g, w_in, w_out, sgu_ln_g, sgu_ln_b, sgu_w, sgu_b,
                 shift_mu, w0, w2, a0, a2, k_k, k_a, r_k, gn_g, gn_b):
    bsz, t, _ = x.shape
    f32 = jnp.float32
    h = rms_norm(x, norm_g)
    z = h @ w_in
    za, zs, g_b = jnp.split(z, [3 * D_A, 3 * D_A + N_SHIFT], axis=-1)

    u, v, g_a = jnp.split(za, 3, axis=-1)
    vn = layer_norm(jax.nn.gelu(v, approximate=False), sgu_ln_g, sgu_ln_b)
    L = min(t, SGU_CHUNK)
    f = sgu_mix(vn.reshape(bsz, t // L, L, A_HEADS, A_HEAD_DIM), sgu_w, sgu_b).reshape(bsz, t, D_A)
    out_a = jax.nn.gelu(u, approximate=False) * f * jax.nn.silu(g_a)

    new_shift = zs[:, -1:]
    zs_mix = token_shift(zs, shift0, shift_mu).astype(f32)
    r, k, vb, hw, ha = jnp.split(zs_mix, [D_B, 2 * D_B, 3 * D_B, 3 * D_B + LORA_W], axis=-1)
    d = w0.astype(f32) + jnp.tanh(hw) @ w2.astype(f32)
    decay = jnp.exp(-jnp.exp(-jax.nn.softplus(-d) - W_OFFSET))
    a = jax.nn.sigmoid(a0.astype(f32) + ha @ a2.astype(f32))
    hs = (bsz, t, B_HEADS, B_HEAD_DIM)
    kk = (k * k_k.astype(f32)).reshape(hs)
    kk = kk / jnp.maximum(jnp.linalg.norm(kk, axis=-1, keepdims=True), 1e-12)
    k = k * (1.0 + (a - 1.0) * k_a.astype(f32))
    r_h, k_h, v_h, a_h = r.reshape(hs), k.reshape(hs), vb.reshape(hs), a.reshape(hs)
    y, S = wkv7_scan(wkv0.astype(f32), r_h, decay.reshape(hs), k_h, v_h, kk, kk * a_h)
    mu = jnp.mean(y, axis=-1, keepdims=True)
    var = jnp.mean(jnp.square(y - mu), axis=-1, keepdims=True)
    y = ((y - mu) * lax.rsqrt(var + GN_EPS)).reshape(bsz, t, D_B)
    y = y * gn_g.astype(f32) + gn_b.astype(f32)
    bonus = jnp.sum(r_h * k_h * r_k.astype(f32), axis=-1, keepdims=True) * v_h
    y = y + bonus.reshape(bsz, t, D_B)
    out_b = y.astype(x.dtype) * jax.nn.silu(g_b)

    o = jnp.concatenate([out_a, out_b], axis=-1) @ w_out
    return x + o, S, new_shift, vn


def setup_inputs(seed: int = 0) -> dict:
    key = jax.random.key(seed)
    ks = jax.random.split(key, 24)
    nrm = lambda k, s, sc: jax.random.normal(k, s, jnp.float32) * sc
    return {
        "x_prompt": nrm(ks[0], (BATCH, SEQ, D_MODEL), 1.0),
        "x_sample": nrm(ks[1], (DEC_BATCH, DEC_SEQ, D_MODEL), 1.0),
        "state_b_wkv": nrm(ks[2], (DEPTH, DEC_BATCH, B_HEADS, B_HEAD_DIM, B_HEAD_DIM), 0.5),
        "state_b_shift": nrm(ks[3], (DEPTH, DEC_BATCH, 1, N_SHIFT), 1.0),
        "norm_g": 1.0 + nrm(ks[4], (DEPTH, D_MODEL), 0.02),
        "w_in": nrm(ks[5], (DEPTH, D_MODEL, D_PROJ), D_MODEL ** -0.5),
        "w_out": nrm(ks[6], (DEPTH, D_MIX, D_MODEL), 0.5 * D_MIX ** -0.5),
        "sgu_ln_g": 1.0 + nrm(ks[7], (DEPTH, D_A), 0.02),
        "sgu_ln_b": nrm(ks[8], (DEPTH, D_A), 0.02),
        "sgu_w": nrm(ks[9], (DEPTH, A_HEADS, SGU_CHUNK, SGU_CHUNK), SGU_CHUNK ** -0.5),
        "sgu_b": 1.0 + nrm(ks[10], (DEPTH, A_HEADS, SGU_CHUNK), 0.1),
        "shift_mu": jax.random.uniform(ks[11], (DEPTH, N_SHIFT), jnp.float32),
        "w0": -3.0 + nrm(ks[12], (DEPTH, D_B), 1.0),
        "w2": nrm(ks[13], (DEPTH, LORA_W, D_B), 0.5 * LORA_W ** -0.5),
        "a0": nrm(ks[14], (DEPTH, D_B), 0.5),
        "a2": nrm(ks[15], (DEPTH, LORA_A, D_B), 0.5 * LORA_A ** -0.5),
        "k_k": 0.85 + nrm(ks[16], (DEPTH, D_B), 0.05),
        "k_a": 1.0 + nrm(ks[17], (DEPTH, D_B), 0.05),
        "r_k": nrm(ks[18], (DEPTH, B_HEADS, B_HEAD_DIM), 0.1),
        "gn_g": 1.0 + nrm(ks[19], (DEPTH, D_B), 0.02),
        "gn_b": nrm(ks[20], (DEPTH, D_B), 0.02),
        "final_g": 1.0 + nrm(ks[21], (D_MODEL,), 0.02),
    }


def reference(x_prompt, x_sample, state_b_wkv, state_b_shift, norm_g, w_in, w_out, sgu_ln_g, sgu_ln_b,
              sgu_w, sgu_b, shift_mu, w0, w2, a0, a2, k_k, k_a, r_k, gn_g, gn_b, final_g):
    yp, ys = x_prompt, x_sample
    bp = x_prompt.shape[0]
    zero_wkv = jnp.zeros((bp, B_HEADS, B_HEAD_DIM, B_HEAD_DIM), jnp.float32)
    zero_shift = jnp.zeros((bp, 1, N_SHIFT), x_prompt.dtype)
    wkv_p, shift_p, wkv_s, shift_s, sgu_v_s = [], [], [], [], []
    for l in range(DEPTH):
        lw = [p[l] for p in (norm_g, w_in, w_out, sgu_ln_g, sgu_ln_b, sgu_w, sgu_b, shift_mu,
                             w0, w2, a0, a2, k_k, k_a, r_k, gn_g, gn_b)]
        yp, S_p, sh_p, _ = hybrid_layer(yp, zero_wkv, zero_shift, *lw)
        ys, S_s, sh_s, vn_s = hybrid_layer(ys, state_b_wkv[l], state_b_shift[l], *lw)
        wkv_p.append(S_p); shift_p.append(sh_p)
        wkv_s.append(S_s); shift_s.append(sh_s); sgu_v_s.append(vn_s)
    y_prompt = rms_norm(yp, final_g)
    y_sample = rms_norm(ys, final_g)
    new_wkv_prompt = jnp.stack(wkv_p)
    new_shift_prompt = jnp.stack(shift_p)
    new_wkv_sample = jnp.stack(wkv_s)
    new_shift_sample = jnp.stack(shift_s)
    new_sgu_v_sample = jnp.stack(sgu_v_s)
    return (y_prompt, y_sample, new_wkv_prompt, new_shift_prompt, new_wkv_sample, new_shift_sample, new_sgu_v_sample)
```

```python
from concourse.bass_utils import run_bass_kernel_spmd
import numpy as np
import concourse.bass as bass
import concourse.mybir as mybir

F32 = mybir.dt.float32
BF16 = mybir.dt.bfloat16
AF = mybir.ActivationFunctionType
ALU = mybir.AluOpType

SEM_CAP = 16384


def _key(x):
    sub = None
    if isinstance(x, tuple):
        x, sub = x
    t = getattr(x, "tensor", x)
    return (t.name, sub)


class Prog:
    COMPUTE = ("pe", "act", "dve", "pool")

    def __init__(self, nc, stack):
        self.nc = nc
        self.stack = stack
        self.ops = []
        self.state = {}
        self.eng = {"pe": nc.tensor, "act": nc.scalar, "dve": nc.vector, "pool": nc.gpsimd, "sp": nc.sync}
        self.psum_names = set()
        self.bank_last = {}

    def sb(self, name, shape, dt):
        return self.stack.enter_context(self.nc.sbuf_tensor(name, list(shape), dt))

    def ps(self, name, shape, dt):
        self.psum_names.add(name)
        return self.stack.enter_context(self.nc.psum_tensor(name, list(shape), dt))

    def _states(self, key, create=True):
        name, sub = key
        d = self.state.setdefault(name, {})
        if sub is None:
            if None not in d:
                d[None] = [None, []]
            return list(d.values())
        out = []
        if sub not in d:
            d[sub] = [None, []]
        out.append(d[sub])
        if None in d:
            out.append(d[None])
        return out

    def op(self, eng, fn, reads=(), writes=(), dma_tag=None, wait_all=False, cost=100.0, lat=0.0):
        idx = len(self.ops)
        deps = {}
        rk = [_key(r) for r in reads if r is not None and not isinstance(r, (int, float))]
        wk = [_key(w) for w in writes if w is not None]
        for k in rk:
            for st in self._states(k):
                if st[0] is not None:
                    deps.setdefault(st[0], "raw")
        for k in wk:
            for st in self._states(k):
                if st[0] is not None:
                    deps.setdefault(st[0], "waw")
                for r in st[1]:
                    if r != idx:
                        deps.setdefault(r, "war")
        for name in {k[0] for k in rk + wk if k[0] in self.psum_names}:
            bl = self.bank_last.setdefault(name, {})
            for e2, i2 in bl.items():
                if e2 != eng:
                    deps.setdefault(i2, "bank")
                else:
                    deps.setdefault(i2, "order")
            bl[eng] = idx
        for k in rk:
            name, sub = k
            if sub is None:
                for st in self._states(k):
                    st[1].append(idx)
            else:
                self.state[name][sub][1].append(idx)
        for k in wk:
            name, sub = k
            if sub is None:
                d = self.state[name]
                for s in list(d.keys()):
                    d[s] = [idx, []]
            else:
                self.state[name][sub] = [idx, []]
        o = dict(eng=eng, fn=fn, deps=deps, signal=False, dma_tag=dma_tag, wait_all=wait_all, val=None,
                 cost=float(cost), lat=float(lat))
        need = []
        for d, kind in deps.items():
            p = self.ops[d]
            if p["dma_tag"] is None and dma_tag is None and p["eng"] == eng:
                if eng == "pe":
                    continue
                if kind == "order":
                    continue
            need.append(d)
            p["signal"] = True
        o["need"] = need
        if dma_tag is not None:
            o["signal"] = True
        self.ops.append(o)
        return idx


    def schedule(self, window=24, slack=60.0):
        import bisect
        ops = self.ops
        n = len(ops)
        succ = [[] for _ in range(n)]
        ndeps = [0] * n
        for i, o in enumerate(ops):
            ndeps[i] = len(o["deps"])
            for d in o["deps"]:
                succ[d].append(i)
        ready = [0.0] * n
        engs = sorted({o["eng"] for o in ops})
        free = {e: 0.0 for e in engs}
        rel = {e: [] for e in engs}
        for i, o in enumerate(ops):
            if ndeps[i] == 0:
                rel[o["eng"]].append(i)
        order = []
        done = 0
        cur_tbl = [None]
        TBL = 1300.0

        def eff(e, t, i):
            st_ = max(t, ready[i])
            if e == "act":
                tb = ops[i].get("tbl")
                if tb is not None and tb != cur_tbl[0]:
                    st_ += TBL
            return st_
        while done < n:
            best = None
            for e in engs:
                cand = rel[e]
                if not cand:
                    continue
                t = free[e]
                lim = cand[:window]
                tmin = min(eff(e, t, i) for i in lim)
                for i in lim:
                    if eff(e, t, i) <= tmin + slack:
                        pick = i
                        break
                stt = eff(e, t, pick)
                if best is None or stt < best[0]:
                    best = (stt, e, pick)
            stt, e, i = best
            rel[e].remove(i)
            o = ops[i]
            if e == "act" and o.get("tbl") is not None:
                cur_tbl[0] = o["tbl"]
            fin = stt + o["cost"] + o["lat"]
            free[e] = stt + o["cost"]
            order.append(i)
            done += 1
            for sidx in succ[i]:
                so = ops[sidx]
                if o["dma_tag"] is not None or so["eng"] != e:
                    l = 180.0
                elif e == "pe":
                    l = 0.0
                else:
                    l = 60.0 if o["deps"] and ops[sidx]["deps"].get(i) in ("raw", "waw") else 0.0
                r = fin + l
                if r > ready[sidx]:
                    ready[sidx] = r
                ndeps[sidx] -= 1
                if ndeps[sidx] == 0:
                    bisect.insort(rel[so["eng"]], sidx)
        self.est_ns = max(free.values())
        remap = {old: new for new, old in enumerate(order)}
        new_ops = []
        for old in order:
            o = ops[old]
            o["deps"] = {remap[d]: k for d, k in o["deps"].items()}
            o["need"] = [remap[d] for d in o["need"]]
            new_ops.append(o)
        self.ops = new_ops

    def emit(self):
        nc = self.nc
        counters = {}
        for o in self.ops:
            if not o["signal"]:
                continue
            key = ("dma", o["dma_tag"]) if o["dma_tag"] is not None else ("eng", o["eng"])
            inc = 16 if o["dma_tag"] is not None else 1
            counters[key] = counters.get(key, 0) + inc
            o["key"] = key
            o["val"] = counters[key]
        totals = dict(counters)
        hw = {}

        def hwsem(key, k):
            if (key, k) not in hw:
                hw[(key, k)] = self.stack.enter_context(nc.semaphore(f"s_{key[0]}_{key[1]}_{k}"))
            return hw[(key, k)]

        waited = {}
        n_wait = 0
        for o in self.ops:
            e = self.eng[o["eng"]]
            tgt = {}
            for d in o["need"]:
                p = self.ops[d]
                v = totals[p["key"]] if p["wait_all"] else p["val"]
                tgt[p["key"]] = max(tgt.get(p["key"], 0), v)
            for key, v in tgt.items():
                if waited.get((o["eng"], key), 0) >= v:
                    continue
                waited[(o["eng"], key)] = v
                k = (v - 1) // SEM_CAP
                e.wait_ge(hwsem(key, k), v - k * SEM_CAP)
                n_wait += 1
            ins = o["fn"](e)
            if o["signal"]:
                v = o["val"]
                k = (v - 1) // SEM_CAP
                inc = 16 if o["dma_tag"] is not None else 1
                ins.then_inc(hwsem(o["key"], k), inc)
        self.n_wait = n_wait
        return totals, hwsem

    def finish(self, out_tags, eng="sp"):
        totals, hwsem = self._fin
        e = self.eng[eng]
        for t in out_tags:
            key = ("dma", t)
            if key in totals:
                v = totals[key]
                k = (v - 1) // SEM_CAP
                e.wait_ge(hwsem(key, k), v - k * SEM_CAP)

    def emit_all(self, out_tags, eng="sp", sched=True):
        if sched:
            self.schedule()
        self._fin = self.emit()
        self.finish(out_tags, eng)


def _nfree(ap):
    n = 1
    for d in list(ap.shape)[1:]:
        n *= int(d)
    return n


class Ops:
    def __init__(self, prog):
        self.p = prog

    def _ew(self, out, *ins):
        n = _nfree(out)
        ps = any((getattr(getattr(x, "tensor", None), "name", None) in self.p.psum_names) for x in ins if x is not None
                 and not isinstance(x, (int, float)))
        return (160.0 + 1.05 * n) if ps else (100.0 + 0.8 * n)

    @staticmethod
    def _aps(*xs):
        return [x for x in xs if x is not None and not isinstance(x, (int, float))]

    def mm(self, out, lhsT, rhs, start=True, stop=True, wk=None, rk=()):
        self.p.op("pe", lambda e: e.matmul(out, lhsT, rhs, start=start, stop=stop),
                  reads=[lhsT, rhs] if not rk else list(rk), writes=[wk if wk is not None else out],
                  cost=8.0 + 0.43 * max(_nfree(rhs), 64), lat=120.0)

    def tr(self, out, in_, ident, wk=None, rk=()):
        self.p.op("pe", lambda e: e.transpose(out, in_, ident),
                  reads=[in_, ident] if not rk else list(rk) + [ident], writes=[wk if wk is not None else out],
                  cost=8.0 + 0.43 * max(_nfree(in_), 64), lat=120.0)

    def act(self, out, in_, func, bias=None, scale=None, accum=None, eng="act", wk=None, rk=None):
        kw = {}
        if bias is not None:
            kw["bias"] = bias
        if scale is not None:
            kw["scale"] = scale
        if accum is not None:
            kw["accum_out"] = accum
        reads = self._aps(in_, bias, scale) if rk is None else list(rk) + self._aps(bias, scale)
        writes = [wk if wk is not None else out] + ([accum] if accum is not None else [])
        tbl = {AF.Exp: "ln_exp", AF.Ln: "ln_exp", AF.Gelu: "gelu", AF.Tanh: "gelu", AF.Silu: "silu", AF.Sigmoid: "sig",
               AF.Sqrt: "sqrt"}.get(func)
        i = self.p.op("act", lambda e: e.activation(out, in_, func, **kw), reads=reads, writes=writes,
                      cost=self._ew(out, in_) + (90.0 if accum is not None else 0.0))
        self.p.ops[i]["tbl"] = tbl

    def tt(self, out, a, b, op, eng="dve", wk=None, rk=None):
        self.p.op(eng, lambda e: e.tensor_tensor(out, a, b, op),
                  reads=[a, b] if rk is None else list(rk), writes=[wk if wk is not None else out],
                  cost=self._ew(out, a, b) * (2.0 if eng == "pool" else 1.0))

    def ts(self, out, a, s1, s2=None, op0=ALU.mult, op1=None, eng="dve", wk=None, rk=None, accum=None):
        kw = {}
        if op1 is not None:
            kw["op1"] = op1
        if accum is not None:
            kw["accum_out"] = accum
        reads = self._aps(a, s1, s2) if rk is None else list(rk) + self._aps(s1, s2)
        writes = [wk if wk is not None else out] + ([accum] if accum is not None else [])
        self.p.op(eng, lambda e: e.tensor_scalar(out, a, s1, s2, op0, **kw), reads=reads, writes=writes,
                  cost=self._ew(out, a) * (2.0 if eng == "pool" else 1.0))

    def stt(self, out, in0, scalar, in1, op0, op1, eng="dve", wk=None, rk=None):
        reads = self._aps(in0, scalar, in1) if rk is None else list(rk) + self._aps(scalar)
        self.p.op(eng, lambda e: e.scalar_tensor_tensor(out, in0, scalar, in1, op0, op1),
                  reads=reads, writes=[wk if wk is not None else out],
                  cost=self._ew(out, in0, in1) * (2.0 if eng == "pool" else 1.0))

    def copy(self, out, in_, eng="dve", wk=None, rk=None):
        if eng == "act":
            self.p.op("act", lambda e: e.copy(out, in_), reads=[in_] if rk is None else list(rk),
                      writes=[wk if wk is not None else out], cost=self._ew(out, in_))
        else:
            self.p.op(eng, lambda e: e.tensor_copy(out, in_), reads=[in_] if rk is None else list(rk),
                      writes=[wk if wk is not None else out], cost=self._ew(out, in_) * (2.0 if eng == "pool" else 1.0))

    def memset(self, out, val, eng="dve", wk=None):
        self.p.op(eng, lambda e: e.memset(out, val), reads=[], writes=[wk if wk is not None else out],
                  cost=60.0 + 0.5 * _nfree(out))

    def recip(self, out, in_, wk=None):
        self.p.op("dve", lambda e: e.reciprocal(out, in_), reads=[in_], writes=[wk if wk is not None else out],
                  cost=self._ew(out, in_) * 1.5)

    def scan(self, out, d0, d1, initial, op0, op1, wk=None, rk=None):
        reads = self._aps(d0, d1, initial) if rk is None else list(rk)
        self.p.op("dve", lambda e: e.tensor_tensor_scan(out, d0, d1, initial, op0, op1),
                  reads=reads, writes=[wk if wk is not None else out], cost=100.0 + 2.1 * _nfree(out))

    def bn_stats(self, out, in_, wk=None, rk=None):
        self.p.op("dve", lambda e: e.bn_stats(out, in_), reads=[in_] if rk is None else list(rk),
                  writes=[wk if wk is not None else out], cost=self._ew(in_, in_))

    def bn_aggr(self, out, in_, wk=None):
        self.p.op("dve", lambda e: e.bn_aggr(out, in_), reads=[in_], writes=[wk if wk is not None else out])

    def dma(self, out, in_, tag, q="sp", wk=None, rk=None, wait_all=False):
        nbytes = 128 * _nfree(out) * (2 if out.dtype == BF16 else 4)
        self.p.op(q, lambda e: e.dma_start(out=out, in_=in_), reads=[in_] if rk is None else list(rk),
                  writes=[wk if wk is not None else out], dma_tag=tag, wait_all=wait_all,
                  cost=70.0, lat=2000.0 + nbytes / 150.0)
import contextlib
import math

D = 2048
DA = 1024
NSH = 3200
DP = 7296
NORM_EPS = 1e-6
LN_EPS = 1e-5
GN_EPS = 64e-5
C0 = math.exp(-0.5)
NQ = 25
PV_MU, PV_W0, PV_A0, PV_KK, PV_KA, PV_RK, PV_GNG, PV_GNB, PV_SGUB = 0, 25, 33, 41, 49, 57, 65, 73, 81
PV_HM = 89
PV_N = 91


def build(NP, NS, SBW=256):
    nc = bass.Bass("TRN2", target_bir_lowering=False)
    st = contextlib.ExitStack()
    with st:
        P = Prog(nc, st)
        O = Ops(P)
        NST = NS * 64
        assert NP % SBW == 0
        din = lambda n, s, dt=F32: nc.dram_tensor(n, list(s), dt, kind="ExternalInput").ap()
        dout = lambda n, s, dt=F32: nc.dram_tensor(n, list(s), dt, kind="ExternalOutput").ap()
        xp_d = din("xp", [NP, D]); xs_d = din("xs", [NST, D])
        hs0_d = din("hs0", [NS, 8, 128, 128]); shT_d = din("shiftT", [128, NS, NQ])
        win_d = din("w_in", [D, DP]); wout_d = din("w_out", [D, D])
        normg_d = din("i_normg", [128, 16]); cmat_d = din("i_cmat", [128, 5, 128])
        lnbc_d = din("i_lnbc", [128, 2, DA]); fing_d = din("i_fing", [128, D]); pvec_d = din("i_pvec", [128, PV_N])
        wsT_d = din("i_wsT", [128, 8, 128]); w2a2_d = din("i_w2a2", [128, DA])
        yp_d = dout("yp", [NP, D]); ys_d = dout("ys", [NST, D])
        hp_d = dout("hp_out", [8, 128, 128]); hs_d = dout("hs_out", [NS, 8, 128, 128])
        shp_d = dout("shp_out", [128, NQ]); shs_d = dout("shs_out", [128, NS, NQ])
        vn_d = dout("vn_out", [NST, DA])
        WbA = nc.dram_tensor("WbA", [6, 128, 16, 512], BF16, kind="Internal").ap()
        WbB = nc.dram_tensor("WbB", [33, 128, 16, 128], BF16, kind="Internal").ap()
        WbO = nc.dram_tensor("WbO", [4, 128, 16, 512], BF16, kind="Internal").ap()

        NT = SBW // 128
        normg = P.sb("normg", [128, 16], F32)
        cmat = P.sb("cmat", [128, 5, 128], F32)
        cmb = P.sb("cmb", [128, 5, 128], BF16)
        lnbc = P.sb("lnbc", [128, 2, DA], F32)
        fing = P.sb("fing", [128, D], F32)
        pvec = P.sb("pvec", [128, PV_N], F32)
        omka = P.sb("omka", [128, 8], F32)
        hbias = P.sb("hbias", [128, 16], F32)
        mhalf = P.sb("mhalf", [128, SBW], F32)
        tg = P.sb("tg", [128, SBW], F32)
        wsTb = P.sb("wsTb", [128, 8, 128], BF16)
        w2a2b = P.sb("w2a2b", [128, DA], BF16)
        HF = P.sb("HF", [128, 8, 128], F32)
        HB = P.sb("HB", [128, 8, 128], BF16)
        HsF = P.sb("HsF", [128, 128], F32)
        HsB = P.sb("HsB", [128, 128], BF16)
        lastcol = P.sb("lastcol", [128, NQ], F32)
        shT = P.sb("shT", [128, NS, NQ], F32)
        shs = P.sb("shs", [128, NS, NQ], F32)
        xt = P.sb("xt", [128, 2 * NT, D], F32)
        junk = P.sb("junk", [128, D], BF16)
        xn = P.sb("xn", [128, D], BF16)
        hT = P.sb("hT", [128, 16, SBW], BF16)
        outT = P.sb("outT", [128, 16, SBW], BF16)
        wB = [P.sb(f"wB{i}", [128, 16, 128], BF16) for i in range(3)]
        wA = [P.sb(f"wA{i}", [128, 16, 512], BF16) for i in range(2)]
        gv = P.sb("gv", [128, NT, DA], F32)
        gus = P.sb("gus", [128, NT, DA], BF16)
        gat = P.sb("gat", [128, 512], F32)
        gat2 = P.sb("gat2", [128, 512], BF16)
        vnb = P.sb("vnb", [128, NT, DA], BF16)
        oa = P.sb("oa", [128, DA], BF16)
        st6 = P.sb("st6", [128, 2, 6], F32)
        mv = P.sb("mv", [128, 2, 2], F32)
        sm = P.sb("sm", [128, 8], F32)
        f32t = lambda n: P.sb(n, [128, SBW], F32)
        zraw = P.sb("zraw", [128, SBW + 1], F32)
        diff = f32t("diff")
        mixl, mixr, mixk, mixv = f32t("mixl"), f32t("mixr"), f32t("mixk"), f32t("mixv")
        sgb = [f32t(f"sgb{i}") for i in range(2)]; bv = [f32t(f"bv{i}") for i in range(2)]; ynT = [f32t(f"ynT{i}") for i in range(2)]
        sd, aa, cum, excl = f32t("sd"), f32t("aa"), f32t("cum"), f32t("excl")
        e_incl, e_excl, e_inv, e_rem = f32t("e_incl"), f32t("e_excl"), f32t("e_inv"), f32t("e_rem")
        kk, sq, rn, tmp, kp, bbv, rkv = (f32t(n) for n in ("kk", "sq", "rn", "tmp", "kp", "bbv", "rkv"))
        ones = f32t("ones")
        nb = P.sb("nb", [128, NT], F32)
        gC = [P.sb(f"gC{i}", [128, NT], F32) for i in range(2)]
        th = P.sb("th", [128, SBW], BF16)
        b16 = lambda n: P.sb(n, [128, SBW], BF16)
        kt, rt, khg, bhg, vbf = ([b16(f"{n}{i}") for i in range(2)] for n in ("kt", "rt", "khg", "bhg", "vbf"))
        kh = [[b16(f"kh{i}_{h}") for h in range(2)] for i in range(2)]
        bh = [[b16(f"bh{i}_{h}") for h in range(2)] for i in range(2)]
        tok3 = [P.sb(f"tok3_{i}", [128, 3, 128], BF16) for i in range(2)]
        Asb = [[P.sb(f"Asb{i}_{h}", [128, 3, 128], BF16) for h in range(2)] for i in range(2)]
        XMS = [[[P.sb(f"XMS{i}_{h}_{j}", [128, 384], BF16) for j in range(2)] for h in range(2)] for i in range(2)]
        TT = [[P.sb(f"TT{i}_{h}", [128, 128], BF16) for h in range(2)] for i in range(2)]
        st6a = P.sb("st6a", [128, 2, 6], F32)
        mva = P.sb("mva", [128, 2], F32)
        sma = P.sb("sma", [128, 2], F32)
        Wsb = P.sb("Wsb", [128, 128], BF16)
        nU = P.sb("nU", [128, 128], BF16)
        ynb = P.sb("ynb", [128, 128], BF16)
        rstd2 = P.sb("rstd2", [128, 2], F32)
        pzs = [P.ps(f"pz{i}", [128, 512], F32) for i in range(2)]
        pA = P.ps("pA", [128, 512], F32)
        pI = [P.ps(f"pI{h}", [128, 512], F32) for h in range(2)]
        pS = P.ps("pS", [128, 512], F32)
        pm = P.ps("pm", [128, 512], F32)
        ptr = P.ps("ptr", [128, 1024], BF16)

        ident_b = lambda n: cmb[:n, 0, :n]
        m_incl = lambda n: cmat[:n, 1, :n]
        m_strict = lambda n: cmat[:n, 2, :n]
        m_low = lambda n: cmat[:n, 3, :n]
        blockones = cmat[:, 4, :]
        pv = lambda base, j: pvec[:, base + j:base + j + 1]

        cl = lambda out, in_: O.dma(out, in_, "const", wait_all=True)
        cl(normg[:], normg_d); cl(cmat[:], cmat_d); cl(lnbc[:], lnbc_d); cl(fing[:], fing_d); cl(pvec[:], pvec_d)
        wsTf = gv[:, 0, :].rearrange("p (h i) -> p h i", i=128)
        w2a2f = gv[:, 1, :]
        O.dma(wsTf, wsT_d, "const", wait_all=True, wk=(gv, 0)); O.dma(w2a2f, w2a2_d, "const", wait_all=True, wk=(gv, 1)); cl(shT[:], shT_d)
        O.copy(cmb[:], cmat[:])
        O.copy(w2a2b[:], w2a2f, eng="act", rk=[(gv, 1)])
        O.ts(omka[:], pvec[:, PV_KA:PV_KA + 8], -1.0, 1.0, op0=ALU.mult, op1=ALU.add)
        O.ts(hbias[:], pvec[:, PV_W0:PV_W0 + 16], -1.0, None, op0=ALU.mult)
        O.memset(mhalf[:], -0.5)
        for h in range(8):
            O.tt(wsTb[:, h, :], wsTf[:, h, :], cmat[:, 1, :], ALU.mult, rk=[(gv, 0), cmat])
        O.memset(HF[:], 0.0); O.memset(HB[:], 0.0); O.memset(lastcol[:], 0.0); O.memset(ones[:], 1.0)
        O.memset(zraw[:, 0:1], 0.0)

        pieces = [(7168, 128), (3072, 2048), (5120, 2048), (0, 2048), (2048, 1024)]
        stg_f = [xt[:, 0, :], xt[:, 1, :]]
        stg_b = [junk, xn]
        k = 0
        for pi, (c0, w) in enumerate(pieces):
            for c in range(16):
                sf = stg_f[k % 2]; sbf = stg_b[k % 2]
                O.dma(sf[:, 0:w], win_d[c * 128:(c + 1) * 128, c0:c0 + w], f"stgf{k % 2}", wk=(xt, k % 2))
                if k % 2 == 0:
                    O.ts(sbf[:, 0:w], sf[:, 0:w], normg[:, c:c + 1], None, op0=ALU.mult, rk=[(xt, k % 2)])
                else:
                    O.act(sbf[:, 0:w], sf[:, 0:w], AF.Copy, scale=normg[:, c:c + 1], rk=[(xt, k % 2)])
                if c0 < 3072:
                    g0 = c0 // 512; ng = w // 512
                    for gg in range(ng):
                        O.dma(WbA[g0 + gg, :, c, :], sbf[:, gg * 512:(gg + 1) * 512], f"stgb{k % 2}_{gg}", wk=(WbA, (g0 + gg, c)))
                else:
                    q0 = (c0 - 3072) // 128; nq = w // 128
                    for qq in range(0, nq, 4):
                        nn = min(4, nq - qq)
                        O.dma(WbB[q0 + qq:q0 + qq + nn, :, c, :].rearrange("q p n -> p q n"),
                              sbf[:, qq * 128:(qq + nn) * 128].rearrange("p (q n) -> p q n", n=128), f"stgb{k % 2}_{qq // 4}",
                              wk=(WbB, ((q0 + qq) // 4, c)))
                k += 1
        for c in range(16):
            sf = stg_f[k % 2]; sbf = stg_b[k % 2]
            O.dma(sf[:, 0:2048], wout_d[c * 128:(c + 1) * 128, :], f"stgf{k % 2}", wk=(xt, k % 2))
            O.copy(sbf[:, 0:2048], sf[:, 0:2048], eng="act" if k % 2 else "dve", rk=[(xt, k % 2)])
            for gg in range(4):
                O.dma(WbO[gg, :, c, :], sbf[:, gg * 512:(gg + 1) * 512], f"stgb{k % 2}_{gg}", wk=(WbO, (gg, c)))
            k += 1

        wslotB = [0]
        wslotA = [0]

        def load_wB(q):
            s = wslotB[0] % 3; wslotB[0] += 1
            O.dma(wB[s][:], WbB[q], f"wB{s}", rk=[(WbB, (q // 4, c)) for c in range(16)])
            return wB[s]

        def load_wA(T, g):
            s = wslotA[0] % 2; wslotA[0] += 1
            O.dma(wA[s][:], T[g], f"wA{s}", rk=[(T, (g, c)) for c in range(16)])
            return wA[s]

        def rsqrt(out, in_, n, scale, eps):
            O.ts(out, in_, scale, eps, op0=ALU.mult, op1=ALU.add)
            O.act(out, out, AF.Ln)
            O.act(out, out, AF.Exp, scale=-0.5)

        def run_merged(gens):
            act = [[g, float(w), 0] for g, w in gens if g is not None]
            while act:
                a = min(act, key=lambda t: t[2] / t[1])
                try:
                    next(a[0]); a[2] += 1
                except StopIteration:
                    act.remove(a)

        def run_seq(g):
            for _ in g:
                pass

        def superblock(x_src, y_dst, tiles, W, sample, seq0=0, sbp=0):
            nt = len(tiles)
            xi = lambda ti: sbp * NT + ti

            def stage0():
                for ti, (off, Pt) in enumerate(tiles):
                    O.dma(xt[:Pt, xi(ti), :], x_src[off:off + Pt, :], f"x{xi(ti)}", wk=(xt, xi(ti)))
                    O.act(junk[:Pt, :], xt[:Pt, xi(ti), :], AF.Square, accum=sm[:Pt, 0:1], rk=[(xt, xi(ti))])
                    rsqrt(sm[:Pt, 2:3], sm[:Pt, 0:1], 1, 1.0 / D, NORM_EPS)
                    O.ts(xn[:Pt, :], xt[:Pt, xi(ti), :], sm[:Pt, 2:3], None, op0=ALU.mult, rk=[(xt, xi(ti))])
                    for c4 in range(4):
                        for j in range(4):
                            c = c4 * 4 + j
                            O.tr(ptr[:, j * Pt:(j + 1) * Pt], xn[:Pt, c * 128:(c + 1) * 128], ident_b(Pt))
                        src = ptr[:, 0:4 * Pt].rearrange("p (j t) -> p j t", t=Pt)
                        O.copy(hT[:, c4 * 4:c4 * 4 + 4, off:off + Pt], src, eng="act" if c4 % 2 else "dve", wk=(hT, ti))
                        yield

            def proj_B(q, half):
                w = load_wB(q)
                for c in range(16):
                    O.mm(pzs[half][:, 0:W], w[:, c, :], hT[:, c, 0:W], start=(c == 0), stop=(c == 15))
                return pzs[half][:, 0:W]

            def shifted(q, half, dst):
                src = proj_B(q, half)
                O.copy(zraw[:, 1:W + 1], src, eng="act")
                if not sample:
                    O.copy(zraw[:, 0:1], lastcol[:, q:q + 1])
                O.tt(diff[:, 0:W], zraw[:, 0:W], zraw[:, 1:W + 1], ALU.subtract)
                if sample:
                    for ti, (off, Pt) in enumerate(tiles):
                        O.tt(diff[:, off:off + 1], shT[:, seq0 + ti, q:q + 1], zraw[:, off + 1:off + 2], ALU.subtract)
                        O.copy(shs[:, seq0 + ti, q:q + 1], zraw[:, off + Pt:off + Pt + 1])
                else:
                    O.copy(lastcol[:, q:q + 1], zraw[:, W:W + 1])
                O.stt(dst[:, 0:W], diff[:, 0:W], pv(PV_MU, q), zraw[:, 1:W + 1], ALU.mult, ALU.add)

            def lora():
                shifted(24, 0, mixl)
                O.act(tg[0:64, 0:W], mixl[0:64, 0:W], AF.Exp, scale=-2.0)
                O.act(tg[0:64, 0:W], tg[0:64, 0:W], AF.Ln, bias=1.0)
                O.act(tg[0:64, 0:W], tg[0:64, 0:W], AF.Exp, scale=-1.0)
                O.ts(th[0:64, 0:W], tg[0:64, 0:W], 2.0, -1.0, op0=ALU.mult, op1=ALU.add)
                O.copy(th[64:128, 0:W], mixl[64:128, 0:W])
                yield

            def prep(p):
                s = p % 2
                shifted(p, 1, mixr); yield
                shifted(8 + p, 0, mixk); yield
                shifted(16 + p, 1, mixv); yield
                src = proj_B(25 + p, 0)
                O.act(tg[:, 0:W], src, AF.Exp, scale=-1.0)
                O.act(tg[:, 0:W], tg[:, 0:W], AF.Ln, bias=1.0)
                O.act(tg[:, 0:W], tg[:, 0:W], AF.Exp, scale=-1.0)
                O.tt(sgb[s][:, 0:W], tg[:, 0:W], src, ALU.mult); yield
                O.mm(pm[:, 0:W], w2a2b[0:64, p * 128:(p + 1) * 128], th[0:64, 0:W])
                O.act(sd[:, 0:W], pm[:, 0:W], AF.Exp, bias=hbias[:, p:p + 1], scale=-1.0)
                O.act(sd[:, 0:W], sd[:, 0:W], AF.Ln, bias=1.0)
                O.act(sd[:, 0:W], sd[:, 0:W], AF.Exp, scale=-1.0)
                O.mm(pm[:, 256:256 + W], w2a2b[64:128, p * 128:(p + 1) * 128], th[64:128, 0:W])
                O.act(aa[:, 0:W], pm[:, 256:256 + W], AF.Exp, bias=hbias[:, 8 + p:9 + p], scale=-1.0)
                O.act(aa[:, 0:W], aa[:, 0:W], AF.Ln, bias=1.0)
                O.act(aa[:, 0:W], aa[:, 0:W], AF.Exp, scale=-1.0); yield
                for ti, (off, Pt) in enumerate(tiles):
                    O.scan(cum[:, off:off + Pt], ones[:, off:off + Pt], sd[:, off:off + Pt], 0.0, ALU.mult, ALU.add)
                    O.ts(nb[:, ti:ti + 1], cum[:, off + Pt - 1:off + Pt], -C0, None, op0=ALU.mult)
                    O.act(gC[s][:, ti:ti + 1], cum[:, off + Pt - 1:off + Pt], AF.Exp, scale=-C0)
                    O.act(e_rem[:, off:off + Pt], cum[:, off:off + Pt], AF.Exp, scale=C0, bias=nb[:, ti:ti + 1])
                yield
                O.act(e_incl[:, 0:W], cum[:, 0:W], AF.Exp, scale=-C0)
                O.tt(excl[:, 0:W], cum[:, 0:W], sd[:, 0:W], ALU.subtract, eng="pool")
                O.act(e_excl[:, 0:W], excl[:, 0:W], AF.Exp, scale=-C0)
                O.act(e_inv[:, 0:W], cum[:, 0:W], AF.Exp, scale=C0); yield
                O.ts(kk[:, 0:W], mixk[:, 0:W], pv(PV_KK, p), None, op0=ALU.mult)
                O.tt(sq[:, 0:W], kk[:, 0:W], kk[:, 0:W], ALU.mult, eng="pool")
                O.mm(pm[:, 0:W], blockones, sq[:, 0:W])
                O.ts(rn[:, 0:W], pm[:, 0:W], 1e-24, None, op0=ALU.max)
                O.act(rn[:, 0:W], rn[:, 0:W], AF.Ln)
                O.act(rn[:, 0:W], rn[:, 0:W], AF.Exp, scale=-0.5); yield
                O.tt(kk[:, 0:W], kk[:, 0:W], rn[:, 0:W], ALU.mult)
                O.ts(tmp[:, 0:W], aa[:, 0:W], pv(PV_KA, p), omka[:, p:p + 1], op0=ALU.mult, op1=ALU.add, eng="pool")
                O.tt(kp[:, 0:W], mixk[:, 0:W], tmp[:, 0:W], ALU.mult)
                O.tt(bbv[:, 0:W], kk[:, 0:W], aa[:, 0:W], ALU.mult); yield
                O.tt(kt[s][:, 0:W], kk[:, 0:W], e_excl[:, 0:W], ALU.mult, eng="pool")
                O.tt(rt[s][:, 0:W], mixr[:, 0:W], e_incl[:, 0:W], ALU.mult)
                for hd in range(2):
                    O.stt(kh[s][hd][:, 0:W], kp[:, 0:W], pv(PV_HM, hd), e_inv[:, 0:W], ALU.mult, ALU.mult)
                yield
                for hd in range(2):
                    O.stt(bh[s][hd][:, 0:W], bbv[:, 0:W], pv(PV_HM, hd), e_inv[:, 0:W], ALU.mult, ALU.mult)
                O.tt(khg[s][:, 0:W], kp[:, 0:W], e_rem[:, 0:W], ALU.mult, eng="pool")
                O.tt(bhg[s][:, 0:W], bbv[:, 0:W], e_rem[:, 0:W], ALU.mult); yield
                O.copy(vbf[s][:, 0:W], mixv[:, 0:W], eng="act")
                O.stt(rkv[:, 0:W], mixr[:, 0:W], pv(PV_RK, p), kp[:, 0:W], ALU.mult, ALU.mult)
                O.mm(pm[:, 256:256 + W], blockones, rkv[:, 0:W])
                O.tt(bv[s][:, 0:W], pm[:, 256:256 + W], mixv[:, 0:W], ALU.mult); yield

            def inv_unit(u):
                p, ti = u // nt, u % nt
                s, par = p % 2, u % 2
                off, Pt = tiles[ti]
                sl = slice(off, off + Pt)
                w = Pt
                for j, srcb in enumerate((vbf[s], khg[s], bhg[s])):
                    O.tr(ptr[:Pt, j * 128:(j + 1) * 128], srcb[:, sl], ident_b(128))
                O.copy(tok3[par][:Pt, :, :], ptr[:Pt, 0:384].rearrange("p (j f) -> p j f", f=128), eng="act")
                yield
                for hd in range(2):
                    hs = slice(hd * 64, (hd + 1) * 64)
                    O.mm(pA[:Pt, 0:Pt], kh[s][hd][:, sl], kt[s][:, sl])
                    O.mm(pA[:Pt, 128:128 + Pt], kh[s][hd][:, sl], rt[s][:, sl])
                    O.mm(pA[:Pt, 256:256 + Pt], bh[s][hd][:, sl], kt[s][:, sl])
                    O.mm(pA[:Pt, 384:384 + Pt], bh[s][hd][:, sl], rt[s][:, sl])
                    O.mm(pI[hd][:Pt, 0:Pt], kt[s][:, sl], bh[s][hd][:, sl])
                    yield
                    X0 = XMS[par][hd][0]
                    O.tt(Asb[par][hd][:Pt, 0, :Pt], pA[:Pt, 0:Pt], m_strict(Pt), ALU.mult)
                    O.tt(Asb[par][hd][:Pt, 1, :Pt], pA[:Pt, 128:128 + Pt], m_incl(Pt), ALU.mult)
                    O.tt(Asb[par][hd][:Pt, 2, :Pt], pA[:Pt, 384:384 + Pt], m_incl(Pt), ALU.mult)
                    O.stt(X0[:Pt, w:2 * w], pA[:Pt, 256:256 + Pt], -1.0, m_strict(Pt), ALU.mult, ALU.mult, wk=(X0, "x"))
                    O.stt(X0[:Pt, 0:w], pI[hd][:Pt, 0:Pt], -1.0, m_low(Pt), ALU.mult, ALU.mult, wk=(X0, "m"))
                    yield
                nlev = int(math.log2(Pt))
                for k in range(nlev):
                    for hd in range(2):
                        cur = XMS[par][hd][k % 2]; nxt = XMS[par][hd][(k + 1) % 2]
                        Mk, Xk, Sk = cur[:Pt, 0:w], cur[:Pt, w:2 * w], cur[:Pt, 2 * w:3 * w]
                        if k == 0:
                            O.mm(pI[hd][:Pt, w:2 * w], Mk, Xk, rk=[(cur, "m"), (cur, "x")])
                            O.mm(pI[hd][:Pt, 0:w], Xk, Mk, rk=[(cur, "m"), (cur, "x")])
                            O.tt(nxt[:Pt, 2 * w:3 * w], Xk, cmat[:Pt, 0, :Pt], ALU.add, rk=[(cur, "x"), cmat], wk=(nxt, "s"))
                            O.copy(nxt[:Pt, 0:2 * w], pI[hd][:Pt, 0:2 * w], eng="act", wk=(nxt, "mx"))
                        elif k < nlev - 1:
                            O.mm(pI[hd][:Pt, w:3 * w], Mk, cur[:Pt, w:3 * w], rk=[(cur, "mx"), (cur, "s")])
                            O.mm(pI[hd][:Pt, 0:w], Xk, Mk, rk=[(cur, "mx")])
                            O.copy(nxt[:Pt, 0:2 * w], pI[hd][:Pt, 0:2 * w], eng="act", wk=(nxt, "mx"))
                            O.tt(nxt[:Pt, 2 * w:3 * w], pI[hd][:Pt, 2 * w:3 * w], Sk, ALU.add, rk=[pI[hd], (cur, "s")],
                                 wk=(nxt, "s"))
                        else:
                            O.mm(pI[hd][:Pt, 2 * w:3 * w], Mk, Sk, rk=[(cur, "mx"), (cur, "s")])
                            O.tt(TT[par][hd][:Pt, :Pt], pI[hd][:Pt, 2 * w:3 * w], Sk, ALU.add, rk=[pI[hd], (cur, "s")])
                        yield

            def state_unit(u):
                p, ti = u // nt, u % nt
                s, par = p % 2, u % 2
                off, Pt = tiles[ti]
                sl = slice(off, off + Pt)
                if sample:
                    O.dma(HsF[:], hs0_d[seq0 + ti, p], "hsin")
                    O.copy(HsB[:], HsF[:])
                    Hf, Hb = HsF, HsB
                    Hfv = lambda a, b: HsF[a, b]
                    Hbv = HsB[:, :]
                else:
                    Hf, Hb = HF, HB
                    Hfv = lambda a, b: HF[a, p, b]
                    Hbv = HB[:, p, :]
                hkey = (Hf, None if sample else p)
                hbkey = (Hb, None if sample else p)
                T3 = tok3[par]
                Vt = lambda hd: T3[:Pt, 0, hd * 64:(hd + 1) * 64]
                A_ = Asb[par]
                O.mm(pS[:Pt, 0:128], kt[s][:, sl], Hbv, start=True, stop=False, rk=[kt[s], hbkey])
                for hd in range(2):
                    O.mm(pS[:Pt, hd * 64:(hd + 1) * 64], A_[hd][:Pt, 0, :Pt], Vt(hd), start=False, stop=(hd == 1))
                O.copy(Wsb[:Pt, :], pS[:Pt, 0:128], eng="act")
                yield
                for hd in range(2):
                    O.mm(pS[:Pt, 128 + hd * 64:128 + (hd + 1) * 64], TT[par][hd][:Pt, :Pt], Wsb[:Pt, hd * 64:(hd + 1) * 64])
                O.act(nU[:Pt, :], pS[:Pt, 128:256], AF.Copy, scale=-1.0)
                yield
                O.mm(pS[:Pt, 256:384], rt[s][:, sl], Hbv, start=True, stop=False, rk=[rt[s], hbkey])
                for hd in range(2):
                    O.mm(pS[:Pt, 256 + hd * 64:256 + (hd + 1) * 64], A_[hd][:Pt, 1, :Pt], Vt(hd), start=False, stop=False)
                for hd in range(2):
                    O.mm(pS[:Pt, 256 + hd * 64:256 + (hd + 1) * 64], A_[hd][:Pt, 2, :Pt], nU[:Pt, hd * 64:(hd + 1) * 64],
                         start=False, stop=(hd == 1))
                O.mm(pS[:, 384:512], T3[:Pt, 1, :], T3[:Pt, 0, :], start=True, stop=False)
                O.mm(pS[:, 384:512], T3[:Pt, 2, :], nU[:Pt, :], start=False, stop=True)
                yield
                for hd in range(2):
                    hs = slice(hd * 64, (hd + 1) * 64)
                    O.stt(Hfv(hs, hs), Hfv(hs, hs), gC[s][hs, ti:ti + 1], pS[hs, 384 + hd * 64:384 + (hd + 1) * 64],
                          ALU.mult, ALU.add, rk=[hkey, pS, gC[s]], wk=hkey)
                if sample:
                    O.dma(hs_d[seq0 + ti, p], HsF[:], "hsout")
                else:
                    O.copy(Hbv, HF[:, p, :], eng="act", rk=[hkey], wk=hbkey)
                yield
                for hd in range(2):
                    O.bn_stats(st6[:Pt, hd, :], pS[:Pt, 256 + hd * 64:256 + (hd + 1) * 64])
                    O.bn_aggr(mv[:Pt, hd, :], st6[:Pt, hd, :])
                rsqrt(rstd2[:Pt, :], mv[:Pt, :, 1], 2, 1.0, GN_EPS)
                for hd in range(2):
                    O.ts(ynb[:Pt, hd * 64:(hd + 1) * 64], pS[:Pt, 256 + hd * 64:256 + (hd + 1) * 64], mv[:Pt, hd, 0:1],
                         rstd2[:Pt, hd:hd + 1], op0=ALU.subtract, op1=ALU.mult)
                yield
                O.tr(ptr[:, 512:512 + Pt], ynb[:Pt, :], ident_b(Pt))
                O.act(ynT[s][:, sl], ptr[:, 512:512 + Pt], AF.Identity, scale=pv(PV_GNG, p), bias=pv(PV_GNB, p))
                yield
                if ti == nt - 1:
                    O.tt(ynT[s][:, 0:W], ynT[s][:, 0:W], bv[s][:, 0:W], ALU.add)
                    O.tt(outT[:, 8 + p, 0:W], ynT[s][:, 0:W], sgb[s][:, 0:W], ALU.mult, wk=(outT, ("b", p)))
                    yield

            def a_branch():
                for gi, g in enumerate((2, 3, 0, 1, 4, 5)):
                    w = load_wA(WbA, g)
                    col = (g % 2) * 512
                    for ti, (off, Pt) in enumerate(tiles):
                        half = (gi * nt + ti) % 2
                        acc = pzs[half][:Pt, :]
                        for c in range(16):
                            O.mm(acc, hT[:, c, off:off + Pt], w[:, c, :], start=(c == 0), stop=(c == 15), rk=[(hT, ti), w])
                        if g in (2, 3):
                            O.act(gv[:Pt, ti, col:col + 512], acc, AF.Gelu, wk=(gv, ti))
                        elif g in (0, 1):
                            O.act(gus[:Pt, ti, col:col + 512], acc, AF.Gelu, wk=(gus, ti))
                        else:
                            O.act(gat[:Pt, :], acc, AF.Tanh, scale=0.5)
                            O.stt(gat2[:Pt, :], gat[:Pt, :], 1.0, acc, ALU.add, ALU.mult)
                            O.stt(gus[:Pt, ti, col:col + 512], gat2[:Pt, :], 0.5, gus[:Pt, ti, col:col + 512], ALU.mult,
                                  ALU.mult, rk=[(gus, ti), gat2], wk=(gus, ti))
                        yield
                    if g == 3:
                        for ti, (off, Pt) in enumerate(tiles):
                            for j in range(2):
                                O.bn_stats(st6a[:Pt, j, :], gv[:Pt, ti, j * 512:(j + 1) * 512], rk=[(gv, ti)])
                            O.bn_aggr(mva[:Pt, :], st6a[:Pt, :, :].rearrange("p a b -> p (a b)"))
                            rsqrt(sma[:Pt, 1:2], mva[:Pt, 1:2], 1, 1.0, LN_EPS)
                            O.ts(gv[:Pt, ti, :], gv[:Pt, ti, :], mva[:Pt, 0:1], sma[:Pt, 1:2], op0=ALU.subtract, op1=ALU.mult,
                                 rk=[(gv, ti)], wk=(gv, ti))
                            O.tt(gv[:Pt, ti, :], gv[:Pt, ti, :], lnbc[:Pt, 0, :], ALU.mult, rk=[(gv, ti), lnbc], wk=(gv, ti))
                            yield
                            if sample:
                                O.tt(gv[:Pt, ti, :], gv[:Pt, ti, :], lnbc[:Pt, 1, :], ALU.add, rk=[(gv, ti), lnbc], wk=(gv, ti))
                                O.dma(vn_d[seq0 * 64 + off:seq0 * 64 + off + Pt, :], gv[:Pt, ti, :], f"vn{ti}",
                                      rk=[(gv, ti)])
                                O.copy(vnb[:Pt, ti, :], gv[:Pt, ti, :], eng="act", rk=[(gv, ti)], wk=(vnb, ti))
                            else:
                                O.tt(vnb[:Pt, ti, :], gv[:Pt, ti, :], lnbc[:Pt, 1, :], ALU.add, rk=[(gv, ti), lnbc],
                                     wk=(vnb, ti))
                            yield
                for ti, (off, Pt) in enumerate(tiles):
                    for h in range(8):
                        O.mm(pzs[h // 4][:Pt, (h % 4) * 128:(h % 4 + 1) * 128], wsTb[:Pt, h, :Pt],
                             vnb[:Pt, ti, h * 128:(h + 1) * 128], rk=[wsTb, (vnb, ti)])
                    for h in range(8):
                        O.stt(oa[:Pt, h * 128:(h + 1) * 128], pzs[h // 4][:Pt, (h % 4) * 128:(h % 4 + 1) * 128],
                              pv(PV_SGUB, h)[:Pt, :], gus[:Pt, ti, h * 128:(h + 1) * 128], ALU.add, ALU.mult,
                              rk=[pzs[h // 4], (gus, ti), pvec])
                    yield
                    for c4 in range(2):
                        for j in range(4):
                            c = c4 * 4 + j
                            O.tr(ptr[:, j * Pt:(j + 1) * Pt], oa[:Pt, c * 128:(c + 1) * 128], ident_b(Pt))
                        src = ptr[:, 0:4 * Pt].rearrange("p (j t) -> p j t", t=Pt)
                        O.copy(outT[:, c4 * 4:c4 * 4 + 4, off:off + Pt], src, eng="act" if c4 % 2 else "dve",
                               wk=(outT, ("a", ti, c4)))
                        yield

            def out_proj():
                for g in range(4):
                    w = load_wA(WbO, g)
                    for ti, (off, Pt) in enumerate(tiles):
                        half = (g * nt + ti) % 2
                        acc = pzs[half][:Pt, :]
                        for c in range(16):
                            O.mm(acc, outT[:, c, off:off + Pt], w[:, c, :], start=(c == 0), stop=(c == 15), rk=[outT, w])
                        O.tt(xt[:Pt, xi(ti), g * 512:(g + 1) * 512], acc, xt[:Pt, xi(ti), g * 512:(g + 1) * 512], ALU.add,
                             rk=[pzs[half], (xt, xi(ti))], wk=(xt, xi(ti)))
                        yield
                for ti, (off, Pt) in enumerate(tiles):
                    O.act(junk[:Pt, :], xt[:Pt, xi(ti), :], AF.Square, accum=sm[:Pt, 5:6], rk=[(xt, xi(ti))])
                    rsqrt(sm[:Pt, 7:8], sm[:Pt, 5:6], 1, 1.0 / D, NORM_EPS)
                    O.stt(xt[:Pt, xi(ti), :], xt[:Pt, xi(ti), :], sm[:Pt, 7:8], fing[:Pt, :], ALU.mult, ALU.mult,
                          rk=[(xt, xi(ti)), fing], wk=(xt, xi(ti)))
                    O.dma(y_dst[off:off + Pt, :], xt[:Pt, xi(ti), :], f"y{xi(ti)}", rk=[(xt, xi(ti))])
                    yield

            run_seq(stage0())
            run_seq(lora())
            run_seq(prep(0))
            NU = 8 * nt
            ab = a_branch()
            n_inv = 3 + 2 * int(math.log2(tiles[0][1]))
            if nt == 2:
                run_merged([(inv_unit(0), n_inv), (ab_slice(ab, 4), 4)])
                for u in range(NU):
                    p = u // nt
                    gens = [(state_unit(u), 7)]
                    if u + 1 < NU:
                        gens.append((inv_unit(u + 1), n_inv))
                    if u % nt == 0 and p + 1 < 8:
                        gens.append((prep(p + 1), 13))
                    else:
                        gens.append((ab_slice(ab, 5), 5))
                    run_merged(gens)
            else:
                for u in range(NU):
                    if u > 0 and u % nt == 0:
                        run_seq(prep(u // nt))
                    run_merged([(inv_unit(u), n_inv), (ab_slice(ab, 4), 4)])
                    run_seq(state_unit(u))
            run_seq(ab)
            run_seq(out_proj())

        def ab_slice(g, n):
            for _ in range(n):
                try:
                    next(g)
                except StopIteration:
                    return
                yield

        for sbi in range(NP // SBW):
            superblock(xp_d[sbi * SBW:(sbi + 1) * SBW, :], yp_d[sbi * SBW:(sbi + 1) * SBW, :],
                       [(i * 128, 128) for i in range(SBW // 128)], SBW, False, 0, sbi % 2)
        O.dma(shp_d, lastcol[:], "misc_out")
        O.dma(hp_d.rearrange("q p f -> p q f"), HF[:], "misc_out")
        for s0 in range(0, NS, NT):
            n = min(NT, NS - s0)
            superblock(xs_d[s0 * 64:(s0 + n) * 64, :], ys_d[s0 * 64:(s0 + n) * 64, :], [(i * 64, 64) for i in range(n)], n * 64, True, s0,
                       (NP // SBW + s0 // NT) % 2)
        if NS > 0:
            O.dma(shs_d, shs[:], "misc_out")
        out_tags = ["misc_out", "hsout"] + [f"y{i}" for i in range(2 * NT)] + [f"vn{i}" for i in range(NT)]
        P.emit_all(out_tags, eng="sp")
        n_ops = len(P.ops)
    return nc, n_ops
N_CORES = 8
_NC_CACHE = {}


def _consts():
    i = np.arange(128)
    ident = np.eye(128, dtype=np.float32)
    incl = (i[:, None] <= i[None, :]).astype(np.float32)
    strict = (i[:, None] < i[None, :]).astype(np.float32)
    low = (i[:, None] > i[None, :]).astype(np.float32)
    blk = ((i[:, None] // 64) == (i[None, :] // 64)).astype(np.float32)
    return np.ascontiguousarray(np.stack([ident, incl, strict, low, blk], axis=1))


def _cols(v, n):
    return np.ascontiguousarray(np.asarray(v, np.float32).reshape(n, 128).T)


def _shared_inputs(norm_g, w_in, w_out, sgu_ln_g, sgu_ln_b, sgu_w, sgu_b, shift_mu, w0, w2, a0, a2, k_k, k_a, r_k,
                   gn_g, gn_b, final_g):
    pvec = np.concatenate([
        _cols(shift_mu[0], 25), _cols(w0[0], 8), _cols(a0[0], 8), _cols(k_k[0], 8), _cols(k_a[0], 8),
        _cols(np.asarray(r_k[0]).reshape(-1), 8), _cols(gn_g[0], 8), _cols(gn_b[0], 8),
        np.ascontiguousarray(np.asarray(sgu_b[0], np.float32).T),
        np.stack([(np.arange(128) < 64), (np.arange(128) >= 64)], axis=1).astype(np.float32)], axis=1)
    return {
        "w_in": np.ascontiguousarray(np.asarray(w_in[0], np.float32)),
        "w_out": np.ascontiguousarray(np.asarray(w_out[0], np.float32)),
        "i_normg": _cols(norm_g[0], 16),
        "i_cmat": _consts(),
        "i_lnbc": np.ascontiguousarray(np.broadcast_to(
            np.stack([np.asarray(sgu_ln_g[0], np.float32), np.asarray(sgu_ln_b[0], np.float32)])[None], (128, 2, DA))),
        "i_fing": np.ascontiguousarray(np.broadcast_to(np.asarray(final_g, np.float32)[None], (128, D))),
        "i_pvec": np.ascontiguousarray(pvec.astype(np.float32)),
        "i_wsT": np.ascontiguousarray(np.transpose(np.asarray(sgu_w[0], np.float32), (2, 0, 1))),
        "i_w2a2": np.ascontiguousarray(np.concatenate([np.asarray(w2[0], np.float32), np.asarray(a2[0], np.float32)], 0)),
    }


def _h_blockdiag(S):
    n = S.shape[0]
    out = np.zeros((n, 8, 128, 128), np.float32)
    Ht = np.transpose(S, (0, 1, 3, 2))
    out[:, :, 0:64, 0:64] = Ht[:, 0::2]
    out[:, :, 64:128, 64:128] = Ht[:, 1::2]
    return out


def _h_unblock(Hbd):
    n = Hbd.shape[0]
    S = np.zeros((n, 16, 64, 64), np.float32)
    S[:, 0::2] = np.transpose(Hbd[:, :, 0:64, 0:64], (0, 1, 3, 2))
    S[:, 1::2] = np.transpose(Hbd[:, :, 64:128, 64:128], (0, 1, 3, 2))
    return S


def run_layout(x_prompt, x_sample, state_b_wkv, state_b_shift, shared, n_cores, NP, NS):
    key = (NP, NS)
    if key not in _NC_CACHE:
        _NC_CACHE[key] = build(NP, NS)[0]
    nc = _NC_CACHE[key]
    B = x_prompt.shape[0]
    in_maps = []
    for c in range(n_cores):
        m = dict(shared)
        m["xp"] = np.ascontiguousarray(x_prompt[c], np.float32) if c < B else np.zeros((NP, D), np.float32)
        sl = slice(c * NS, (c + 1) * NS)
        m["xs"] = np.ascontiguousarray(np.asarray(x_sample[sl], np.float32).reshape(NS * 64, D))
        m["hs0"] = _h_blockdiag(np.asarray(state_b_wkv[0, sl], np.float32))
        sh = np.asarray(state_b_shift[0, sl, 0, :], np.float32).reshape(NS, 25, 128)
        m["shiftT"] = np.ascontiguousarray(np.transpose(sh, (2, 0, 1)))
        in_maps.append(m)
    res = run_bass_kernel_spmd(nc, in_maps, core_ids=list(range(n_cores)))
    R = res.results
    y_prompt = np.stack([R[c]["yp"] for c in range(B)]).astype(np.float32)
    y_sample = np.concatenate([R[c]["ys"].reshape(NS, 64, D) for c in range(n_cores)]).astype(np.float32)
    wkv_p = np.concatenate([_h_unblock(R[c]["hp_out"][None]) for c in range(B)])[None]
    shp = np.stack([R[c]["shp_out"].T.reshape(1, NSH) for c in range(B)])[None]
    wkv_s = np.concatenate([_h_unblock(R[c]["hs_out"]) for c in range(n_cores)])[None]
    shs = np.concatenate([np.transpose(R[c]["shs_out"], (1, 2, 0)).reshape(NS, 1, NSH) for c in range(n_cores)])[None]
    vn = np.concatenate([R[c]["vn_out"].reshape(NS, 64, DA) for c in range(n_cores)])[None]
    return (y_prompt, y_sample, wkv_p.astype(np.float32), shp.astype(np.float32), wkv_s.astype(np.float32),
            shs.astype(np.float32), vn.astype(np.float32))


def kernel(x_prompt, x_sample, state_b_wkv, state_b_shift, norm_g, w_in, w_out, sgu_ln_g, sgu_ln_b,
           sgu_w, sgu_b, shift_mu, w0, w2, a0, a2, k_k, k_a, r_k, gn_g, gn_b, final_g):
    shared = _shared_inputs(norm_g, w_in, w_out, sgu_ln_g, sgu_ln_b, sgu_w, sgu_b, shift_mu, w0, w2, a0, a2,
                            k_k, k_a, r_k, gn_g, gn_b, final_g)
    x_prompt = np.asarray(x_prompt); x_sample = np.asarray(x_sample)
    return run_layout(x_prompt, x_sample, np.asarray(state_b_wkv), np.asarray(state_b_shift), shared,
                      N_CORES, x_prompt.shape[1], x_sample.shape[0] // N_CORES)
```

```python
from concourse.bass_utils import run_bass_kernel_spmd
import numpy as np
import concourse.bass as bass
import concourse.mybir as mybir

F32 = mybir.dt.float32
BF16 = mybir.dt.bfloat16
AF = mybir.ActivationFunctionType
ALU = mybir.AluOpType

SEM_CAP = 16384


def _key(x):
    sub = None
    if isinstance(x, tuple):
        x, sub = x
    t = getattr(x, "tensor", x)
    return (t.name, sub)


class Prog:
    COMPUTE = ("pe", "act", "dve", "pool")

    def __init__(self, nc, stack):
        self.nc = nc
        self.stack = stack
        self.ops = []
        self.state = {}
        self.eng = {"pe": nc.tensor, "act": nc.scalar, "dve": nc.vector, "pool": nc.gpsimd, "sp": nc.sync}
        self.psum_names = set()
        self.bank_last = {}

    def sb(self, name, shape, dt):
        return self.stack.enter_context(self.nc.sbuf_tensor(name, list(shape), dt))

    def ps(self, name, shape, dt):
        self.psum_names.add(name)
        return self.stack.enter_context(self.nc.psum_tensor(name, list(shape), dt))

    def _states(self, key, create=True):
        name, sub = key
        d = self.state.setdefault(name, {})
        if sub is None:
            if None not in d:
                d[None] = [None, []]
            return list(d.values())
        out = []
        if sub not in d:
            d[sub] = [None, []]
        out.append(d[sub])
        if None in d:
            out.append(d[None])
        return out

    def op(self, eng, fn, reads=(), writes=(), dma_tag=None, wait_all=False, cost=100.0, lat=0.0):
        idx = len(self.ops)
        deps = {}
        rk = [_key(r) for r in reads if r is not None and not isinstance(r, (int, float))]
        wk = [_key(w) for w in writes if w is not None]
        for k in rk:
            for st in self._states(k):
                if st[0] is not None:
                    deps.setdefault(st[0], "raw")
        for k in wk:
            for st in self._states(k):
                if st[0] is not None:
                    deps.setdefault(st[0], "waw")
                for r in st[1]:
                    if r != idx:
                        deps.setdefault(r, "war")
        for name in {k[0] for k in rk + wk if k[0] in self.psum_names}:
            bl = self.bank_last.setdefault(name, {})
            for e2, i2 in bl.items():
                if e2 != eng:
                    deps.setdefault(i2, "bank")
                else:
                    deps.setdefault(i2, "order")
            bl[eng] = idx
        for k in rk:
            name, sub = k
            if sub is None:
                for st in self._states(k):
                    st[1].append(idx)
            else:
                self.state[name][sub][1].append(idx)
        for k in wk:
            name, sub = k
            if sub is None:
                d = self.state[name]
                for s in list(d.keys()):
                    d[s] = [idx, []]
            else:
                self.state[name][sub] = [idx, []]
        o = dict(eng=eng, fn=fn, deps=deps, signal=False, dma_tag=dma_tag, wait_all=wait_all, val=None,
                 cost=float(cost), lat=float(lat))
        need = []
        for d, kind in deps.items():
            p = self.ops[d]
            if p["dma_tag"] is None and dma_tag is None and p["eng"] == eng:
                if eng == "pe":
                    continue
                if kind == "order":
                    continue
            need.append(d)
            p["signal"] = True
        o["need"] = need
        if dma_tag is not None:
            o["signal"] = True
        self.ops.append(o)
        return idx


    def schedule(self, window=64, slack=120.0):
        import bisect
        ops = self.ops
        n = len(ops)
        succ = [[] for _ in range(n)]
        ndeps = [0] * n
        for i, o in enumerate(ops):
            ndeps[i] = len(o["deps"])
            for d in o["deps"]:
                succ[d].append(i)
        ready = [0.0] * n
        engs = sorted({o["eng"] for o in ops})
        free = {e: 0.0 for e in engs}
        rel = {e: [] for e in engs}
        for i, o in enumerate(ops):
            if ndeps[i] == 0:
                rel[o["eng"]].append(i)
        order = []
        done = 0
        cur_tbl = [None]
        TBL = 1300.0

        def eff(e, t, i):
            st_ = max(t, ready[i])
            if e == "act":
                tb = ops[i].get("tbl")
                if tb is not None and tb != cur_tbl[0]:
                    st_ += TBL
            return st_
        while done < n:
            best = None
            for e in engs:
                cand = rel[e]
                if not cand:
                    continue
                t = free[e]
                lim = cand[:window]
                tmin = min(eff(e, t, i) for i in lim)
                for i in lim:
                    if eff(e, t, i) <= tmin + slack:
                        pick = i
                        break
                stt = eff(e, t, pick)
                if best is None or stt < best[0]:
                    best = (stt, e, pick)
            stt, e, i = best
            rel[e].remove(i)
            o = ops[i]
            if e == "act" and o.get("tbl") is not None:
                cur_tbl[0] = o["tbl"]
            fin = stt + o["cost"] + o["lat"]
            free[e] = stt + o["cost"]
            order.append(i)
            done += 1
            for sidx in succ[i]:
                so = ops[sidx]
                if o["dma_tag"] is not None or so["eng"] != e:
                    l = 180.0
                elif e == "pe":
                    l = 0.0
                else:
                    l = 60.0 if o["deps"] and ops[sidx]["deps"].get(i) in ("raw", "waw") else 0.0
                r = fin + l
                if r > ready[sidx]:
                    ready[sidx] = r
                ndeps[sidx] -= 1
                if ndeps[sidx] == 0:
                    bisect.insort(rel[so["eng"]], sidx)
        self.est_ns = max(free.values())
        remap = {old: new for new, old in enumerate(order)}
        new_ops = []
        for old in order:
            o = ops[old]
            o["deps"] = {remap[d]: k for d, k in o["deps"].items()}
            o["need"] = [remap[d] for d in o["need"]]
            new_ops.append(o)
        self.ops = new_ops

    def emit(self):
        nc = self.nc
        counters = {}
        for o in self.ops:
            if not o["signal"]:
                continue
            key = ("dma", o["dma_tag"]) if o["dma_tag"] is not None else ("eng", o["eng"])
            inc = 16 if o["dma_tag"] is not None else 1
            counters[key] = counters.get(key, 0) + inc
            o["key"] = key
            o["val"] = counters[key]
        totals = dict(counters)
        hw = {}

        def hwsem(key, k):
            if (key, k) not in hw:
                hw[(key, k)] = self.stack.enter_context(nc.semaphore(f"s_{key[0]}_{key[1]}_{k}"))
            return hw[(key, k)]

        waited = {}
        n_wait = 0
        for o in self.ops:
            e = self.eng[o["eng"]]
            tgt = {}
            for d in o["need"]:
                p = self.ops[d]
                v = totals[p["key"]] if p["wait_all"] else p["val"]
                tgt[p["key"]] = max(tgt.get(p["key"], 0), v)
            for key, v in tgt.items():
                if waited.get((o["eng"], key), 0) >= v:
                    continue
                waited[(o["eng"], key)] = v
                k = (v - 1) // SEM_CAP
                e.wait_ge(hwsem(key, k), v - k * SEM_CAP)
                n_wait += 1
            ins = o["fn"](e)
            if o["signal"]:
                v = o["val"]
                k = (v - 1) // SEM_CAP
                inc = 16 if o["dma_tag"] is not None else 1
                ins.then_inc(hwsem(o["key"], k), inc)
        self.n_wait = n_wait
        return totals, hwsem

    def finish(self, out_tags, eng="sp"):
        totals, hwsem = self._fin
        e = self.eng[eng]
        for t in out_tags:
            key = ("dma", t)
            if key in totals:
                v = totals[key]
                k = (v - 1) // SEM_CAP
                e.wait_ge(hwsem(key, k), v - k * SEM_CAP)

    def emit_all(self, out_tags, eng="sp", sched=True):
        if sched:
            self.schedule()
        self._fin = self.emit()
        self.finish(out_tags, eng)


def _nfree(ap):
    n = 1
    for d in list(ap.shape)[1:]:
        n *= int(d)
    return n


class Ops:
    def __init__(self, prog):
        self.p = prog

    def _ew(self, out, *ins):
        n = _nfree(out)
        ps = any((getattr(getattr(x, "tensor", None), "name", None) in self.p.psum_names) for x in ins if x is not None
                 and not isinstance(x, (int, float)))
        return (160.0 + 1.05 * n) if ps else (100.0 + 0.8 * n)

    @staticmethod
    def _aps(*xs):
        return [x for x in xs if x is not None and not isinstance(x, (int, float))]

    def mm(self, out, lhsT, rhs, start=True, stop=True, wk=None, rk=()):
        self.p.op("pe", lambda e: e.matmul(out, lhsT, rhs, start=start, stop=stop),
                  reads=[lhsT, rhs] if not rk else list(rk), writes=[wk if wk is not None else out],
                  cost=8.0 + 0.43 * max(_nfree(rhs), 64), lat=120.0)

    def tr(self, out, in_, ident, wk=None, rk=()):
        self.p.op("pe", lambda e: e.transpose(out, in_, ident),
                  reads=[in_, ident] if not rk else list(rk) + [ident], writes=[wk if wk is not None else out],
                  cost=8.0 + 0.43 * max(_nfree(in_), 64), lat=120.0)

    def act(self, out, in_, func, bias=None, scale=None, accum=None, eng="act", wk=None, rk=None):
        kw = {}
        if bias is not None:
            kw["bias"] = bias
        if scale is not None:
            kw["scale"] = scale
        if accum is not None:
            kw["accum_out"] = accum
        reads = self._aps(in_, bias, scale) if rk is None else list(rk) + self._aps(bias, scale)
        writes = [wk if wk is not None else out] + ([accum] if accum is not None else [])
        tbl = {AF.Exp: "ln_exp", AF.Ln: "ln_exp", AF.Gelu: "gelu", AF.Tanh: "gelu", AF.Silu: "silu", AF.Sigmoid: "sig",
               AF.Sqrt: "sqrt"}.get(func)
        i = self.p.op("act", lambda e: e.activation(out, in_, func, **kw), reads=reads, writes=writes,
                      cost=self._ew(out, in_) + (90.0 if accum is not None else 0.0))
        self.p.ops[i]["tbl"] = tbl

    def tt(self, out, a, b, op, eng="dve", wk=None, rk=None):
        self.p.op(eng, lambda e: e.tensor_tensor(out, a, b, op),
                  reads=[a, b] if rk is None else list(rk), writes=[wk if wk is not None else out],
                  cost=self._ew(out, a, b) * (2.0 if eng == "pool" else 1.0))

    def ts(self, out, a, s1, s2=None, op0=ALU.mult, op1=None, eng="dve", wk=None, rk=None, accum=None):
        kw = {}
        if op1 is not None:
            kw["op1"] = op1
        if accum is not None:
            kw["accum_out"] = accum
        reads = self._aps(a, s1, s2) if rk is None else list(rk) + self._aps(s1, s2)
        writes = [wk if wk is not None else out] + ([accum] if accum is not None else [])
        self.p.op(eng, lambda e: e.tensor_scalar(out, a, s1, s2, op0, **kw), reads=reads, writes=writes,
                  cost=self._ew(out, a) * (2.0 if eng == "pool" else 1.0))

    def stt(self, out, in0, scalar, in1, op0, op1, eng="dve", wk=None, rk=None):
        reads = self._aps(in0, scalar, in1) if rk is None else list(rk) + self._aps(scalar)
        self.p.op(eng, lambda e: e.scalar_tensor_tensor(out, in0, scalar, in1, op0, op1),
                  reads=reads, writes=[wk if wk is not None else out],
                  cost=self._ew(out, in0, in1) * (2.0 if eng == "pool" else 1.0))

    def copy(self, out, in_, eng="dve", wk=None, rk=None):
        if eng == "act":
            self.p.op("act", lambda e: e.copy(out, in_), reads=[in_] if rk is None else list(rk),
                      writes=[wk if wk is not None else out], cost=self._ew(out, in_))
        else:
            self.p.op(eng, lambda e: e.tensor_copy(out, in_), reads=[in_] if rk is None else list(rk),
                      writes=[wk if wk is not None else out], cost=self._ew(out, in_) * (2.0 if eng == "pool" else 1.0))

    def memset(self, out, val, eng="dve", wk=None):
        self.p.op(eng, lambda e: e.memset(out, val), reads=[], writes=[wk if wk is not None else out],
                  cost=60.0 + 0.5 * _nfree(out))

    def recip(self, out, in_, wk=None):
        self.p.op("dve", lambda e: e.reciprocal(out, in_), reads=[in_], writes=[wk if wk is not None else out],
                  cost=self._ew(out, in_) * 1.5)

    def scan(self, out, d0, d1, initial, op0, op1, wk=None, rk=None):
        reads = self._aps(d0, d1, initial) if rk is None else list(rk)
        self.p.op("dve", lambda e: e.tensor_tensor_scan(out, d0, d1, initial, op0, op1),
                  reads=reads, writes=[wk if wk is not None else out], cost=100.0 + 2.1 * _nfree(out))

    def bn_stats(self, out, in_, wk=None, rk=None):
        self.p.op("dve", lambda e: e.bn_stats(out, in_), reads=[in_] if rk is None else list(rk),
                  writes=[wk if wk is not None else out], cost=self._ew(in_, in_))

    def bn_aggr(self, out, in_, wk=None):
        self.p.op("dve", lambda e: e.bn_aggr(out, in_), reads=[in_], writes=[wk if wk is not None else out])

    def dma(self, out, in_, tag, q="sp", wk=None, rk=None, wait_all=False):
        nbytes = 128 * _nfree(out) * (2 if out.dtype == BF16 else 4)
        self.p.op(q, lambda e: e.dma_start(out=out, in_=in_), reads=[in_] if rk is None else list(rk),
                  writes=[wk if wk is not None else out], dma_tag=tag, wait_all=wait_all,
                  cost=70.0, lat=2000.0 + nbytes / 150.0)
import contextlib
import math

D = 2048
DA = 1024
NSH = 3200
DP = 7296
NORM_EPS = 1e-6
LN_EPS = 1e-5
GN_EPS = 64e-5
C0 = math.exp(-0.5)
NQ = 25
PV_MU, PV_W0, PV_A0, PV_KK, PV_KA, PV_RK, PV_GNG, PV_GNB, PV_SGUB = 0, 25, 33, 41, 49, 57, 65, 73, 81
PV_HM = 89
PV_N = 91


def build(NP, NS, SBW=256):
    nc = bass.Bass("TRN2", target_bir_lowering=False)
    st = contextlib.ExitStack()
    with st:
        P = Prog(nc, st)
        O = Ops(P)
        NST = NS * 64
        assert NP % SBW == 0
        din = lambda n, s, dt=F32: nc.dram_tensor(n, list(s), dt, kind="ExternalInput").ap()
        dout = lambda n, s, dt=F32: nc.dram_tensor(n, list(s), dt, kind="ExternalOutput").ap()
        xp_d = din("xp", [NP, D]); xs_d = din("xs", [NST, D])
        hs0_d = din("hs0", [NS, 8, 128, 128]); shT_d = din("shiftT", [128, NS, NQ])
        win_d = din("w_in", [D, DP]); wout_d = din("w_out", [D, D])
        normg_d = din("i_normg", [128, 16]); cmat_d = din("i_cmat", [128, 5, 128])
        lnbc_d = din("i_lnbc", [128, 2, DA]); fing_d = din("i_fing", [128, D]); pvec_d = din("i_pvec", [128, PV_N])
        wsT_d = din("i_wsT", [128, 8, 128]); w2a2_d = din("i_w2a2", [128, DA])
        yp_d = dout("yp", [NP, D]); ys_d = dout("ys", [NST, D])
        hp_d = dout("hp_out", [8, 128, 128]); hs_d = dout("hs_out", [NS, 8, 128, 128])
        shp_d = dout("shp_out", [128, NQ]); shs_d = dout("shs_out", [128, NS, NQ])
        vn_d = dout("vn_out", [NST, DA])
        WbA = nc.dram_tensor("WbA", [6, 128, 16, 512], BF16, kind="Internal").ap()
        WbB = nc.dram_tensor("WbB", [33, 128, 16, 128], BF16, kind="Internal").ap()
        WbO = nc.dram_tensor("WbO", [4, 128, 16, 512], BF16, kind="Internal").ap()

        NT = SBW // 128
        normg = P.sb("normg", [128, 16], F32)
        cmat = P.sb("cmat", [128, 5, 128], F32)
        cmb = P.sb("cmb", [128, 5, 128], BF16)
        lnbc = P.sb("lnbc", [128, 2, DA], F32)
        fing = P.sb("fing", [128, D], F32)
        pvec = P.sb("pvec", [128, PV_N], F32)
        omka = P.sb("omka", [128, 8], F32)
        hbias = P.sb("hbias", [128, 16], F32)
        mhalf = P.sb("mhalf", [128, SBW], F32)
        tg = P.sb("tg", [128, SBW], F32)
        wsTb = P.sb("wsTb", [128, 8, 128], BF16)
        w2a2b = P.sb("w2a2b", [128, DA], BF16)
        HF = P.sb("HF", [128, 8, 128], F32)
        HB = P.sb("HB", [128, 8, 128], BF16)
        HsF = P.sb("HsF", [128, 128], F32)
        HsB = P.sb("HsB", [128, 128], BF16)
        lastcol = P.sb("lastcol", [128, NQ], F32)
        shT = P.sb("shT", [128, NS, NQ], F32)
        shs = P.sb("shs", [128, NS, NQ], F32)
        xt = P.sb("xt", [128, 2 * NT, D], F32)
        junk = P.sb("junk", [128, D], BF16)
        xn = P.sb("xn", [128, D], BF16)
        hT = P.sb("hT", [128, 16, SBW], BF16)
        outT = P.sb("outT", [128, 16, SBW], BF16)
        wB = [P.sb(f"wB{i}", [128, 16, 128], BF16) for i in range(3)]
        wA = [P.sb(f"wA{i}", [128, 16, 512], BF16) for i in range(2)]
        gv = P.sb("gv", [128, NT, DA], F32)
        gus = P.sb("gus", [128, NT, DA], BF16)
        gat = P.sb("gat", [128, 512], F32)
        gat2 = P.sb("gat2", [128, 512], BF16)
        vnb = P.sb("vnb", [128, NT, DA], BF16)
        oa = P.sb("oa", [128, DA], BF16)
        st6 = P.sb("st6", [128, 2, 6], F32)
        mv = P.sb("mv", [128, 2, 2], F32)
        sm = P.sb("sm", [128, 8], F32)
        f32t = lambda n: P.sb(n, [128, SBW], F32)
        zraw = P.sb("zraw", [128, SBW + 1], F32)
        diff = f32t("diff")
        mixl, mixr, mixk, mixv = f32t("mixl"), f32t("mixr"), f32t("mixk"), f32t("mixv")
        sgb = [f32t(f"sgb{i}") for i in range(2)]; bv = [f32t(f"bv{i}") for i in range(2)]; ynT = [f32t(f"ynT{i}") for i in range(2)]
        sd, aa, cum, excl = f32t("sd"), f32t("aa"), f32t("cum"), f32t("excl")
        e_incl, e_excl, e_inv, e_rem = f32t("e_incl"), f32t("e_excl"), f32t("e_inv"), f32t("e_rem")
        kk, sq, rn, tmp, kp, bbv, rkv = (f32t(n) for n in ("kk", "sq", "rn", "tmp", "kp", "bbv", "rkv"))
        ones = f32t("ones")
        nb = P.sb("nb", [128, NT], F32)
        gC = [P.sb(f"gC{i}", [128, NT], F32) for i in range(2)]
        th = P.sb("th", [128, SBW], BF16)
        b16 = lambda n: P.sb(n, [128, SBW], BF16)
        kt, rt, khg, bhg, vbf = ([b16(f"{n}{i}") for i in range(2)] for n in ("kt", "rt", "khg", "bhg", "vbf"))
        kh = [[b16(f"kh{i}_{h}") for h in range(2)] for i in range(2)]
        bh = [[b16(f"bh{i}_{h}") for h in range(2)] for i in range(2)]
        tok3 = [P.sb(f"tok3_{i}", [128, 3, 128], BF16) for i in range(2)]
        Asb = [[P.sb(f"Asb{i}_{h}", [128, 3, 128], BF16) for h in range(2)] for i in range(2)]
        XMS = [[[P.sb(f"XMS{i}_{h}_{j}", [128, 384], BF16) for j in range(2)] for h in range(2)] for i in range(2)]
        TT = [[P.sb(f"TT{i}_{h}", [128, 128], BF16) for h in range(2)] for i in range(2)]
        st6a = P.sb("st6a", [128, 2, 6], F32)
        mva = P.sb("mva", [128, 2], F32)
        sma = P.sb("sma", [128, 2], F32)
        Wsb = P.sb("Wsb", [128, 128], BF16)
        nU = P.sb("nU", [128, 128], BF16)
        ynb = P.sb("ynb", [128, 128], BF16)
        rstd2 = P.sb("rstd2", [128, 2], F32)
        pzs = [P.ps(f"pz{i}", [128, 512], F32) for i in range(2)]
        pA = P.ps("pA", [128, 512], F32)
        pI = [P.ps(f"pI{h}", [128, 512], F32) for h in range(2)]
        pS = P.ps("pS", [128, 512], F32)
        pm = P.ps("pm", [128, 512], F32)
        ptr = P.ps("ptr", [128, 1024], BF16)

        ident_b = lambda n: cmb[:n, 0, :n]
        m_incl = lambda n: cmat[:n, 1, :n]
        m_strict = lambda n: cmat[:n, 2, :n]
        m_low = lambda n: cmat[:n, 3, :n]
        blockones = cmat[:, 4, :]
        pv = lambda base, j: pvec[:, base + j:base + j + 1]

        cl = lambda out, in_: O.dma(out, in_, "const", wait_all=True)
        cl(normg[:], normg_d); cl(cmat[:], cmat_d); cl(lnbc[:], lnbc_d); cl(fing[:], fing_d); cl(pvec[:], pvec_d)
        wsTf = gv[:, 0, :].rearrange("p (h i) -> p h i", i=128)
        w2a2f = gv[:, 1, :]
        O.dma(wsTf, wsT_d, "const", wait_all=True, wk=(gv, 0)); O.dma(w2a2f, w2a2_d, "const", wait_all=True, wk=(gv, 1)); cl(shT[:], shT_d)
        O.copy(cmb[:], cmat[:])
        O.copy(w2a2b[:], w2a2f, eng="act", rk=[(gv, 1)])
        O.ts(omka[:], pvec[:, PV_KA:PV_KA + 8], -1.0, 1.0, op0=ALU.mult, op1=ALU.add)
        O.ts(hbias[:], pvec[:, PV_W0:PV_W0 + 16], -1.0, None, op0=ALU.mult)
        O.memset(mhalf[:], -0.5)
        for h in range(8):
            O.tt(wsTb[:, h, :], wsTf[:, h, :], cmat[:, 1, :], ALU.mult, rk=[(gv, 0), cmat])
        O.memset(HF[:], 0.0); O.memset(HB[:], 0.0); O.memset(lastcol[:], 0.0); O.memset(ones[:], 1.0)
        O.memset(zraw[:, 0:1], 0.0)

        pieces = [(7168, 128), (3072, 2048), (5120, 2048), (0, 2048), (2048, 1024)]
        stg_f = [xt[:, 0, :], xt[:, 1, :]]
        stg_b = [junk, xn]
        k = 0
        for pi, (c0, w) in enumerate(pieces):
            for c in range(16):
                sf = stg_f[k % 2]; sbf = stg_b[k % 2]
                O.dma(sf[:, 0:w], win_d[c * 128:(c + 1) * 128, c0:c0 + w], f"stgf{k % 2}", wk=(xt, k % 2))
                if k % 2 == 0:
                    O.ts(sbf[:, 0:w], sf[:, 0:w], normg[:, c:c + 1], None, op0=ALU.mult, rk=[(xt, k % 2)])
                else:
                    O.act(sbf[:, 0:w], sf[:, 0:w], AF.Copy, scale=normg[:, c:c + 1], rk=[(xt, k % 2)])
                if c0 < 3072:
                    g0 = c0 // 512; ng = w // 512
                    for gg in range(ng):
                        O.dma(WbA[g0 + gg, :, c, :], sbf[:, gg * 512:(gg + 1) * 512], f"stgb{k % 2}_{gg}", wk=(WbA, (g0 + gg, c)))
                else:
                    q0 = (c0 - 3072) // 128; nq = w // 128
                    for qq in range(0, nq, 4):
                        nn = min(4, nq - qq)
                        O.dma(WbB[q0 + qq:q0 + qq + nn, :, c, :].rearrange("q p n -> p q n"),
                              sbf[:, qq * 128:(qq + nn) * 128].rearrange("p (q n) -> p q n", n=128), f"stgb{k % 2}_{qq // 4}",
                              wk=(WbB, ((q0 + qq) // 4, c)))
                k += 1
        for c in range(16):
            sf = stg_f[k % 2]; sbf = stg_b[k % 2]
            O.dma(sf[:, 0:2048], wout_d[c * 128:(c + 1) * 128, :], f"stgf{k % 2}", wk=(xt, k % 2))
            O.copy(sbf[:, 0:2048], sf[:, 0:2048], eng="act" if k % 2 else "dve", rk=[(xt, k % 2)])
            for gg in range(4):
                O.dma(WbO[gg, :, c, :], sbf[:, gg * 512:(gg + 1) * 512], f"stgb{k % 2}_{gg}", wk=(WbO, (gg, c)))
            k += 1

        wslotB = [0]
        wslotA = [0]

        def load_wB(q):
            s = wslotB[0] % 3; wslotB[0] += 1
            O.dma(wB[s][:], WbB[q], f"wB{s}", rk=[(WbB, (q // 4, c)) for c in range(16)])
            return wB[s]

        def load_wA(T, g):
            s = wslotA[0] % 2; wslotA[0] += 1
            O.dma(wA[s][:], T[g], f"wA{s}", rk=[(T, (g, c)) for c in range(16)])
            return wA[s]

        def rsqrt(out, in_, n, scale, eps):
            O.ts(out, in_, scale, eps, op0=ALU.mult, op1=ALU.add)
            O.act(out, out, AF.Ln)
            O.act(out, out, AF.Exp, scale=-0.5)

        def run_merged(gens):
            act = [[g, float(w), 0] for g, w in gens if g is not None]
            while act:
                a = min(act, key=lambda t: t[2] / t[1])
                try:
                    next(a[0]); a[2] += 1
                except StopIteration:
                    act.remove(a)

        def run_seq(g):
            for _ in g:
                pass

        def superblock(x_src, y_dst, tiles, W, sample, seq0=0, sbp=0):
            nt = len(tiles)
            xi = lambda ti: sbp * NT + ti

            def stage0():
                for ti, (off, Pt) in enumerate(tiles):
                    O.dma(xt[:Pt, xi(ti), :], x_src[off:off + Pt, :], f"x{xi(ti)}", wk=(xt, xi(ti)))
                    O.act(junk[:Pt, :], xt[:Pt, xi(ti), :], AF.Square, accum=sm[:Pt, 0:1], rk=[(xt, xi(ti))])
                    rsqrt(sm[:Pt, 2:3], sm[:Pt, 0:1], 1, 1.0 / D, NORM_EPS)
                    O.ts(xn[:Pt, :], xt[:Pt, xi(ti), :], sm[:Pt, 2:3], None, op0=ALU.mult, rk=[(xt, xi(ti))])
                    for c4 in range(4):
                        for j in range(4):
                            c = c4 * 4 + j
                            O.tr(ptr[:, j * Pt:(j + 1) * Pt], xn[:Pt, c * 128:(c + 1) * 128], ident_b(Pt))
                        src = ptr[:, 0:4 * Pt].rearrange("p (j t) -> p j t", t=Pt)
                        O.copy(hT[:, c4 * 4:c4 * 4 + 4, off:off + Pt], src, eng="act" if c4 % 2 else "dve", wk=(hT, ti))
                        yield

            def proj_B(q, half):
                w = load_wB(q)
                for c in range(16):
                    O.mm(pzs[half][:, 0:W], w[:, c, :], hT[:, c, 0:W], start=(c == 0), stop=(c == 15))
                return pzs[half][:, 0:W]

            def shifted(q, half, dst):
                src = proj_B(q, half)
                O.copy(zraw[:, 1:W + 1], src, eng="act")
                if not sample:
                    O.copy(zraw[:, 0:1], lastcol[:, q:q + 1])
                O.tt(diff[:, 0:W], zraw[:, 0:W], zraw[:, 1:W + 1], ALU.subtract)
                if sample:
                    for ti, (off, Pt) in enumerate(tiles):
                        O.tt(diff[:, off:off + 1], shT[:, seq0 + ti, q:q + 1], zraw[:, off + 1:off + 2], ALU.subtract)
                        O.copy(shs[:, seq0 + ti, q:q + 1], zraw[:, off + Pt:off + Pt + 1])
                else:
                    O.copy(lastcol[:, q:q + 1], zraw[:, W:W + 1])
                O.stt(dst[:, 0:W], diff[:, 0:W], pv(PV_MU, q), zraw[:, 1:W + 1], ALU.mult, ALU.add)

            def lora():
                shifted(24, 0, mixl)
                O.act(tg[0:64, 0:W], mixl[0:64, 0:W], AF.Exp, scale=-2.0)
                O.act(tg[0:64, 0:W], tg[0:64, 0:W], AF.Ln, bias=1.0)
                O.act(tg[0:64, 0:W], tg[0:64, 0:W], AF.Exp, scale=-1.0)
                O.ts(th[0:64, 0:W], tg[0:64, 0:W], 2.0, -1.0, op0=ALU.mult, op1=ALU.add)
                O.copy(th[64:128, 0:W], mixl[64:128, 0:W])
                yield

            def prep(p):
                s = p % 2
                shifted(p, 1, mixr); yield
                shifted(8 + p, 0, mixk); yield
                shifted(16 + p, 1, mixv); yield
                src = proj_B(25 + p, 0)
                O.act(tg[:, 0:W], src, AF.Exp, scale=-1.0)
                O.act(tg[:, 0:W], tg[:, 0:W], AF.Ln, bias=1.0)
                O.act(tg[:, 0:W], tg[:, 0:W], AF.Exp, scale=-1.0)
                O.tt(sgb[s][:, 0:W], tg[:, 0:W], src, ALU.mult); yield
                O.mm(pm[:, 0:W], w2a2b[0:64, p * 128:(p + 1) * 128], th[0:64, 0:W])
                O.act(sd[:, 0:W], pm[:, 0:W], AF.Exp, bias=hbias[:, p:p + 1], scale=-1.0)
                O.act(sd[:, 0:W], sd[:, 0:W], AF.Ln, bias=1.0)
                O.act(sd[:, 0:W], sd[:, 0:W], AF.Exp, scale=-1.0)
                O.mm(pm[:, 256:256 + W], w2a2b[64:128, p * 128:(p + 1) * 128], th[64:128, 0:W])
                O.act(aa[:, 0:W], pm[:, 256:256 + W], AF.Exp, bias=hbias[:, 8 + p:9 + p], scale=-1.0)
                O.act(aa[:, 0:W], aa[:, 0:W], AF.Ln, bias=1.0)
                O.act(aa[:, 0:W], aa[:, 0:W], AF.Exp, scale=-1.0); yield
                for ti, (off, Pt) in enumerate(tiles):
                    O.scan(cum[:, off:off + Pt], ones[:, off:off + Pt], sd[:, off:off + Pt], 0.0, ALU.mult, ALU.add)
                    O.ts(nb[:, ti:ti + 1], cum[:, off + Pt - 1:off + Pt], -C0, None, op0=ALU.mult)
                    O.act(gC[s][:, ti:ti + 1], cum[:, off + Pt - 1:off + Pt], AF.Exp, scale=-C0)
                    O.act(e_rem[:, off:off + Pt], cum[:, off:off + Pt], AF.Exp, scale=C0, bias=nb[:, ti:ti + 1])
                yield
                O.act(e_incl[:, 0:W], cum[:, 0:W], AF.Exp, scale=-C0)
                O.tt(excl[:, 0:W], cum[:, 0:W], sd[:, 0:W], ALU.subtract, eng="pool")
                O.act(e_excl[:, 0:W], excl[:, 0:W], AF.Exp, scale=-C0)
                O.act(e_inv[:, 0:W], cum[:, 0:W], AF.Exp, scale=C0); yield
                O.ts(kk[:, 0:W], mixk[:, 0:W], pv(PV_KK, p), None, op0=ALU.mult)
                O.tt(sq[:, 0:W], kk[:, 0:W], kk[:, 0:W], ALU.mult, eng="pool")
                O.mm(pm[:, 0:W], blockones, sq[:, 0:W])
                O.ts(rn[:, 0:W], pm[:, 0:W], 1e-24, None, op0=ALU.max)
                O.act(rn[:, 0:W], rn[:, 0:W], AF.Ln)
                O.act(rn[:, 0:W], rn[:, 0:W], AF.Exp, scale=-0.5); yield
                O.tt(kk[:, 0:W], kk[:, 0:W], rn[:, 0:W], ALU.mult)
                O.ts(tmp[:, 0:W], aa[:, 0:W], pv(PV_KA, p), omka[:, p:p + 1], op0=ALU.mult, op1=ALU.add, eng="pool")
                O.tt(kp[:, 0:W], mixk[:, 0:W], tmp[:, 0:W], ALU.mult)
                O.tt(bbv[:, 0:W], kk[:, 0:W], aa[:, 0:W], ALU.mult); yield
                O.tt(kt[s][:, 0:W], kk[:, 0:W], e_excl[:, 0:W], ALU.mult, eng="pool")
                O.tt(rt[s][:, 0:W], mixr[:, 0:W], e_incl[:, 0:W], ALU.mult)
                for hd in range(2):
                    O.stt(kh[s][hd][:, 0:W], kp[:, 0:W], pv(PV_HM, hd), e_inv[:, 0:W], ALU.mult, ALU.mult)
                yield
                for hd in range(2):
                    O.stt(bh[s][hd][:, 0:W], bbv[:, 0:W], pv(PV_HM, hd), e_inv[:, 0:W], ALU.mult, ALU.mult)
                O.tt(khg[s][:, 0:W], kp[:, 0:W], e_rem[:, 0:W], ALU.mult, eng="pool")
                O.tt(bhg[s][:, 0:W], bbv[:, 0:W], e_rem[:, 0:W], ALU.mult); yield
                O.copy(vbf[s][:, 0:W], mixv[:, 0:W], eng="act")
                O.stt(rkv[:, 0:W], mixr[:, 0:W], pv(PV_RK, p), kp[:, 0:W], ALU.mult, ALU.mult)
                O.mm(pm[:, 256:256 + W], blockones, rkv[:, 0:W])
                O.tt(bv[s][:, 0:W], pm[:, 256:256 + W], mixv[:, 0:W], ALU.mult); yield

            def inv_unit(u):
                p, ti = u // nt, u % nt
                s, par = p % 2, u % 2
                off, Pt = tiles[ti]
                sl = slice(off, off + Pt)
                w = Pt
                for j, srcb in enumerate((vbf[s], khg[s], bhg[s])):
                    O.tr(ptr[:Pt, j * 128:(j + 1) * 128], srcb[:, sl], ident_b(128))
                O.copy(tok3[par][:Pt, :, :], ptr[:Pt, 0:384].rearrange("p (j f) -> p j f", f=128), eng="act")
                yield
                for hd in range(2):
                    hs = slice(hd * 64, (hd + 1) * 64)
                    O.mm(pA[:Pt, 0:Pt], kh[s][hd][:, sl], kt[s][:, sl])
                    O.mm(pA[:Pt, 128:128 + Pt], kh[s][hd][:, sl], rt[s][:, sl])
                    O.mm(pA[:Pt, 256:256 + Pt], bh[s][hd][:, sl], kt[s][:, sl])
                    O.mm(pA[:Pt, 384:384 + Pt], bh[s][hd][:, sl], rt[s][:, sl])
                    O.mm(pI[hd][:Pt, 0:Pt], kt[s][:, sl], bh[s][hd][:, sl])
                    yield
                    X0 = XMS[par][hd][0]
                    O.tt(Asb[par][hd][:Pt, 0, :Pt], pA[:Pt, 0:Pt], m_strict(Pt), ALU.mult)
                    O.tt(Asb[par][hd][:Pt, 1, :Pt], pA[:Pt, 128:128 + Pt], m_incl(Pt), ALU.mult)
                    O.tt(Asb[par][hd][:Pt, 2, :Pt], pA[:Pt, 384:384 + Pt], m_incl(Pt), ALU.mult)
                    O.stt(X0[:Pt, w:2 * w], pA[:Pt, 256:256 + Pt], -1.0, m_strict(Pt), ALU.mult, ALU.mult, wk=(X0, "x"))
                    O.stt(X0[:Pt, 0:w], pI[hd][:Pt, 0:Pt], -1.0, m_low(Pt), ALU.mult, ALU.mult, wk=(X0, "m"))
                    yield
                nlev = int(math.log2(Pt))
                for k in range(nlev):
                    for hd in range(2):
                        cur = XMS[par][hd][k % 2]; nxt = XMS[par][hd][(k + 1) % 2]
                        Mk, Xk, Sk = cur[:Pt, 0:w], cur[:Pt, w:2 * w], cur[:Pt, 2 * w:3 * w]
                        if k == 0:
                            O.mm(pI[hd][:Pt, w:2 * w], Mk, Xk, rk=[(cur, "m"), (cur, "x")])
                            O.mm(pI[hd][:Pt, 0:w], Xk, Mk, rk=[(cur, "m"), (cur, "x")])
                            O.tt(nxt[:Pt, 2 * w:3 * w], Xk, cmat[:Pt, 0, :Pt], ALU.add, rk=[(cur, "x"), cmat], wk=(nxt, "s"))
                            O.copy(nxt[:Pt, 0:2 * w], pI[hd][:Pt, 0:2 * w], eng="act", wk=(nxt, "mx"))
                        elif k < nlev - 1:
                            O.mm(pI[hd][:Pt, w:3 * w], Mk, cur[:Pt, w:3 * w], rk=[(cur, "mx"), (cur, "s")])
                            O.mm(pI[hd][:Pt, 0:w], Xk, Mk, rk=[(cur, "mx")])
                            O.copy(nxt[:Pt, 0:2 * w], pI[hd][:Pt, 0:2 * w], eng="act", wk=(nxt, "mx"))
                            O.tt(nxt[:Pt, 2 * w:3 * w], pI[hd][:Pt, 2 * w:3 * w], Sk, ALU.add, rk=[pI[hd], (cur, "s")],
                                 wk=(nxt, "s"))
                        else:
                            O.mm(pI[hd][:Pt, 2 * w:3 * w], Mk, Sk, rk=[(cur, "mx"), (cur, "s")])
                            O.tt(TT[par][hd][:Pt, :Pt], pI[hd][:Pt, 2 * w:3 * w], Sk, ALU.add, rk=[pI[hd], (cur, "s")])
                        yield

            def state_unit(u):
                p, ti = u // nt, u % nt
                s, par = p % 2, u % 2
                off, Pt = tiles[ti]
                sl = slice(off, off + Pt)
                if sample:
                    O.dma(HsF[:], hs0_d[seq0 + ti, p], "hsin")
                    O.copy(HsB[:], HsF[:])
                    Hf, Hb = HsF, HsB
                    Hfv = lambda a, b: HsF[a, b]
                    Hbv = HsB[:, :]
                else:
                    Hf, Hb = HF, HB
                    Hfv = lambda a, b: HF[a, p, b]
                    Hbv = HB[:, p, :]
                hkey = (Hf, None if sample else p)
                hbkey = (Hb, None if sample else p)
                T3 = tok3[par]
                Vt = lambda hd: T3[:Pt, 0, hd * 64:(hd + 1) * 64]
                A_ = Asb[par]
                O.mm(pS[:Pt, 0:128], kt[s][:, sl], Hbv, start=True, stop=False, rk=[kt[s], hbkey])
                for hd in range(2):
                    O.mm(pS[:Pt, hd * 64:(hd + 1) * 64], A_[hd][:Pt, 0, :Pt], Vt(hd), start=False, stop=(hd == 1))
                O.copy(Wsb[:Pt, :], pS[:Pt, 0:128], eng="act")
                yield
                for hd in range(2):
                    O.mm(pS[:Pt, 128 + hd * 64:128 + (hd + 1) * 64], TT[par][hd][:Pt, :Pt], Wsb[:Pt, hd * 64:(hd + 1) * 64])
                O.act(nU[:Pt, :], pS[:Pt, 128:256], AF.Copy, scale=-1.0)
                yield
                O.mm(pS[:Pt, 256:384], rt[s][:, sl], Hbv, start=True, stop=False, rk=[rt[s], hbkey])
                for hd in range(2):
                    O.mm(pS[:Pt, 256 + hd * 64:256 + (hd + 1) * 64], A_[hd][:Pt, 1, :Pt], Vt(hd), start=False, stop=False)
                for hd in range(2):
                    O.mm(pS[:Pt, 256 + hd * 64:256 + (hd + 1) * 64], A_[hd][:Pt, 2, :Pt], nU[:Pt, hd * 64:(hd + 1) * 64],
                         start=False, stop=(hd == 1))
                O.mm(pS[:, 384:512], T3[:Pt, 1, :], T3[:Pt, 0, :], start=True, stop=False)
                O.mm(pS[:, 384:512], T3[:Pt, 2, :], nU[:Pt, :], start=False, stop=True)
                yield
                for hd in range(2):
                    hs = slice(hd * 64, (hd + 1) * 64)
                    O.stt(Hfv(hs, hs), Hfv(hs, hs), gC[s][hs, ti:ti + 1], pS[hs, 384 + hd * 64:384 + (hd + 1) * 64],
                          ALU.mult, ALU.add, rk=[hkey, pS, gC[s]], wk=hkey)
                if sample:
                    O.dma(hs_d[seq0 + ti, p], HsF[:], "hsout")
                else:
                    O.copy(Hbv, HF[:, p, :], eng="act", rk=[hkey], wk=hbkey)
                yield
                for hd in range(2):
                    O.bn_stats(st6[:Pt, hd, :], pS[:Pt, 256 + hd * 64:256 + (hd + 1) * 64])
                    O.bn_aggr(mv[:Pt, hd, :], st6[:Pt, hd, :])
                rsqrt(rstd2[:Pt, :], mv[:Pt, :, 1], 2, 1.0, GN_EPS)
                for hd in range(2):
                    O.ts(ynb[:Pt, hd * 64:(hd + 1) * 64], pS[:Pt, 256 + hd * 64:256 + (hd + 1) * 64], mv[:Pt, hd, 0:1],
                         rstd2[:Pt, hd:hd + 1], op0=ALU.subtract, op1=ALU.mult)
                yield
                O.tr(ptr[:, 512:512 + Pt], ynb[:Pt, :], ident_b(Pt))
                O.act(ynT[s][:, sl], ptr[:, 512:512 + Pt], AF.Identity, scale=pv(PV_GNG, p), bias=pv(PV_GNB, p))
                yield
                if ti == nt - 1:
                    O.tt(ynT[s][:, 0:W], ynT[s][:, 0:W], bv[s][:, 0:W], ALU.add)
                    O.tt(outT[:, 8 + p, 0:W], ynT[s][:, 0:W], sgb[s][:, 0:W], ALU.mult, wk=(outT, ("b", p)))
                    yield

            def a_branch():
                for gi, g in enumerate((2, 3, 0, 1, 4, 5)):
                    w = load_wA(WbA, g)
                    col = (g % 2) * 512
                    for ti, (off, Pt) in enumerate(tiles):
                        half = (gi * nt + ti) % 2
                        acc = pzs[half][:Pt, :]
                        for c in range(16):
                            O.mm(acc, hT[:, c, off:off + Pt], w[:, c, :], start=(c == 0), stop=(c == 15), rk=[(hT, ti), w])
                        if g in (2, 3):
                            O.act(gv[:Pt, ti, col:col + 512], acc, AF.Gelu, wk=(gv, ti))
                        elif g in (0, 1):
                            O.act(gus[:Pt, ti, col:col + 512], acc, AF.Gelu, wk=(gus, ti))
                        else:
                            O.act(gat[:Pt, :], acc, AF.Tanh, scale=0.5)
                            O.stt(gat2[:Pt, :], gat[:Pt, :], 1.0, acc, ALU.add, ALU.mult)
                            O.stt(gus[:Pt, ti, col:col + 512], gat2[:Pt, :], 0.5, gus[:Pt, ti, col:col + 512], ALU.mult,
                                  ALU.mult, rk=[(gus, ti), gat2], wk=(gus, ti))
                        yield
                    if g == 3:
                        for ti, (off, Pt) in enumerate(tiles):
                            for j in range(2):
                                O.bn_stats(st6a[:Pt, j, :], gv[:Pt, ti, j * 512:(j + 1) * 512], rk=[(gv, ti)])
                            O.bn_aggr(mva[:Pt, :], st6a[:Pt, :, :].rearrange("p a b -> p (a b)"))
                            rsqrt(sma[:Pt, 1:2], mva[:Pt, 1:2], 1, 1.0, LN_EPS)
                            O.ts(gv[:Pt, ti, :], gv[:Pt, ti, :], mva[:Pt, 0:1], sma[:Pt, 1:2], op0=ALU.subtract, op1=ALU.mult,
                                 rk=[(gv, ti)], wk=(gv, ti))
                            O.tt(gv[:Pt, ti, :], gv[:Pt, ti, :], lnbc[:Pt, 0, :], ALU.mult, rk=[(gv, ti), lnbc], wk=(gv, ti))
                            yield
                            if sample:
                                O.tt(gv[:Pt, ti, :], gv[:Pt, ti, :], lnbc[:Pt, 1, :], ALU.add, rk=[(gv, ti), lnbc], wk=(gv, ti))
                                O.dma(vn_d[seq0 * 64 + off:seq0 * 64 + off + Pt, :], gv[:Pt, ti, :], f"vn{ti}",
                                      rk=[(gv, ti)])
                                O.copy(vnb[:Pt, ti, :], gv[:Pt, ti, :], eng="act", rk=[(gv, ti)], wk=(vnb, ti))
                            else:
                                O.tt(vnb[:Pt, ti, :], gv[:Pt, ti, :], lnbc[:Pt, 1, :], ALU.add, rk=[(gv, ti), lnbc],
                                     wk=(vnb, ti))
                            yield
                for ti, (off, Pt) in enumerate(tiles):
                    for h in range(8):
                        O.mm(pzs[h // 4][:Pt, (h % 4) * 128:(h % 4 + 1) * 128], wsTb[:Pt, h, :Pt],
                             vnb[:Pt, ti, h * 128:(h + 1) * 128], rk=[wsTb, (vnb, ti)])
                    for h in range(8):
                        O.stt(oa[:Pt, h * 128:(h + 1) * 128], pzs[h // 4][:Pt, (h % 4) * 128:(h % 4 + 1) * 128],
                              pv(PV_SGUB, h)[:Pt, :], gus[:Pt, ti, h * 128:(h + 1) * 128], ALU.add, ALU.mult,
                              rk=[pzs[h // 4], (gus, ti), pvec])
                    yield
                    for c4 in range(2):
                        for j in range(4):
                            c = c4 * 4 + j
                            O.tr(ptr[:, j * Pt:(j + 1) * Pt], oa[:Pt, c * 128:(c + 1) * 128], ident_b(Pt))
                        src = ptr[:, 0:4 * Pt].rearrange("p (j t) -> p j t", t=Pt)
                        O.copy(outT[:, c4 * 4:c4 * 4 + 4, off:off + Pt], src, eng="act" if c4 % 2 else "dve",
                               wk=(outT, ("a", ti, c4)))
                        yield

            def out_proj():
                for g in range(4):
                    w = load_wA(WbO, g)
                    for ti, (off, Pt) in enumerate(tiles):
                        half = (g * nt + ti) % 2
                        acc = pzs[half][:Pt, :]
                        for c in range(16):
                            O.mm(acc, outT[:, c, off:off + Pt], w[:, c, :], start=(c == 0), stop=(c == 15), rk=[outT, w])
                        O.tt(xt[:Pt, xi(ti), g * 512:(g + 1) * 512], acc, xt[:Pt, xi(ti), g * 512:(g + 1) * 512], ALU.add,
                             rk=[pzs[half], (xt, xi(ti))], wk=(xt, xi(ti)))
                        yield
                for ti, (off, Pt) in enumerate(tiles):
                    O.act(junk[:Pt, :], xt[:Pt, xi(ti), :], AF.Square, accum=sm[:Pt, 5:6], rk=[(xt, xi(ti))])
                    rsqrt(sm[:Pt, 7:8], sm[:Pt, 5:6], 1, 1.0 / D, NORM_EPS)
                    O.stt(xt[:Pt, xi(ti), :], xt[:Pt, xi(ti), :], sm[:Pt, 7:8], fing[:Pt, :], ALU.mult, ALU.mult,
                          rk=[(xt, xi(ti)), fing], wk=(xt, xi(ti)))
                    O.dma(y_dst[off:off + Pt, :], xt[:Pt, xi(ti), :], f"y{xi(ti)}", rk=[(xt, xi(ti))])
                    yield

            run_seq(stage0())
            run_seq(lora())
            run_seq(prep(0))
            NU = 8 * nt
            ab = a_branch()
            n_inv = 3 + 2 * int(math.log2(tiles[0][1]))
            if nt == 2:
                run_merged([(inv_unit(0), n_inv), (ab_slice(ab, 4), 4)])
                for u in range(NU):
                    p = u // nt
                    gens = [(state_unit(u), 7)]
                    if u + 1 < NU:
                        gens.append((inv_unit(u + 1), n_inv))
                    if u % nt == 0 and p + 1 < 8:
                        gens.append((prep(p + 1), 13))
                    else:
                        gens.append((ab_slice(ab, 5), 5))
                    run_merged(gens)
            else:
                for u in range(NU):
                    if u > 0 and u % nt == 0:
                        run_seq(prep(u // nt))
                    run_merged([(inv_unit(u), n_inv), (ab_slice(ab, 4), 4)])
                    run_seq(state_unit(u))
            run_seq(ab)
            run_seq(out_proj())

        def ab_slice(g, n):
            for _ in range(n):
                try:
                    next(g)
                except StopIteration:
                    return
                yield

        for sbi in range(NP // SBW):
            superblock(xp_d[sbi * SBW:(sbi + 1) * SBW, :], yp_d[sbi * SBW:(sbi + 1) * SBW, :],
                       [(i * 128, 128) for i in range(SBW // 128)], SBW, False, 0, sbi % 2)
        O.dma(shp_d, lastcol[:], "misc_out")
        O.dma(hp_d.rearrange("q p f -> p q f"), HF[:], "misc_out")
        for s0 in range(0, NS, NT):
            n = min(NT, NS - s0)
            superblock(xs_d[s0 * 64:(s0 + n) * 64, :], ys_d[s0 * 64:(s0 + n) * 64, :], [(i * 64, 64) for i in range(n)], n * 64, True, s0,
                       (NP // SBW + s0 // NT) % 2)
        if NS > 0:
            O.dma(shs_d, shs[:], "misc_out")
        out_tags = ["misc_out", "hsout"] + [f"y{i}" for i in range(2 * NT)] + [f"vn{i}" for i in range(NT)]
        P.emit_all(out_tags, eng="sp")
        n_ops = len(P.ops)
    return nc, n_ops
N_CORES = 8
_NC_CACHE = {}


def _consts():
    i = np.arange(128)
    ident = np.eye(128, dtype=np.float32)
    incl = (i[:, None] <= i[None, :]).astype(np.float32)
    strict = (i[:, None] < i[None, :]).astype(np.float32)
    low = (i[:, None] > i[None, :]).astype(np.float32)
    blk = ((i[:, None] // 64) == (i[None, :] // 64)).astype(np.float32)
    return np.ascontiguousarray(np.stack([ident, incl, strict, low, blk], axis=1))


def _cols(v, n):
    return np.ascontiguousarray(np.asarray(v, np.float32).reshape(n, 128).T)


def _shared_inputs(norm_g, w_in, w_out, sgu_ln_g, sgu_ln_b, sgu_w, sgu_b, shift_mu, w0, w2, a0, a2, k_k, k_a, r_k,
                   gn_g, gn_b, final_g):
    pvec = np.concatenate([
        _cols(shift_mu[0], 25), _cols(w0[0], 8), _cols(a0[0], 8), _cols(k_k[0], 8), _cols(k_a[0], 8),
        _cols(np.asarray(r_k[0]).reshape(-1), 8), _cols(gn_g[0], 8), _cols(gn_b[0], 8),
        np.ascontiguousarray(np.asarray(sgu_b[0], np.float32).T),
        np.stack([(np.arange(128) < 64), (np.arange(128) >= 64)], axis=1).astype(np.float32)], axis=1)
    return {
        "w_in": np.ascontiguousarray(np.asarray(w_in[0], np.float32)),
        "w_out": np.ascontiguousarray(np.asarray(w_out[0], np.float32)),
        "i_normg": _cols(norm_g[0], 16),
        "i_cmat": _consts(),
        "i_lnbc": np.ascontiguousarray(np.broadcast_to(
            np.stack([np.asarray(sgu_ln_g[0], np.float32), np.asarray(sgu_ln_b[0], np.float32)])[None], (128, 2, DA))),
        "i_fing": np.ascontiguousarray(np.broadcast_to(np.asarray(final_g, np.float32)[None], (128, D))),
        "i_pvec": np.ascontiguousarray(pvec.astype(np.float32)),
        "i_wsT": np.ascontiguousarray(np.transpose(np.asarray(sgu_w[0], np.float32), (2, 0, 1))),
        "i_w2a2": np.ascontiguousarray(np.concatenate([np.asarray(w2[0], np.float32), np.asarray(a2[0], np.float32)], 0)),
    }


def _h_blockdiag(S):
    n = S.shape[0]
    out = np.zeros((n, 8, 128, 128), np.float32)
    Ht = np.transpose(S, (0, 1, 3, 2))
    out[:, :, 0:64, 0:64] = Ht[:, 0::2]
    out[:, :, 64:128, 64:128] = Ht[:, 1::2]
    return out


def _h_unblock(Hbd):
    n = Hbd.shape[0]
    S = np.zeros((n, 16, 64, 64), np.float32)
    S[:, 0::2] = np.transpose(Hbd[:, :, 0:64, 0:64], (0, 1, 3, 2))
    S[:, 1::2] = np.transpose(Hbd[:, :, 64:128, 64:128], (0, 1, 3, 2))
    return S


def run_layout(x_prompt, x_sample, state_b_wkv, state_b_shift, shared, n_cores, NP, NS):
    key = (NP, NS)
    if key not in _NC_CACHE:
        _NC_CACHE[key] = build(NP, NS)[0]
    nc = _NC_CACHE[key]
    B = x_prompt.shape[0]
    in_maps = []
    for c in range(n_cores):
        m = dict(shared)
        m["xp"] = np.ascontiguousarray(x_prompt[c], np.float32) if c < B else np.zeros((NP, D), np.float32)
        sl = slice(c * NS, (c + 1) * NS)
        m["xs"] = np.ascontiguousarray(np.asarray(x_sample[sl], np.float32).reshape(NS * 64, D))
        m["hs0"] = _h_blockdiag(np.asarray(state_b_wkv[0, sl], np.float32))
        sh = np.asarray(state_b_shift[0, sl, 0, :], np.float32).reshape(NS, 25, 128)
        m["shiftT"] = np.ascontiguousarray(np.transpose(sh, (2, 0, 1)))
        in_maps.append(m)
    res = run_bass_kernel_spmd(nc, in_maps, core_ids=list(range(n_cores)))
    R = res.results
    y_prompt = np.stack([R[c]["yp"] for c in range(B)]).astype(np.float32)
    y_sample = np.concatenate([R[c]["ys"].reshape(NS, 64, D) for c in range(n_cores)]).astype(np.float32)
    wkv_p = np.concatenate([_h_unblock(R[c]["hp_out"][None]) for c in range(B)])[None]
    shp = np.stack([R[c]["shp_out"].T.reshape(1, NSH) for c in range(B)])[None]
    wkv_s = np.concatenate([_h_unblock(R[c]["hs_out"]) for c in range(n_cores)])[None]
    shs = np.concatenate([np.transpose(R[c]["shs_out"], (1, 2, 0)).reshape(NS, 1, NSH) for c in range(n_cores)])[None]
    vn = np.concatenate([R[c]["vn_out"].reshape(NS, 64, DA) for c in range(n_cores)])[None]
    return (y_prompt, y_sample, wkv_p.astype(np.float32), shp.astype(np.float32), wkv_s.astype(np.float32),
            shs.astype(np.float32), vn.astype(np.float32))


def kernel(x_prompt, x_sample, state_b_wkv, state_b_shift, norm_g, w_in, w_out, sgu_ln_g, sgu_ln_b,
           sgu_w, sgu_b, shift_mu, w0, w2, a0, a2, k_k, k_a, r_k, gn_g, gn_b, final_g):
    shared = _shared_inputs(norm_g, w_in, w_out, sgu_ln_g, sgu_ln_b, sgu_w, sgu_b, shift_mu, w0, w2, a0, a2,
                            k_k, k_a, r_k, gn_g, gn_b, final_g)
    x_prompt = np.asarray(x_prompt); x_sample = np.asarray(x_sample)
    return run_layout(x_prompt, x_sample, np.asarray(state_b_wkv), np.asarray(state_b_shift), shared,
                      N_CORES, x_prompt.shape[1], x_sample.shape[0] // N_CORES)
```

```python
from concourse.bass_utils import run_bass_kernel_spmd
import numpy as np
import concourse.bass as bass
import concourse.mybir as mybir

F32 = mybir.dt.float32
BF16 = mybir.dt.bfloat16
AF = mybir.ActivationFunctionType
ALU = mybir.AluOpType

SEM_CAP = 16384


def _key(x):
    sub = None
    if isinstance(x, tuple):
        x, sub = x
    t = getattr(x, "tensor", x)
    return (t.name, sub)


class Prog:
    COMPUTE = ("pe", "act", "dve", "pool")

    def __init__(self, nc, stack):
        self.nc = nc
        self.stack = stack
        self.ops = []
        self.state = {}
        self.eng = {"pe": nc.tensor, "act": nc.scalar, "dve": nc.vector, "pool": nc.gpsimd, "sp": nc.sync}
        self.psum_names = set()
        self.bank_last = {}

    def sb(self, name, shape, dt):
        return self.stack.enter_context(self.nc.sbuf_tensor(name, list(shape), dt))

    def ps(self, name, shape, dt):
        self.psum_names.add(name)
        return self.stack.enter_context(self.nc.psum_tensor(name, list(shape), dt))

    def _states(self, key, create=True):
        name, sub = key
        d = self.state.setdefault(name, {})
        if sub is None:
            if None not in d:
                d[None] = [None, []]
            return list(d.values())
        out = []
        if sub not in d:
            d[sub] = [None, []]
        out.append(d[sub])
        if None in d:
            out.append(d[None])
        return out

    def op(self, eng, fn, reads=(), writes=(), dma_tag=None, wait_all=False, cost=100.0, lat=0.0):
        idx = len(self.ops)
        deps = {}
        rk = [_key(r) for r in reads if r is not None and not isinstance(r, (int, float))]
        wk = [_key(w) for w in writes if w is not None]
        for k in rk:
            for st in self._states(k):
                if st[0] is not None:
                    deps.setdefault(st[0], "raw")
        for k in wk:
            for st in self._states(k):
                if st[0] is not None:
                    deps.setdefault(st[0], "waw")
                for r in st[1]:
                    if r != idx:
                        deps.setdefault(r, "war")
        for name in {k[0] for k in rk + wk if k[0] in self.psum_names}:
            bl = self.bank_last.setdefault(name, {})
            for e2, i2 in bl.items():
                if e2 != eng:
                    deps.setdefault(i2, "bank")
                else:
                    deps.setdefault(i2, "order")
            bl[eng] = idx
        for k in rk:
            name, sub = k
            if sub is None:
                for st in self._states(k):
                    st[1].append(idx)
            else:
                self.state[name][sub][1].append(idx)
        for k in wk:
            name, sub = k
            if sub is None:
                d = self.state[name]
                for s in list(d.keys()):
                    d[s] = [idx, []]
            else:
                self.state[name][sub] = [idx, []]
        o = dict(eng=eng, fn=fn, deps=deps, signal=False, dma_tag=dma_tag, wait_all=wait_all, val=None,
                 cost=float(cost), lat=float(lat))
        need = []
        for d, kind in deps.items():
            p = self.ops[d]
            if p["dma_tag"] is None and dma_tag is None and p["eng"] == eng:
                if eng == "pe":
                    continue
                if kind == "order":
                    continue
            need.append(d)
            p["signal"] = True
        o["need"] = need
        if dma_tag is not None:
            o["signal"] = True
        self.ops.append(o)
        return idx


    def schedule(self, window=64, slack=120.0, use_cp=True):
        import bisect
        ops = self.ops
        n = len(ops)
        succ = [[] for _ in range(n)]
        ndeps = [0] * n
        for i, o in enumerate(ops):
            ndeps[i] = len(o["deps"])
            for d in o["deps"]:
                succ[d].append(i)
        ready = [0.0] * n
        blv = [0.0] * n
        for i in range(n - 1, -1, -1):
            m = 0.0
            for sidx in succ[i]:
                if blv[sidx] > m:
                    m = blv[sidx]
            blv[i] = m + ops[i]["cost"] + ops[i]["lat"] + 100.0
        prio = [(-blv[i], i) for i in range(n)] if use_cp else [(i, i) for i in range(n)]
        engs = sorted({o["eng"] for o in ops})
        free = {e: 0.0 for e in engs}
        rel = {e: [] for e in engs}
        for i, o in enumerate(ops):
            if ndeps[i] == 0:
                bisect.insort(rel[o["eng"]], (prio[i], i))
        order = []
        done = 0
        cur_tbl = [None]
        TBL = 1300.0

        def eff(e, t, i):
            st_ = max(t, ready[i])
            if e == "act":
                tb = ops[i].get("tbl")
                if tb is not None and tb != cur_tbl[0]:
                    st_ += TBL
            return st_
        while done < n:
            best = None
            for e in engs:
                cand = rel[e]
                if not cand:
                    continue
                t = free[e]
                lim = [c_[1] for c_ in cand[:window]]
                tmin = min(eff(e, t, i) for i in lim)
                for i in lim:
                    if eff(e, t, i) <= tmin + slack:
                        pick = i
                        break
                stt = eff(e, t, pick)
                if best is None or stt < best[0]:
                    best = (stt, e, pick)
            stt, e, i = best
            rel[e].remove((prio[i], i))
            o = ops[i]
            if e == "act" and o.get("tbl") is not None:
                cur_tbl[0] = o["tbl"]
            fin = stt + o["cost"] + o["lat"]
            free[e] = stt + o["cost"]
            order.append(i)
            done += 1
            for sidx in succ[i]:
                so = ops[sidx]
                if o["dma_tag"] is not None or so["eng"] != e:
                    l = 180.0
                elif e == "pe":
                    l = 0.0
                else:
                    l = 60.0 if o["deps"] and ops[sidx]["deps"].get(i) in ("raw", "waw") else 0.0
                r = fin + l
                if r > ready[sidx]:
                    ready[sidx] = r
                ndeps[sidx] -= 1
                if ndeps[sidx] == 0:
                    bisect.insort(rel[so["eng"]], (prio[sidx], sidx))
        self.est_ns = max(free.values())
        remap = {old: new for new, old in enumerate(order)}
        new_ops = []
        for old in order:
            o = ops[old]
            o["deps"] = {remap[d]: k for d, k in o["deps"].items()}
            o["need"] = [remap[d] for d in o["need"]]
            new_ops.append(o)
        self.ops = new_ops

    def emit(self):
        nc = self.nc
        counters = {}
        for o in self.ops:
            if not o["signal"]:
                continue
            key = ("dma", o["dma_tag"]) if o["dma_tag"] is not None else ("eng", o["eng"])
            inc = 16 if o["dma_tag"] is not None else 1
            counters[key] = counters.get(key, 0) + inc
            o["key"] = key
            o["val"] = counters[key]
        totals = dict(counters)
        hw = {}

        def hwsem(key, k):
            if (key, k) not in hw:
                hw[(key, k)] = self.stack.enter_context(nc.semaphore(f"s_{key[0]}_{key[1]}_{k}"))
            return hw[(key, k)]

        waited = {}
        n_wait = 0
        for o in self.ops:
            e = self.eng[o["eng"]]
            tgt = {}
            for d in o["need"]:
                p = self.ops[d]
                v = totals[p["key"]] if p["wait_all"] else p["val"]
                tgt[p["key"]] = max(tgt.get(p["key"], 0), v)
            for key, v in tgt.items():
                if waited.get((o["eng"], key), 0) >= v:
                    continue
                waited[(o["eng"], key)] = v
                k = (v - 1) // SEM_CAP
                e.wait_ge(hwsem(key, k), v - k * SEM_CAP)
                n_wait += 1
            ins = o["fn"](e)
            if o["signal"]:
                v = o["val"]
                k = (v - 1) // SEM_CAP
                inc = 16 if o["dma_tag"] is not None else 1
                ins.then_inc(hwsem(o["key"], k), inc)
        self.n_wait = n_wait
        return totals, hwsem

    def finish(self, out_tags, eng="sp"):
        totals, hwsem = self._fin
        e = self.eng[eng]
        for t in out_tags:
            key = ("dma", t)
            if key in totals:
                v = totals[key]
                k = (v - 1) // SEM_CAP
                e.wait_ge(hwsem(key, k), v - k * SEM_CAP)

    def emit_all(self, out_tags, eng="sp", sched=True):
        if sched:
            self.schedule()
        self._fin = self.emit()
        self.finish(out_tags, eng)


def _nfree(ap):
    n = 1
    for d in list(ap.shape)[1:]:
        n *= int(d)
    return n


class Ops:
    def __init__(self, prog):
        self.p = prog

    def _ew(self, out, *ins):
        n = _nfree(out)
        ps = any((getattr(getattr(x, "tensor", None), "name", None) in self.p.psum_names) for x in ins if x is not None
                 and not isinstance(x, (int, float)))
        return (160.0 + 1.05 * n) if ps else (100.0 + 0.8 * n)

    @staticmethod
    def _aps(*xs):
        return [x for x in xs if x is not None and not isinstance(x, (int, float))]

    def mm(self, out, lhsT, rhs, start=True, stop=True, wk=None, rk=()):
        self.p.op("pe", lambda e: e.matmul(out, lhsT, rhs, start=start, stop=stop),
                  reads=[lhsT, rhs] if not rk else list(rk), writes=[wk if wk is not None else out],
                  cost=8.0 + 0.43 * max(_nfree(rhs), 64), lat=120.0)

    def tr(self, out, in_, ident, wk=None, rk=()):
        self.p.op("pe", lambda e: e.transpose(out, in_, ident),
                  reads=[in_, ident] if not rk else list(rk) + [ident], writes=[wk if wk is not None else out],
                  cost=8.0 + 0.43 * max(_nfree(in_), 64), lat=120.0)

    def act(self, out, in_, func, bias=None, scale=None, accum=None, eng="act", wk=None, rk=None):
        kw = {}
        if bias is not None:
            kw["bias"] = bias
        if scale is not None:
            kw["scale"] = scale
        if accum is not None:
            kw["accum_out"] = accum
        reads = self._aps(in_, bias, scale) if rk is None else list(rk) + self._aps(bias, scale)
        writes = [wk if wk is not None else out] + ([accum] if accum is not None else [])
        tbl = {AF.Exp: "ln_exp", AF.Ln: "ln_exp", AF.Gelu: "gelu", AF.Tanh: "gelu", AF.Silu: "silu", AF.Sigmoid: "sig",
               AF.Sqrt: "sqrt"}.get(func)
        i = self.p.op("act", lambda e: e.activation(out, in_, func, **kw), reads=reads, writes=writes,
                      cost=self._ew(out, in_) + (90.0 if accum is not None else 0.0))
        self.p.ops[i]["tbl"] = tbl

    def tt(self, out, a, b, op, eng="dve", wk=None, rk=None):
        self.p.op(eng, lambda e: e.tensor_tensor(out, a, b, op),
                  reads=[a, b] if rk is None else list(rk), writes=[wk if wk is not None else out],
                  cost=self._ew(out, a, b) * (2.0 if eng == "pool" else 1.0))

    def ts(self, out, a, s1, s2=None, op0=ALU.mult, op1=None, eng="dve", wk=None, rk=None, accum=None):
        kw = {}
        if op1 is not None:
            kw["op1"] = op1
        if accum is not None:
            kw["accum_out"] = accum
        reads = self._aps(a, s1, s2) if rk is None else list(rk) + self._aps(s1, s2)
        writes = [wk if wk is not None else out] + ([accum] if accum is not None else [])
        self.p.op(eng, lambda e: e.tensor_scalar(out, a, s1, s2, op0, **kw), reads=reads, writes=writes,
                  cost=self._ew(out, a) * (2.0 if eng == "pool" else 1.0))

    def stt(self, out, in0, scalar, in1, op0, op1, eng="dve", wk=None, rk=None):
        reads = self._aps(in0, scalar, in1) if rk is None else list(rk) + self._aps(scalar)
        self.p.op(eng, lambda e: e.scalar_tensor_tensor(out, in0, scalar, in1, op0, op1),
                  reads=reads, writes=[wk if wk is not None else out],
                  cost=self._ew(out, in0, in1) * (2.0 if eng == "pool" else 1.0))

    def copy(self, out, in_, eng="dve", wk=None, rk=None):
        if eng == "act":
            self.p.op("act", lambda e: e.copy(out, in_), reads=[in_] if rk is None else list(rk),
                      writes=[wk if wk is not None else out], cost=self._ew(out, in_))
        else:
            self.p.op(eng, lambda e: e.tensor_copy(out, in_), reads=[in_] if rk is None else list(rk),
                      writes=[wk if wk is not None else out], cost=self._ew(out, in_) * (2.0 if eng == "pool" else 1.0))

    def memset(self, out, val, eng="dve", wk=None):
        self.p.op(eng, lambda e: e.memset(out, val), reads=[], writes=[wk if wk is not None else out],
                  cost=60.0 + 0.5 * _nfree(out))

    def recip(self, out, in_, wk=None):
        self.p.op("dve", lambda e: e.reciprocal(out, in_), reads=[in_], writes=[wk if wk is not None else out],
                  cost=self._ew(out, in_) * 1.5)

    def scan(self, out, d0, d1, initial, op0, op1, wk=None, rk=None):
        reads = self._aps(d0, d1, initial) if rk is None else list(rk)
        self.p.op("dve", lambda e: e.tensor_tensor_scan(out, d0, d1, initial, op0, op1),
                  reads=reads, writes=[wk if wk is not None else out], cost=100.0 + 2.1 * _nfree(out))

    def bn_stats(self, out, in_, wk=None, rk=None):
        self.p.op("dve", lambda e: e.bn_stats(out, in_), reads=[in_] if rk is None else list(rk),
                  writes=[wk if wk is not None else out], cost=self._ew(in_, in_))

    def bn_aggr(self, out, in_, wk=None):
        self.p.op("dve", lambda e: e.bn_aggr(out, in_), reads=[in_], writes=[wk if wk is not None else out])

    def dma(self, out, in_, tag, q="sp", wk=None, rk=None, wait_all=False):
        nbytes = 128 * _nfree(out) * (2 if out.dtype == BF16 else 4)
        self.p.op(q, lambda e: e.dma_start(out=out, in_=in_), reads=[in_] if rk is None else list(rk),
                  writes=[wk if wk is not None else out], dma_tag=tag, wait_all=wait_all,
                  cost=70.0, lat=2000.0 + nbytes / 150.0)
import contextlib
import math

D = 2048
DA = 1024
NSH = 3200
DP = 7296
NORM_EPS = 1e-6
LN_EPS = 1e-5
GN_EPS = 64e-5
C0 = math.exp(-0.5)
NQ = 25
PV_MU, PV_W0, PV_A0, PV_KK, PV_KA, PV_RK, PV_GNG, PV_GNB, PV_SGUB = 0, 25, 33, 41, 49, 57, 65, 73, 81
PV_HM = 89
PV_N = 91


def build(NP, NS, SBW=256):
    nc = bass.Bass("TRN2", target_bir_lowering=False)
    st = contextlib.ExitStack()
    with st:
        P = Prog(nc, st)
        O = Ops(P)
        NST = NS * 64
        assert NP % SBW == 0
        din = lambda n, s, dt=F32: nc.dram_tensor(n, list(s), dt, kind="ExternalInput").ap()
        dout = lambda n, s, dt=F32: nc.dram_tensor(n, list(s), dt, kind="ExternalOutput").ap()
        xp_d = din("xp", [NP, D]); xs_d = din("xs", [NST, D])
        hs0_d = din("hs0", [NS, 8, 128, 128]); shT_d = din("shiftT", [128, NS, NQ])
        win_d = din("w_in", [D, DP]); wout_d = din("w_out", [D, D])
        normg_d = din("i_normg", [128, 16]); cmat_d = din("i_cmat", [128, 5, 128])
        lnbc_d = din("i_lnbc", [128, 2, DA]); fing_d = din("i_fing", [128, D]); pvec_d = din("i_pvec", [128, PV_N])
        wsT_d = din("i_wsT", [128, 8, 128]); w2a2_d = din("i_w2a2", [128, DA])
        yp_d = dout("yp", [NP, D]); ys_d = dout("ys", [NST, D])
        hp_d = dout("hp_out", [8, 128, 128]); hs_d = dout("hs_out", [NS, 8, 128, 128])
        shp_d = dout("shp_out", [128, NQ]); shs_d = dout("shs_out", [128, NS, NQ])
        vn_d = dout("vn_out", [NST, DA])
        WbA = nc.dram_tensor("WbA", [6, 128, 16, 512], BF16, kind="Internal").ap()
        WbB = nc.dram_tensor("WbB", [33, 128, 16, 128], BF16, kind="Internal").ap()
        WbO = nc.dram_tensor("WbO", [4, 128, 16, 512], BF16, kind="Internal").ap()

        NT = SBW // 128
        normg = P.sb("normg", [128, 16], F32)
        cmat = P.sb("cmat", [128, 5, 128], F32)
        cmb = P.sb("cmb", [128, 5, 128], BF16)
        lnbc = P.sb("lnbc", [128, 2, DA], F32)
        fing = P.sb("fing", [128, D], F32)
        pvec = P.sb("pvec", [128, PV_N], F32)
        omka = P.sb("omka", [128, 8], F32)
        hbias = P.sb("hbias", [128, 16], F32)
        mhalf = P.sb("mhalf", [128, SBW], F32)
        tg = P.sb("tg", [128, SBW], F32)
        wsTb = P.sb("wsTb", [128, 8, 128], BF16)
        w2a2b = P.sb("w2a2b", [128, DA], BF16)
        HF = P.sb("HF", [128, 8, 128], F32)
        HB = P.sb("HB", [128, 8, 128], BF16)
        HsF = P.sb("HsF", [128, 128], F32)
        HsB = P.sb("HsB", [128, 128], BF16)
        lastcol = P.sb("lastcol", [128, NQ], F32)
        shT = P.sb("shT", [128, NS, NQ], F32)
        shs = P.sb("shs", [128, NS, NQ], F32)
        xt = P.sb("xt", [128, 2 * NT, D], F32)
        junk = P.sb("junk", [128, D], BF16)
        xn = P.sb("xn", [128, D], BF16)
        hT = P.sb("hT", [128, 16, SBW], BF16)
        outT = P.sb("outT", [128, 16, SBW], BF16)
        wB = [P.sb(f"wB{i}", [128, 16, 128], BF16) for i in range(3)]
        wA = [P.sb(f"wA{i}", [128, 16, 512], BF16) for i in range(2)]
        gv = P.sb("gv", [128, NT, DA], F32)
        gus = P.sb("gus", [128, NT, DA], BF16)
        gat = P.sb("gat", [128, 512], F32)
        gat2 = P.sb("gat2", [128, 512], BF16)
        vnb = P.sb("vnb", [128, NT, DA], BF16)
        oa = P.sb("oa", [128, DA], BF16)
        st6 = P.sb("st6", [128, 2, 6], F32)
        mv = P.sb("mv", [128, 2, 2], F32)
        sm = P.sb("sm", [128, 8], F32)
        f32t = lambda n: P.sb(n, [128, SBW], F32)
        zraw = P.sb("zraw", [128, SBW + 1], F32)
        diff = f32t("diff")
        mixl, mixr, mixk, mixv = f32t("mixl"), f32t("mixr"), f32t("mixk"), f32t("mixv")
        sgb = [f32t(f"sgb{i}") for i in range(2)]; bv = [f32t(f"bv{i}") for i in range(2)]; ynT = [f32t(f"ynT{i}") for i in range(2)]
        sd, aa, cum, excl = f32t("sd"), f32t("aa"), f32t("cum"), f32t("excl")
        e_incl, e_excl, e_inv, e_rem = f32t("e_incl"), f32t("e_excl"), f32t("e_inv"), f32t("e_rem")
        kk, sq, rn, tmp, kp, bbv, rkv = (f32t(n) for n in ("kk", "sq", "rn", "tmp", "kp", "bbv", "rkv"))
        ones = f32t("ones")
        nb = P.sb("nb", [128, NT], F32)
        gC = [P.sb(f"gC{i}", [128, NT], F32) for i in range(2)]
        th = P.sb("th", [128, SBW], BF16)
        b16 = lambda n: P.sb(n, [128, SBW], BF16)
        kt, rt, khg, bhg, vbf = ([b16(f"{n}{i}") for i in range(2)] for n in ("kt", "rt", "khg", "bhg", "vbf"))
        kh = [[b16(f"kh{i}_{h}") for h in range(2)] for i in range(2)]
        bh = [[b16(f"bh{i}_{h}") for h in range(2)] for i in range(2)]
        tok3 = [P.sb(f"tok3_{i}", [128, 3, 128], BF16) for i in range(2)]
        Asb = [[P.sb(f"Asb{i}_{h}", [128, 3, 128], BF16) for h in range(2)] for i in range(2)]
        XMS = [[[P.sb(f"XMS{i}_{h}_{j}", [128, 384], BF16) for j in range(2)] for h in range(2)] for i in range(2)]
        TT = [[P.sb(f"TT{i}_{h}", [128, 128], BF16) for h in range(2)] for i in range(2)]
        st6a = P.sb("st6a", [128, 2, 6], F32)
        mva = P.sb("mva", [128, 2], F32)
        sma = P.sb("sma", [128, 2], F32)
        Wsb = P.sb("Wsb", [128, 128], BF16)
        nU = P.sb("nU", [128, 128], BF16)
        ynb = P.sb("ynb", [128, 128], BF16)
        rstd2 = P.sb("rstd2", [128, 2], F32)
        pzs = [P.ps(f"pz{i}", [128, 512], F32) for i in range(2)]
        pA = P.ps("pA", [128, 512], F32)
        pI = [P.ps(f"pI{h}", [128, 512], F32) for h in range(2)]
        pS = P.ps("pS", [128, 512], F32)
        pm = P.ps("pm", [128, 512], F32)
        ptr = P.ps("ptr", [128, 1024], BF16)

        ident_b = lambda n: cmb[:n, 0, :n]
        m_incl = lambda n: cmat[:n, 1, :n]
        m_strict = lambda n: cmat[:n, 2, :n]
        m_low = lambda n: cmat[:n, 3, :n]
        blockones = cmat[:, 4, :]
        pv = lambda base, j: pvec[:, base + j:base + j + 1]

        cl = lambda out, in_: O.dma(out, in_, "const", wait_all=True)
        cl(normg[:], normg_d); cl(cmat[:], cmat_d); cl(lnbc[:], lnbc_d); cl(fing[:], fing_d); cl(pvec[:], pvec_d)
        wsTf = gv[:, 0, :].rearrange("p (h i) -> p h i", i=128)
        w2a2f = gv[:, 1, :]
        O.dma(wsTf, wsT_d, "const", wait_all=True, wk=(gv, 0)); O.dma(w2a2f, w2a2_d, "const", wait_all=True, wk=(gv, 1)); cl(shT[:], shT_d)
        O.copy(cmb[:], cmat[:])
        O.copy(w2a2b[:], w2a2f, eng="act", rk=[(gv, 1)])
        O.ts(omka[:], pvec[:, PV_KA:PV_KA + 8], -1.0, 1.0, op0=ALU.mult, op1=ALU.add)
        O.ts(hbias[:], pvec[:, PV_W0:PV_W0 + 16], -1.0, None, op0=ALU.mult)
        O.memset(mhalf[:], -0.5)
        for h in range(8):
            O.tt(wsTb[:, h, :], wsTf[:, h, :], cmat[:, 1, :], ALU.mult, rk=[(gv, 0), cmat])
        O.memset(HF[:], 0.0); O.memset(HB[:], 0.0); O.memset(lastcol[:], 0.0); O.memset(ones[:], 1.0)
        O.memset(zraw[:, 0:1], 0.0)

        pieces = [(7168, 128), (3072, 2048), (5120, 2048), (0, 2048), (2048, 1024)]
        stg_f = [xt[:, 0, :], xt[:, 1, :]]
        stg_b = [junk, xn]
        k = 0
        for pi, (c0, w) in enumerate(pieces):
            for c in range(16):
                sf = stg_f[k % 2]; sbf = stg_b[k % 2]
                O.dma(sf[:, 0:w], win_d[c * 128:(c + 1) * 128, c0:c0 + w], f"stgf{k % 2}", wk=(xt, k % 2))
                if k % 2 == 0:
                    O.ts(sbf[:, 0:w], sf[:, 0:w], normg[:, c:c + 1], None, op0=ALU.mult, rk=[(xt, k % 2)])
                else:
                    O.act(sbf[:, 0:w], sf[:, 0:w], AF.Copy, scale=normg[:, c:c + 1], rk=[(xt, k % 2)])
                if c0 < 3072:
                    g0 = c0 // 512; ng = w // 512
                    for gg in range(ng):
                        O.dma(WbA[g0 + gg, :, c, :], sbf[:, gg * 512:(gg + 1) * 512], f"stgb{k % 2}_{gg}", wk=(WbA, (g0 + gg, c)))
                else:
                    q0 = (c0 - 3072) // 128; nq = w // 128
                    for qq in range(0, nq, 4):
                        nn = min(4, nq - qq)
                        O.dma(WbB[q0 + qq:q0 + qq + nn, :, c, :].rearrange("q p n -> p q n"),
                              sbf[:, qq * 128:(qq + nn) * 128].rearrange("p (q n) -> p q n", n=128), f"stgb{k % 2}_{qq // 4}",
                              wk=(WbB, ((q0 + qq) // 4, c)))
                k += 1
        for c in range(16):
            sf = stg_f[k % 2]; sbf = stg_b[k % 2]
            O.dma(sf[:, 0:2048], wout_d[c * 128:(c + 1) * 128, :], f"stgf{k % 2}", wk=(xt, k % 2))
            O.copy(sbf[:, 0:2048], sf[:, 0:2048], eng="act" if k % 2 else "dve", rk=[(xt, k % 2)])
            for gg in range(4):
                O.dma(WbO[gg, :, c, :], sbf[:, gg * 512:(gg + 1) * 512], f"stgb{k % 2}_{gg}", wk=(WbO, (gg, c)))
            k += 1

        wslotB = [0]
        wslotA = [0]

        def load_wB(q):
            s = wslotB[0] % 3; wslotB[0] += 1
            O.dma(wB[s][:], WbB[q], f"wB{s}", rk=[(WbB, (q // 4, c)) for c in range(16)])
            return wB[s]

        def load_wA(T, g):
            s = wslotA[0] % 2; wslotA[0] += 1
            O.dma(wA[s][:], T[g], f"wA{s}", rk=[(T, (g, c)) for c in range(16)])
            return wA[s]

        def rsqrt(out, in_, n, scale, eps):
            O.ts(out, in_, scale, eps, op0=ALU.mult, op1=ALU.add)
            O.act(out, out, AF.Ln)
            O.act(out, out, AF.Exp, scale=-0.5)

        def run_merged(gens):
            act = [[g, float(w), 0] for g, w in gens if g is not None]
            while act:
                a = min(act, key=lambda t: t[2] / t[1])
                try:
                    next(a[0]); a[2] += 1
                except StopIteration:
                    act.remove(a)

        def run_seq(g):
            for _ in g:
                pass

        def superblock(x_src, y_dst, tiles, W, sample, seq0=0, sbp=0):
            nt = len(tiles)
            xi = lambda ti: sbp * NT + ti

            def stage0():
                for ti, (off, Pt) in enumerate(tiles):
                    O.dma(xt[:Pt, xi(ti), :], x_src[off:off + Pt, :], f"x{xi(ti)}", wk=(xt, xi(ti)))
                    O.act(junk[:Pt, :], xt[:Pt, xi(ti), :], AF.Square, accum=sm[:Pt, 0:1], rk=[(xt, xi(ti))])
                    rsqrt(sm[:Pt, 2:3], sm[:Pt, 0:1], 1, 1.0 / D, NORM_EPS)
                    O.ts(xn[:Pt, :], xt[:Pt, xi(ti), :], sm[:Pt, 2:3], None, op0=ALU.mult, rk=[(xt, xi(ti))])
                    for c4 in range(4):
                        for j in range(4):
                            c = c4 * 4 + j
                            O.tr(ptr[:, j * Pt:(j + 1) * Pt], xn[:Pt, c * 128:(c + 1) * 128], ident_b(Pt))
                        src = ptr[:, 0:4 * Pt].rearrange("p (j t) -> p j t", t=Pt)
                        O.copy(hT[:, c4 * 4:c4 * 4 + 4, off:off + Pt], src, eng="act" if c4 % 2 else "dve", wk=(hT, ti))
                        yield

            def proj_B(q, half):
                w = load_wB(q)
                for c in range(16):
                    O.mm(pzs[half][:, 0:W], w[:, c, :], hT[:, c, 0:W], start=(c == 0), stop=(c == 15))
                return pzs[half][:, 0:W]

            def shifted(q, half, dst):
                src = proj_B(q, half)
                O.copy(zraw[:, 1:W + 1], src, eng="act")
                if not sample:
                    O.copy(zraw[:, 0:1], lastcol[:, q:q + 1])
                O.tt(diff[:, 0:W], zraw[:, 0:W], zraw[:, 1:W + 1], ALU.subtract)
                if sample:
                    for ti, (off, Pt) in enumerate(tiles):
                        O.tt(diff[:, off:off + 1], shT[:, seq0 + ti, q:q + 1], zraw[:, off + 1:off + 2], ALU.subtract)
                        O.copy(shs[:, seq0 + ti, q:q + 1], zraw[:, off + Pt:off + Pt + 1])
                else:
                    O.copy(lastcol[:, q:q + 1], zraw[:, W:W + 1])
                O.stt(dst[:, 0:W], diff[:, 0:W], pv(PV_MU, q), zraw[:, 1:W + 1], ALU.mult, ALU.add)

            def lora():
                shifted(24, 0, mixl)
                O.act(tg[0:64, 0:W], mixl[0:64, 0:W], AF.Exp, scale=-2.0)
                O.act(tg[0:64, 0:W], tg[0:64, 0:W], AF.Ln, bias=1.0)
                O.act(tg[0:64, 0:W], tg[0:64, 0:W], AF.Exp, scale=-1.0)
                O.ts(th[0:64, 0:W], tg[0:64, 0:W], 2.0, -1.0, op0=ALU.mult, op1=ALU.add)
                O.copy(th[64:128, 0:W], mixl[64:128, 0:W])
                yield

            def prep(p):
                s = p % 2
                shifted(p, 1, mixr); yield
                shifted(8 + p, 0, mixk); yield
                shifted(16 + p, 1, mixv); yield
                src = proj_B(25 + p, 0)
                O.act(tg[:, 0:W], src, AF.Exp, scale=-1.0)
                O.act(tg[:, 0:W], tg[:, 0:W], AF.Ln, bias=1.0)
                O.act(tg[:, 0:W], tg[:, 0:W], AF.Exp, scale=-1.0)
                O.tt(sgb[s][:, 0:W], tg[:, 0:W], src, ALU.mult); yield
                O.mm(pm[:, 0:W], w2a2b[0:64, p * 128:(p + 1) * 128], th[0:64, 0:W])
                O.act(sd[:, 0:W], pm[:, 0:W], AF.Exp, bias=hbias[:, p:p + 1], scale=-1.0)
                O.act(sd[:, 0:W], sd[:, 0:W], AF.Ln, bias=1.0)
                O.act(sd[:, 0:W], sd[:, 0:W], AF.Exp, scale=-1.0)
                O.mm(pm[:, 256:256 + W], w2a2b[64:128, p * 128:(p + 1) * 128], th[64:128, 0:W])
                O.act(aa[:, 0:W], pm[:, 256:256 + W], AF.Exp, bias=hbias[:, 8 + p:9 + p], scale=-1.0)
                O.act(aa[:, 0:W], aa[:, 0:W], AF.Ln, bias=1.0)
                O.act(aa[:, 0:W], aa[:, 0:W], AF.Exp, scale=-1.0); yield
                for ti, (off, Pt) in enumerate(tiles):
                    O.scan(cum[:, off:off + Pt], ones[:, off:off + Pt], sd[:, off:off + Pt], 0.0, ALU.mult, ALU.add)
                    O.ts(nb[:, ti:ti + 1], cum[:, off + Pt - 1:off + Pt], -C0, None, op0=ALU.mult)
                    O.act(gC[s][:, ti:ti + 1], cum[:, off + Pt - 1:off + Pt], AF.Exp, scale=-C0)
                    O.act(e_rem[:, off:off + Pt], cum[:, off:off + Pt], AF.Exp, scale=C0, bias=nb[:, ti:ti + 1])
                yield
                O.act(e_incl[:, 0:W], cum[:, 0:W], AF.Exp, scale=-C0)
                O.tt(excl[:, 0:W], cum[:, 0:W], sd[:, 0:W], ALU.subtract, eng="pool")
                O.act(e_excl[:, 0:W], excl[:, 0:W], AF.Exp, scale=-C0)
                O.act(e_inv[:, 0:W], cum[:, 0:W], AF.Exp, scale=C0); yield
                O.ts(kk[:, 0:W], mixk[:, 0:W], pv(PV_KK, p), None, op0=ALU.mult)
                O.tt(sq[:, 0:W], kk[:, 0:W], kk[:, 0:W], ALU.mult, eng="pool")
                O.mm(pm[:, 0:W], blockones, sq[:, 0:W])
                O.ts(rn[:, 0:W], pm[:, 0:W], 1e-24, None, op0=ALU.max)
                O.act(rn[:, 0:W], rn[:, 0:W], AF.Ln)
                O.act(rn[:, 0:W], rn[:, 0:W], AF.Exp, scale=-0.5); yield
                O.tt(kk[:, 0:W], kk[:, 0:W], rn[:, 0:W], ALU.mult)
                O.ts(tmp[:, 0:W], aa[:, 0:W], pv(PV_KA, p), omka[:, p:p + 1], op0=ALU.mult, op1=ALU.add, eng="pool")
                O.tt(kp[:, 0:W], mixk[:, 0:W], tmp[:, 0:W], ALU.mult)
                O.tt(bbv[:, 0:W], kk[:, 0:W], aa[:, 0:W], ALU.mult); yield
                O.tt(kt[s][:, 0:W], kk[:, 0:W], e_excl[:, 0:W], ALU.mult, eng="pool")
                O.tt(rt[s][:, 0:W], mixr[:, 0:W], e_incl[:, 0:W], ALU.mult)
                for hd in range(2):
                    O.stt(kh[s][hd][:, 0:W], kp[:, 0:W], pv(PV_HM, hd), e_inv[:, 0:W], ALU.mult, ALU.mult)
                yield
                for hd in range(2):
                    O.stt(bh[s][hd][:, 0:W], bbv[:, 0:W], pv(PV_HM, hd), e_inv[:, 0:W], ALU.mult, ALU.mult)
                O.tt(khg[s][:, 0:W], kp[:, 0:W], e_rem[:, 0:W], ALU.mult, eng="pool")
                O.tt(bhg[s][:, 0:W], bbv[:, 0:W], e_rem[:, 0:W], ALU.mult); yield
                O.copy(vbf[s][:, 0:W], mixv[:, 0:W], eng="act")
                O.stt(rkv[:, 0:W], mixr[:, 0:W], pv(PV_RK, p), kp[:, 0:W], ALU.mult, ALU.mult)
                O.mm(pm[:, 256:256 + W], blockones, rkv[:, 0:W])
                O.tt(bv[s][:, 0:W], pm[:, 256:256 + W], mixv[:, 0:W], ALU.mult); yield

            def inv_unit(u):
                p, ti = u // nt, u % nt
                s, par = p % 2, u % 2
                off, Pt = tiles[ti]
                sl = slice(off, off + Pt)
                w = Pt
                for j, srcb in enumerate((vbf[s], khg[s], bhg[s])):
                    O.tr(ptr[:Pt, j * 128:(j + 1) * 128], srcb[:, sl], ident_b(128))
                O.copy(tok3[par][:Pt, :, :], ptr[:Pt, 0:384].rearrange("p (j f) -> p j f", f=128), eng="act")
                yield
                for hd in range(2):
                    hs = slice(hd * 64, (hd + 1) * 64)
                    O.mm(pA[:Pt, 0:Pt], kh[s][hd][:, sl], kt[s][:, sl])
                    O.mm(pA[:Pt, 128:128 + Pt], kh[s][hd][:, sl], rt[s][:, sl])
                    O.mm(pA[:Pt, 256:256 + Pt], bh[s][hd][:, sl], kt[s][:, sl])
                    O.mm(pA[:Pt, 384:384 + Pt], bh[s][hd][:, sl], rt[s][:, sl])
                    O.mm(pI[hd][:Pt, 0:Pt], kt[s][:, sl], bh[s][hd][:, sl])
                    yield
                    X0 = XMS[par][hd][0]
                    O.tt(Asb[par][hd][:Pt, 0, :Pt], pA[:Pt, 0:Pt], m_strict(Pt), ALU.mult)
                    O.tt(Asb[par][hd][:Pt, 1, :Pt], pA[:Pt, 128:128 + Pt], m_incl(Pt), ALU.mult)
                    O.tt(Asb[par][hd][:Pt, 2, :Pt], pA[:Pt, 384:384 + Pt], m_incl(Pt), ALU.mult)
                    O.stt(X0[:Pt, w:2 * w], pA[:Pt, 256:256 + Pt], -1.0, m_strict(Pt), ALU.mult, ALU.mult, wk=(X0, "x"))
                    O.stt(X0[:Pt, 0:w], pI[hd][:Pt, 0:Pt], -1.0, m_low(Pt), ALU.mult, ALU.mult, wk=(X0, "m"))
                    yield
                nlev = int(math.log2(Pt))
                for k in range(nlev):
                    for hd in range(2):
                        cur = XMS[par][hd][k % 2]; nxt = XMS[par][hd][(k + 1) % 2]
                        Mk, Xk, Sk = cur[:Pt, 0:w], cur[:Pt, w:2 * w], cur[:Pt, 2 * w:3 * w]
                        if k == 0:
                            O.mm(pI[hd][:Pt, w:2 * w], Mk, Xk, rk=[(cur, "m"), (cur, "x")])
                            O.mm(pI[hd][:Pt, 0:w], Xk, Mk, rk=[(cur, "m"), (cur, "x")])
                            O.tt(nxt[:Pt, 2 * w:3 * w], Xk, cmat[:Pt, 0, :Pt], ALU.add, rk=[(cur, "x"), cmat], wk=(nxt, "s"))
                            O.copy(nxt[:Pt, 0:2 * w], pI[hd][:Pt, 0:2 * w], eng="act", wk=(nxt, "mx"))
                        elif k < nlev - 1:
                            O.mm(pI[hd][:Pt, w:3 * w], Mk, cur[:Pt, w:3 * w], rk=[(cur, "mx"), (cur, "s")])
                            O.mm(pI[hd][:Pt, 0:w], Xk, Mk, rk=[(cur, "mx")])
                            O.copy(nxt[:Pt, 0:2 * w], pI[hd][:Pt, 0:2 * w], eng="act", wk=(nxt, "mx"))
                            O.tt(nxt[:Pt, 2 * w:3 * w], pI[hd][:Pt, 2 * w:3 * w], Sk, ALU.add, rk=[pI[hd], (cur, "s")],
                                 wk=(nxt, "s"))
                        else:
                            O.mm(pI[hd][:Pt, 2 * w:3 * w], Mk, Sk, rk=[(cur, "mx"), (cur, "s")])
                            O.tt(TT[par][hd][:Pt, :Pt], pI[hd][:Pt, 2 * w:3 * w], Sk, ALU.add, rk=[pI[hd], (cur, "s")])
                        yield

            def state_unit(u):
                p, ti = u // nt, u % nt
                s, par = p % 2, u % 2
                off, Pt = tiles[ti]
                sl = slice(off, off + Pt)
                if sample:
                    O.dma(HsF[:], hs0_d[seq0 + ti, p], "hsin")
                    O.copy(HsB[:], HsF[:])
                    Hf, Hb = HsF, HsB
                    Hfv = lambda a, b: HsF[a, b]
                    Hbv = HsB[:, :]
                else:
                    Hf, Hb = HF, HB
                    Hfv = lambda a, b: HF[a, p, b]
                    Hbv = HB[:, p, :]
                hkey = (Hf, None if sample else p)
                hbkey = (Hb, None if sample else p)
                T3 = tok3[par]
                Vt = lambda hd: T3[:Pt, 0, hd * 64:(hd + 1) * 64]
                A_ = Asb[par]
                O.mm(pS[:Pt, 0:128], kt[s][:, sl], Hbv, start=True, stop=False, rk=[kt[s], hbkey])
                for hd in range(2):
                    O.mm(pS[:Pt, hd * 64:(hd + 1) * 64], A_[hd][:Pt, 0, :Pt], Vt(hd), start=False, stop=(hd == 1))
                O.copy(Wsb[:Pt, :], pS[:Pt, 0:128], eng="act")
                yield
                for hd in range(2):
                    O.mm(pS[:Pt, 128 + hd * 64:128 + (hd + 1) * 64], TT[par][hd][:Pt, :Pt], Wsb[:Pt, hd * 64:(hd + 1) * 64])
                O.act(nU[:Pt, :], pS[:Pt, 128:256], AF.Copy, scale=-1.0)
                yield
                O.mm(pS[:Pt, 256:384], rt[s][:, sl], Hbv, start=True, stop=False, rk=[rt[s], hbkey])
                for hd in range(2):
                    O.mm(pS[:Pt, 256 + hd * 64:256 + (hd + 1) * 64], A_[hd][:Pt, 1, :Pt], Vt(hd), start=False, stop=False)
                for hd in range(2):
                    O.mm(pS[:Pt, 256 + hd * 64:256 + (hd + 1) * 64], A_[hd][:Pt, 2, :Pt], nU[:Pt, hd * 64:(hd + 1) * 64],
                         start=False, stop=(hd == 1))
                O.mm(pS[:, 384:512], T3[:Pt, 1, :], T3[:Pt, 0, :], start=True, stop=False)
                O.mm(pS[:, 384:512], T3[:Pt, 2, :], nU[:Pt, :], start=False, stop=True)
                yield
                for hd in range(2):
                    hs = slice(hd * 64, (hd + 1) * 64)
                    O.stt(Hfv(hs, hs), Hfv(hs, hs), gC[s][hs, ti:ti + 1], pS[hs, 384 + hd * 64:384 + (hd + 1) * 64],
                          ALU.mult, ALU.add, rk=[hkey, pS, gC[s]], wk=hkey)
                if sample:
                    O.dma(hs_d[seq0 + ti, p], HsF[:], "hsout")
                else:
                    O.copy(Hbv, HF[:, p, :], eng="act", rk=[hkey], wk=hbkey)
                yield
                for hd in range(2):
                    O.bn_stats(st6[:Pt, hd, :], pS[:Pt, 256 + hd * 64:256 + (hd + 1) * 64])
                    O.bn_aggr(mv[:Pt, hd, :], st6[:Pt, hd, :])
                rsqrt(rstd2[:Pt, :], mv[:Pt, :, 1], 2, 1.0, GN_EPS)
                for hd in range(2):
                    O.ts(ynb[:Pt, hd * 64:(hd + 1) * 64], pS[:Pt, 256 + hd * 64:256 + (hd + 1) * 64], mv[:Pt, hd, 0:1],
                         rstd2[:Pt, hd:hd + 1], op0=ALU.subtract, op1=ALU.mult)
                yield
                O.tr(ptr[:, 512:512 + Pt], ynb[:Pt, :], ident_b(Pt))
                O.act(ynT[s][:, sl], ptr[:, 512:512 + Pt], AF.Identity, scale=pv(PV_GNG, p), bias=pv(PV_GNB, p))
                yield
                if ti == nt - 1:
                    O.tt(ynT[s][:, 0:W], ynT[s][:, 0:W], bv[s][:, 0:W], ALU.add)
                    O.tt(outT[:, 8 + p, 0:W], ynT[s][:, 0:W], sgb[s][:, 0:W], ALU.mult, wk=(outT, ("b", p)))
                    yield

            def a_branch():
                for gi, g in enumerate((2, 3, 0, 1, 4, 5)):
                    w = load_wA(WbA, g)
                    col = (g % 2) * 512
                    for ti, (off, Pt) in enumerate(tiles):
                        half = (gi * nt + ti) % 2
                        acc = pzs[half][:Pt, :]
                        for c in range(16):
                            O.mm(acc, hT[:, c, off:off + Pt], w[:, c, :], start=(c == 0), stop=(c == 15), rk=[(hT, ti), w])
                        if g in (2, 3):
                            O.act(gv[:Pt, ti, col:col + 512], acc, AF.Gelu, wk=(gv, ti))
                        elif g in (0, 1):
                            O.act(gus[:Pt, ti, col:col + 512], acc, AF.Gelu, wk=(gus, ti))
                        else:
                            O.act(gat[:Pt, :], acc, AF.Tanh, scale=0.5)
                            O.stt(gat2[:Pt, :], gat[:Pt, :], 1.0, acc, ALU.add, ALU.mult)
                            O.stt(gus[:Pt, ti, col:col + 512], gat2[:Pt, :], 0.5, gus[:Pt, ti, col:col + 512], ALU.mult,
                                  ALU.mult, rk=[(gus, ti), gat2], wk=(gus, ti))
                        yield
                    if g == 3:
                        for ti, (off, Pt) in enumerate(tiles):
                            for j in range(2):
                                O.bn_stats(st6a[:Pt, j, :], gv[:Pt, ti, j * 512:(j + 1) * 512], rk=[(gv, ti)])
                            O.bn_aggr(mva[:Pt, :], st6a[:Pt, :, :].rearrange("p a b -> p (a b)"))
                            rsqrt(sma[:Pt, 1:2], mva[:Pt, 1:2], 1, 1.0, LN_EPS)
                            O.ts(gv[:Pt, ti, :], gv[:Pt, ti, :], mva[:Pt, 0:1], sma[:Pt, 1:2], op0=ALU.subtract, op1=ALU.mult,
                                 rk=[(gv, ti)], wk=(gv, ti))
                            O.tt(gv[:Pt, ti, :], gv[:Pt, ti, :], lnbc[:Pt, 0, :], ALU.mult, rk=[(gv, ti), lnbc], wk=(gv, ti))
                            yield
                            if sample:
                                O.tt(gv[:Pt, ti, :], gv[:Pt, ti, :], lnbc[:Pt, 1, :], ALU.add, rk=[(gv, ti), lnbc], wk=(gv, ti))
                                O.dma(vn_d[seq0 * 64 + off:seq0 * 64 + off + Pt, :], gv[:Pt, ti, :], f"vn{ti}",
                                      rk=[(gv, ti)])
                                O.copy(vnb[:Pt, ti, :], gv[:Pt, ti, :], eng="act", rk=[(gv, ti)], wk=(vnb, ti))
                            else:
                                O.tt(vnb[:Pt, ti, :], gv[:Pt, ti, :], lnbc[:Pt, 1, :], ALU.add, rk=[(gv, ti), lnbc],
                                     wk=(vnb, ti))
                            yield
                for ti, (off, Pt) in enumerate(tiles):
                    for h in range(8):
                        O.mm(pzs[h // 4][:Pt, (h % 4) * 128:(h % 4 + 1) * 128], wsTb[:Pt, h, :Pt],
                             vnb[:Pt, ti, h * 128:(h + 1) * 128], rk=[wsTb, (vnb, ti)])
                    for h in range(8):
                        O.stt(oa[:Pt, h * 128:(h + 1) * 128], pzs[h // 4][:Pt, (h % 4) * 128:(h % 4 + 1) * 128],
                              pv(PV_SGUB, h)[:Pt, :], gus[:Pt, ti, h * 128:(h + 1) * 128], ALU.add, ALU.mult,
                              rk=[pzs[h // 4], (gus, ti), pvec])
                    yield
                    for c4 in range(2):
                        for j in range(4):
                            c = c4 * 4 + j
                            O.tr(ptr[:, j * Pt:(j + 1) * Pt], oa[:Pt, c * 128:(c + 1) * 128], ident_b(Pt))
                        src = ptr[:, 0:4 * Pt].rearrange("p (j t) -> p j t", t=Pt)
                        O.copy(outT[:, c4 * 4:c4 * 4 + 4, off:off + Pt], src, eng="act" if c4 % 2 else "dve",
                               wk=(outT, ("a", ti, c4)))
                        yield

            def out_proj():
                for g in range(4):
                    w = load_wA(WbO, g)
                    for ti, (off, Pt) in enumerate(tiles):
                        half = (g * nt + ti) % 2
                        acc = pzs[half][:Pt, :]
                        for c in range(16):
                            O.mm(acc, outT[:, c, off:off + Pt], w[:, c, :], start=(c == 0), stop=(c == 15), rk=[outT, w])
                        O.tt(xt[:Pt, xi(ti), g * 512:(g + 1) * 512], acc, xt[:Pt, xi(ti), g * 512:(g + 1) * 512], ALU.add,
                             rk=[pzs[half], (xt, xi(ti))], wk=(xt, xi(ti)))
                        yield
                for ti, (off, Pt) in enumerate(tiles):
                    O.act(junk[:Pt, :], xt[:Pt, xi(ti), :], AF.Square, accum=sm[:Pt, 5:6], rk=[(xt, xi(ti))])
                    rsqrt(sm[:Pt, 7:8], sm[:Pt, 5:6], 1, 1.0 / D, NORM_EPS)
                    O.stt(xt[:Pt, xi(ti), :], xt[:Pt, xi(ti), :], sm[:Pt, 7:8], fing[:Pt, :], ALU.mult, ALU.mult,
                          rk=[(xt, xi(ti)), fing], wk=(xt, xi(ti)))
                    O.dma(y_dst[off:off + Pt, :], xt[:Pt, xi(ti), :], f"y{xi(ti)}", rk=[(xt, xi(ti))])
                    yield

            run_seq(stage0())
            run_seq(lora())
            run_seq(prep(0))
            NU = 8 * nt
            ab = a_branch()
            n_inv = 3 + 2 * int(math.log2(tiles[0][1]))
            if nt == 2:
                run_merged([(inv_unit(0), n_inv), (ab_slice(ab, 4), 4)])
                for u in range(NU):
                    p = u // nt
                    gens = [(state_unit(u), 7)]
                    if u + 1 < NU:
                        gens.append((inv_unit(u + 1), n_inv))
                    if u % nt == 0 and p + 1 < 8:
                        gens.append((prep(p + 1), 13))
                    else:
                        gens.append((ab_slice(ab, 5), 5))
                    run_merged(gens)
            else:
                for u in range(NU):
                    if u > 0 and u % nt == 0:
                        run_seq(prep(u // nt))
                    run_merged([(inv_unit(u), n_inv), (ab_slice(ab, 4), 4)])
                    run_seq(state_unit(u))
            run_seq(ab)
            run_seq(out_proj())

        def ab_slice(g, n):
            for _ in range(n):
                try:
                    next(g)
                except StopIteration:
                    return
                yield

        for sbi in range(NP // SBW):
            superblock(xp_d[sbi * SBW:(sbi + 1) * SBW, :], yp_d[sbi * SBW:(sbi + 1) * SBW, :],
                       [(i * 128, 128) for i in range(SBW // 128)], SBW, False, 0, sbi % 2)
        O.dma(shp_d, lastcol[:], "misc_out")
        O.dma(hp_d.rearrange("q p f -> p q f"), HF[:], "misc_out")
        for s0 in range(0, NS, NT):
            n = min(NT, NS - s0)
            superblock(xs_d[s0 * 64:(s0 + n) * 64, :], ys_d[s0 * 64:(s0 + n) * 64, :], [(i * 64, 64) for i in range(n)], n * 64, True, s0,
                       (NP // SBW + s0 // NT) % 2)
        if NS > 0:
            O.dma(shs_d, shs[:], "misc_out")
        out_tags = ["misc_out", "hsout"] + [f"y{i}" for i in range(2 * NT)] + [f"vn{i}" for i in range(NT)]
        P.emit_all(out_tags, eng="sp")
        n_ops = len(P.ops)
    return nc, n_ops
N_CORES = 8
_NC_CACHE = {}


def _consts():
    i = np.arange(128)
    ident = np.eye(128, dtype=np.float32)
    incl = (i[:, None] <= i[None, :]).astype(np.float32)
    strict = (i[:, None] < i[None, :]).astype(np.float32)
    low = (i[:, None] > i[None, :]).astype(np.float32)
    blk = ((i[:, None] // 64) == (i[None, :] // 64)).astype(np.float32)
    return np.ascontiguousarray(np.stack([ident, incl, strict, low, blk], axis=1))


def _cols(v, n):
    return np.ascontiguousarray(np.asarray(v, np.float32).reshape(n, 128).T)


def _shared_inputs(norm_g, w_in, w_out, sgu_ln_g, sgu_ln_b, sgu_w, sgu_b, shift_mu, w0, w2, a0, a2, k_k, k_a, r_k,
                   gn_g, gn_b, final_g):
    pvec = np.concatenate([
        _cols(shift_mu[0], 25), _cols(w0[0], 8), _cols(a0[0], 8), _cols(k_k[0], 8), _cols(k_a[0], 8),
        _cols(np.asarray(r_k[0]).reshape(-1), 8), _cols(gn_g[0], 8), _cols(gn_b[0], 8),
        np.ascontiguousarray(np.asarray(sgu_b[0], np.float32).T),
        np.stack([(np.arange(128) < 64), (np.arange(128) >= 64)], axis=1).astype(np.float32)], axis=1)
    return {
        "w_in": np.ascontiguousarray(np.asarray(w_in[0], np.float32)),
        "w_out": np.ascontiguousarray(np.asarray(w_out[0], np.float32)),
        "i_normg": _cols(norm_g[0], 16),
        "i_cmat": _consts(),
        "i_lnbc": np.ascontiguousarray(np.broadcast_to(
            np.stack([np.asarray(sgu_ln_g[0], np.float32), np.asarray(sgu_ln_b[0], np.float32)])[None], (128, 2, DA))),
        "i_fing": np.ascontiguousarray(np.broadcast_to(np.asarray(final_g, np.float32)[None], (128, D))),
        "i_pvec": np.ascontiguousarray(pvec.astype(np.float32)),
        "i_wsT": np.ascontiguousarray(np.transpose(np.asarray(sgu_w[0], np.float32), (2, 0, 1))),
        "i_w2a2": np.ascontiguousarray(np.concatenate([np.asarray(w2[0], np.float32), np.asarray(a2[0], np.float32)], 0)),
    }


def _h_blockdiag(S):
    n = S.shape[0]
    out = np.zeros((n, 8, 128, 128), np.float32)
    Ht = np.transpose(S, (0, 1, 3, 2))
    out[:, :, 0:64, 0:64] = Ht[:, 0::2]
    out[:, :, 64:128, 64:128] = Ht[:, 1::2]
    return out


def _h_unblock(Hbd):
    n = Hbd.shape[0]
    S = np.zeros((n, 16, 64, 64), np.float32)
    S[:, 0::2] = np.transpose(Hbd[:, :, 0:64, 0:64], (0, 1, 3, 2))
    S[:, 1::2] = np.transpose(Hbd[:, :, 64:128, 64:128], (0, 1, 3, 2))
    return S


def run_layout(x_prompt, x_sample, state_b_wkv, state_b_shift, shared, n_cores, NP, NS):
    key = (NP, NS)
    if key not in _NC_CACHE:
        _NC_CACHE[key] = build(NP, NS)[0]
    nc = _NC_CACHE[key]
    B = x_prompt.shape[0]
    in_maps = []
    for c in range(n_cores):
        m = dict(shared)
        m["xp"] = np.ascontiguousarray(x_prompt[c], np.float32) if c < B else np.zeros((NP, D), np.float32)
        sl = slice(c * NS, (c + 1) * NS)
        m["xs"] = np.ascontiguousarray(np.asarray(x_sample[sl], np.float32).reshape(NS * 64, D))
        m["hs0"] = _h_blockdiag(np.asarray(state_b_wkv[0, sl], np.float32))
        sh = np.asarray(state_b_shift[0, sl, 0, :], np.float32).reshape(NS, 25, 128)
        m["shiftT"] = np.ascontiguousarray(np.transpose(sh, (2, 0, 1)))
        in_maps.append(m)
    res = run_bass_kernel_spmd(nc, in_maps, core_ids=list(range(n_cores)))
    R = res.results
    y_prompt = np.stack([R[c]["yp"] for c in range(B)]).astype(np.float32)
    y_sample = np.concatenate([R[c]["ys"].reshape(NS, 64, D) for c in range(n_cores)]).astype(np.float32)
    wkv_p = np.concatenate([_h_unblock(R[c]["hp_out"][None]) for c in range(B)])[None]
    shp = np.stack([R[c]["shp_out"].T.reshape(1, NSH) for c in range(B)])[None]
    wkv_s = np.concatenate([_h_unblock(R[c]["hs_out"]) for c in range(n_cores)])[None]
    shs = np.concatenate([np.transpose(R[c]["shs_out"], (1, 2, 0)).reshape(NS, 1, NSH) for c in range(n_cores)])[None]
    vn = np.concatenate([R[c]["vn_out"].reshape(NS, 64, DA) for c in range(n_cores)])[None]
    return (y_prompt, y_sample, wkv_p.astype(np.float32), shp.astype(np.float32), wkv_s.astype(np.float32),
            shs.astype(np.float32), vn.astype(np.float32))


def kernel(x_prompt, x_sample, state_b_wkv, state_b_shift, norm_g, w_in, w_out, sgu_ln_g, sgu_ln_b,
           sgu_w, sgu_b, shift_mu, w0, w2, a0, a2, k_k, k_a, r_k, gn_g, gn_b, final_g):
    shared = _shared_inputs(norm_g, w_in, w_out, sgu_ln_g, sgu_ln_b, sgu_w, sgu_b, shift_mu, w0, w2, a0, a2,
                            k_k, k_a, r_k, gn_g, gn_b, final_g)
    x_prompt = np.asarray(x_prompt); x_sample = np.asarray(x_sample)
    return run_layout(x_prompt, x_sample, np.asarray(state_b_wkv), np.asarray(state_b_shift), shared,
                      N_CORES, x_prompt.shape[1], x_sample.shape[0] // N_CORES)
```

```python
from concourse.bass_utils import run_bass_kernel_spmd
import sys
import numpy as np
import concourse.bass as bass
import concourse.mybir as mybir

F32 = mybir.dt.float32
BF16 = mybir.dt.bfloat16
AF = mybir.ActivationFunctionType
ALU = mybir.AluOpType

SEM_CAP = 16384


def _key(x):
    sub = None
    if isinstance(x, tuple):
        x, sub = x
    t = getattr(x, "tensor", x)
    return (t.name, sub)


class Prog:
    COMPUTE = ("pe", "act", "dve", "pool")

    def __init__(self, nc, stack):
        self.nc = nc
        self.stack = stack
        self.ops = []
        self.state = {}
        self.eng = {"pe": nc.tensor, "act": nc.scalar, "dve": nc.vector, "pool": nc.gpsimd, "sp": nc.sync}
        self.psum_names = set()
        self.bank_last = {}

    def sb(self, name, shape, dt):
        return self.stack.enter_context(self.nc.sbuf_tensor(name, list(shape), dt))

    def ps(self, name, shape, dt):
        self.psum_names.add(name)
        return self.stack.enter_context(self.nc.psum_tensor(name, list(shape), dt))

    def _states(self, key, create=True):
        name, sub = key
        d = self.state.setdefault(name, {})
        if sub is None:
            if None not in d:
                d[None] = [None, []]
            return list(d.values())
        out = []
        if sub not in d:
            d[sub] = [None, []]
        out.append(d[sub])
        if None in d:
            out.append(d[None])
        return out

    def op(self, eng, fn, reads=(), writes=(), dma_tag=None, wait_all=False, cost=100.0, lat=0.0):
        idx = len(self.ops)
        deps = {}
        rk = [_key(r) for r in reads if r is not None and not isinstance(r, (int, float))]
        wk = [_key(w) for w in writes if w is not None]
        for k in rk:
            for st in self._states(k):
                if st[0] is not None:
                    deps.setdefault(st[0], "raw")
        for k in wk:
            for st in self._states(k):
                if st[0] is not None:
                    deps.setdefault(st[0], "waw")
                for r in st[1]:
                    if r != idx:
                        deps.setdefault(r, "war")
        for name in {k[0] for k in rk + wk if k[0] in self.psum_names}:
            bl = self.bank_last.setdefault(name, {})
            for e2, i2 in bl.items():
                if e2 != eng:
                    deps.setdefault(i2, "bank")
                else:
                    deps.setdefault(i2, "order")
            bl[eng] = idx
        for k in rk:
            name, sub = k
            if sub is None:
                for st in self._states(k):
                    st[1].append(idx)
            else:
                self.state[name][sub][1].append(idx)
        for k in wk:
            name, sub = k
            if sub is None:
                d = self.state[name]
                for s in list(d.keys()):
                    d[s] = [idx, []]
            else:
                self.state[name][sub] = [idx, []]
        o = dict(eng=eng, fn=fn, deps=deps, signal=False, dma_tag=dma_tag, wait_all=wait_all, val=None,
                 cost=float(cost), lat=float(lat), line=sys._getframe(2).f_lineno)
        need = []
        for d, kind in deps.items():
            p = self.ops[d]
            if p["dma_tag"] is None and dma_tag is None and p["eng"] == eng:
                if eng == "pe":
                    continue
                if kind == "order":
                    continue
            need.append(d)
            p["signal"] = True
        o["need"] = need
        if dma_tag is not None:
            o["signal"] = True
        self.ops.append(o)
        return idx


    def schedule(self, window=64, slack=0.0, use_cp=True):
        import bisect
        ops = self.ops
        n = len(ops)
        succ = [[] for _ in range(n)]
        ndeps = [0] * n
        for i, o in enumerate(ops):
            ndeps[i] = len(o["deps"])
            for d in o["deps"]:
                succ[d].append(i)
        ready = [0.0] * n
        blv = [0.0] * n
        for i in range(n - 1, -1, -1):
            m = 0.0
            for sidx in succ[i]:
                if blv[sidx] > m:
                    m = blv[sidx]
            blv[i] = m + ops[i]["cost"] + ops[i]["lat"] + 100.0
        prio = [(-blv[i], i) for i in range(n)] if use_cp else [(i, i) for i in range(n)]
        engs = sorted({o["eng"] for o in ops})
        free = {e: 0.0 for e in engs}
        rel = {e: [] for e in engs}
        for i, o in enumerate(ops):
            if ndeps[i] == 0:
                bisect.insort(rel[o["eng"]], (prio[i], i))
        order = []
        done = 0
        last_on = {}
        rdy_src = {}
        cur_tbl = [None]
        TBL = 1300.0

        def eff(e, t, i):
            st_ = max(t, ready[i])
            if e == "act":
                tb = ops[i].get("tbl")
                if tb is not None and tb != cur_tbl[0]:
                    st_ += TBL
            return st_
        while done < n:
            best = None
            for e in engs:
                cand = rel[e]
                if not cand:
                    continue
                t = free[e]
                lim = [c_[1] for c_ in cand[:window]]
                tmin = min(eff(e, t, i) for i in lim)
                for i in lim:
                    if eff(e, t, i) <= tmin + slack:
                        pick = i
                        break
                stt = eff(e, t, pick)
                if best is None or stt < best[0]:
                    best = (stt, e, pick)
            stt, e, i = best
            rel[e].remove((prio[i], i))
            o = ops[i]
            o["t0"] = stt
            o["bind"] = ("res", last_on.get(e)) if free[e] >= ready[i] else ("dep", rdy_src.get(i))
            last_on[e] = i
            if e == "act" and o.get("tbl") is not None:
                cur_tbl[0] = o["tbl"]
            fin = stt + o["cost"] + o["lat"]
            free[e] = stt + o["cost"]
            order.append(i)
            done += 1
            for sidx in succ[i]:
                so = ops[sidx]
                if o["dma_tag"] is not None or so["eng"] != e:
                    l = 180.0
                elif e == "pe":
                    l = 0.0
                else:
                    l = 60.0 if o["deps"] and ops[sidx]["deps"].get(i) in ("raw", "waw") else 0.0
                r = fin + l
                if r > ready[sidx]:
                    ready[sidx] = r
                    rdy_src[sidx] = i
                ndeps[sidx] -= 1
                if ndeps[sidx] == 0:
                    bisect.insort(rel[so["eng"]], (prio[sidx], sidx))
        self.est_ns = max(free.values())
        self._sched_dbg = (ops, order)
        remap = {old: new for new, old in enumerate(order)}
        new_ops = []
        for old in order:
            o = ops[old]
            o["deps"] = {remap[d]: k for d, k in o["deps"].items()}
            o["need"] = [remap[d] for d in o["need"]]
            new_ops.append(o)
        self.ops = new_ops

    def emit(self):
        nc = self.nc
        counters = {}
        for o in self.ops:
            if not o["signal"]:
                continue
            key = ("dma", o["dma_tag"]) if o["dma_tag"] is not None else ("eng", o["eng"])
            inc = 16 if o["dma_tag"] is not None else 1
            counters[key] = counters.get(key, 0) + inc
            o["key"] = key
            o["val"] = counters[key]
        totals = dict(counters)
        hw = {}

        def hwsem(key, k):
            if (key, k) not in hw:
                hw[(key, k)] = self.stack.enter_context(nc.semaphore(f"s_{key[0]}_{key[1]}_{k}"))
            return hw[(key, k)]

        waited = {}
        n_wait = 0
        for o in self.ops:
            e = self.eng[o["eng"]]
            tgt = {}
            for d in o["need"]:
                p = self.ops[d]
                v = totals[p["key"]] if p["wait_all"] else p["val"]
                tgt[p["key"]] = max(tgt.get(p["key"], 0), v)
            for key, v in tgt.items():
                if waited.get((o["eng"], key), 0) >= v:
                    continue
                waited[(o["eng"], key)] = v
                k = (v - 1) // SEM_CAP
                e.wait_ge(hwsem(key, k), v - k * SEM_CAP)
                n_wait += 1
            ins = o["fn"](e)
            if o["signal"]:
                v = o["val"]
                k = (v - 1) // SEM_CAP
                inc = 16 if o["dma_tag"] is not None else 1
                ins.then_inc(hwsem(o["key"], k), inc)
        self.n_wait = n_wait
        return totals, hwsem

    def finish(self, out_tags, eng="sp"):
        totals, hwsem = self._fin
        e = self.eng[eng]
        for t in out_tags:
            key = ("dma", t)
            if key in totals:
                v = totals[key]
                k = (v - 1) // SEM_CAP
                e.wait_ge(hwsem(key, k), v - k * SEM_CAP)

    def emit_all(self, out_tags, eng="sp", sched=True):
        if sched:
            self.schedule()
        self._fin = self.emit()
        self.finish(out_tags, eng)


def _nfree(ap):
    n = 1
    for d in list(ap.shape)[1:]:
        n *= int(d)
    return n


class Ops:
    def __init__(self, prog):
        self.p = prog

    def _ew(self, out, *ins):
        n = _nfree(out)
        ps = any((getattr(getattr(x, "tensor", None), "name", None) in self.p.psum_names) for x in ins if x is not None
                 and not isinstance(x, (int, float)))
        return (160.0 + 1.05 * n) if ps else (100.0 + 0.8 * n)

    @staticmethod
    def _aps(*xs):
        return [x for x in xs if x is not None and not isinstance(x, (int, float))]

    def mm(self, out, lhsT, rhs, start=True, stop=True, wk=None, rk=()):
        self.p.op("pe", lambda e: e.matmul(out, lhsT, rhs, start=start, stop=stop),
                  reads=[lhsT, rhs] if not rk else list(rk), writes=[wk if wk is not None else out],
                  cost=8.0 + 0.43 * max(_nfree(rhs), 64), lat=120.0)

    def tr(self, out, in_, ident, wk=None, rk=()):
        self.p.op("pe", lambda e: e.transpose(out, in_, ident),
                  reads=[in_, ident] if not rk else list(rk) + [ident], writes=[wk if wk is not None else out],
                  cost=8.0 + 0.43 * max(_nfree(in_), 64), lat=120.0)

    def act(self, out, in_, func, bias=None, scale=None, accum=None, eng="act", wk=None, rk=None):
        kw = {}
        if bias is not None:
            kw["bias"] = bias
        if scale is not None:
            kw["scale"] = scale
        if accum is not None:
            kw["accum_out"] = accum
        reads = self._aps(in_, bias, scale) if rk is None else list(rk) + self._aps(bias, scale)
        writes = [wk if wk is not None else out] + ([accum] if accum is not None else [])
        tbl = {AF.Exp: "ln_exp", AF.Ln: "ln_exp", AF.Gelu: "gelu", AF.Tanh: "gelu", AF.Silu: "silu", AF.Sigmoid: "sig",
               AF.Sqrt: "sqrt"}.get(func)
        i = self.p.op("act", lambda e: e.activation(out, in_, func, **kw), reads=reads, writes=writes,
                      cost=self._ew(out, in_) + (90.0 if accum is not None else 0.0))
        self.p.ops[i]["tbl"] = tbl

    def tt(self, out, a, b, op, eng="dve", wk=None, rk=None):
        self.p.op(eng, lambda e: e.tensor_tensor(out, a, b, op),
                  reads=[a, b] if rk is None else list(rk), writes=[wk if wk is not None else out],
                  cost=self._ew(out, a, b) * (2.0 if eng == "pool" else 1.0))

    def ts(self, out, a, s1, s2=None, op0=ALU.mult, op1=None, eng="dve", wk=None, rk=None, accum=None):
        kw = {}
        if op1 is not None:
            kw["op1"] = op1
        if accum is not None:
            kw["accum_out"] = accum
        reads = self._aps(a, s1, s2) if rk is None else list(rk) + self._aps(s1, s2)
        writes = [wk if wk is not None else out] + ([accum] if accum is not None else [])
        self.p.op(eng, lambda e: e.tensor_scalar(out, a, s1, s2, op0, **kw), reads=reads, writes=writes,
                  cost=self._ew(out, a) * (2.0 if eng == "pool" else 1.0))

    def stt(self, out, in0, scalar, in1, op0, op1, eng="dve", wk=None, rk=None):
        reads = self._aps(in0, scalar, in1) if rk is None else list(rk) + self._aps(scalar)
        self.p.op(eng, lambda e: e.scalar_tensor_tensor(out, in0, scalar, in1, op0, op1),
                  reads=reads, writes=[wk if wk is not None else out],
                  cost=self._ew(out, in0, in1) * (2.0 if eng == "pool" else 1.0))

    def copy(self, out, in_, eng="dve", wk=None, rk=None):
        if eng == "act":
            self.p.op("act", lambda e: e.copy(out, in_), reads=[in_] if rk is None else list(rk),
                      writes=[wk if wk is not None else out], cost=self._ew(out, in_))
        else:
            self.p.op(eng, lambda e: e.tensor_copy(out, in_), reads=[in_] if rk is None else list(rk),
                      writes=[wk if wk is not None else out], cost=self._ew(out, in_) * (2.0 if eng == "pool" else 1.0))

    def memset(self, out, val, eng="dve", wk=None):
        self.p.op(eng, lambda e: e.memset(out, val), reads=[], writes=[wk if wk is not None else out],
                  cost=60.0 + 0.5 * _nfree(out))

    def recip(self, out, in_, wk=None):
        self.p.op("dve", lambda e: e.reciprocal(out, in_), reads=[in_], writes=[wk if wk is not None else out],
                  cost=self._ew(out, in_) * 1.5)

    def scan(self, out, d0, d1, initial, op0, op1, wk=None, rk=None):
        reads = self._aps(d0, d1, initial) if rk is None else list(rk)
        self.p.op("dve", lambda e: e.tensor_tensor_scan(out, d0, d1, initial, op0, op1),
                  reads=reads, writes=[wk if wk is not None else out], cost=100.0 + 2.1 * _nfree(out))

    def bn_stats(self, out, in_, wk=None, rk=None):
        self.p.op("dve", lambda e: e.bn_stats(out, in_), reads=[in_] if rk is None else list(rk),
                  writes=[wk if wk is not None else out], cost=self._ew(in_, in_))

    def bn_aggr(self, out, in_, wk=None):
        self.p.op("dve", lambda e: e.bn_aggr(out, in_), reads=[in_], writes=[wk if wk is not None else out])

    def dma(self, out, in_, tag, q="sp", wk=None, rk=None, wait_all=False):
        nbytes = 128 * _nfree(out) * (2 if out.dtype == BF16 else 4)
        self.p.op(q, lambda e: e.dma_start(out=out, in_=in_), reads=[in_] if rk is None else list(rk),
                  writes=[wk if wk is not None else out], dma_tag=tag, wait_all=wait_all,
                  cost=70.0, lat=2000.0 + nbytes / 150.0)
import contextlib
import math

D = 2048
DA = 1024
NSH = 3200
DP = 7296
NORM_EPS = 1e-6
LN_EPS = 1e-5
GN_EPS = 64e-5
C0 = math.exp(-0.5)
NQ = 25
PV_MU, PV_W0, PV_A0, PV_KK, PV_KA, PV_RK, PV_GNG, PV_GNB, PV_SGUB = 0, 25, 33, 41, 49, 57, 65, 73, 81
PV_HM = 89
PV_N = 91


def build(NP, NS, SBW=256):
    nc = bass.Bass("TRN2", target_bir_lowering=False)
    st = contextlib.ExitStack()
    with st:
        P = Prog(nc, st)
        O = Ops(P)
        NST = NS * 64
        assert NP % SBW == 0
        din = lambda n, s, dt=F32: nc.dram_tensor(n, list(s), dt, kind="ExternalInput").ap()
        dout = lambda n, s, dt=F32: nc.dram_tensor(n, list(s), dt, kind="ExternalOutput").ap()
        xp_d = din("xp", [NP, D]); xs_d = din("xs", [NST, D])
        hs0_d = din("hs0", [NS, 8, 128, 128]); shT_d = din("shiftT", [128, NS, NQ])
        win_d = din("w_in", [D, DP]); wout_d = din("w_out", [D, D])
        normg_d = din("i_normg", [128, 16]); cmat_d = din("i_cmat", [128, 5, 128])
        lnbc_d = din("i_lnbc", [128, 2, DA]); fing_d = din("i_fing", [128, D]); pvec_d = din("i_pvec", [128, PV_N])
        wsT_d = din("i_wsT", [128, 8, 128]); w2a2_d = din("i_w2a2", [128, DA])
        yp_d = dout("yp", [NP, D]); ys_d = dout("ys", [NST, D])
        hp_d = dout("hp_out", [8, 128, 128]); hs_d = dout("hs_out", [NS, 8, 128, 128])
        shp_d = dout("shp_out", [128, NQ]); shs_d = dout("shs_out", [128, NS, NQ])
        vn_d = dout("vn_out", [NST, DA])
        WbA = nc.dram_tensor("WbA", [6, 128, 16, 512], BF16, kind="Internal").ap()
        WbB = nc.dram_tensor("WbB", [33, 128, 16, 128], BF16, kind="Internal").ap()
        WbO = nc.dram_tensor("WbO", [4, 128, 16, 512], BF16, kind="Internal").ap()

        NT = SBW // 128
        normg = P.sb("normg", [128, 16], F32)
        cmat = P.sb("cmat", [128, 5, 128], F32)
        cmb = P.sb("cmb", [128, 5, 128], BF16)
        lnbc = P.sb("lnbc", [128, 2, DA], F32)
        fing = P.sb("fing", [128, D], F32)
        pvec = P.sb("pvec", [128, PV_N], F32)
        omka = P.sb("omka", [128, 8], F32)
        hbias = P.sb("hbias", [128, 16], F32)
        mhalf = P.sb("mhalf", [128, SBW], F32)
        tg = P.sb("tg", [128, SBW], F32)
        wsTb = P.sb("wsTb", [128, 8, 128], BF16)
        w2a2b = P.sb("w2a2b", [128, DA], BF16)
        HF = P.sb("HF", [128, 8, 128], F32)
        HB = P.sb("HB", [128, 8, 128], BF16)
        HsF = P.sb("HsF", [128, 128], F32)
        HsB = P.sb("HsB", [128, 128], BF16)
        lastcol = P.sb("lastcol", [128, NQ], F32)
        shT = P.sb("shT", [128, NS, NQ], F32)
        shs = P.sb("shs", [128, NS, NQ], F32)
        xt = P.sb("xt", [128, 2 * NT, D], F32)
        junk = P.sb("junk", [128, D], BF16)
        xn = P.sb("xn", [128, D], BF16)
        hT = P.sb("hT", [128, 16, SBW], BF16)
        outT = P.sb("outT", [128, 16, SBW], BF16)
        wB = [P.sb(f"wB{i}", [128, 16, 128], BF16) for i in range(3)]
        wA = [P.sb(f"wA{i}", [128, 16, 512], BF16) for i in range(2)]
        gv = P.sb("gv", [128, NT, DA], F32)
        gus = P.sb("gus", [128, NT, DA], BF16)
        gat = P.sb("gat", [128, 512], F32)
        gat2 = P.sb("gat2", [128, 512], BF16)
        vnb = P.sb("vnb", [128, NT, DA], BF16)
        oa = P.sb("oa", [128, DA], BF16)
        st6 = P.sb("st6", [128, 2, 6], F32)
        mv = P.sb("mv", [128, 2, 2], F32)
        sm = P.sb("sm", [128, 8], F32)
        f32t = lambda n: P.sb(n, [128, SBW], F32)
        zraw = P.sb("zraw", [128, SBW + 1], F32)
        diff = f32t("diff")
        mixl, mixr, mixk, mixv = f32t("mixl"), f32t("mixr"), f32t("mixk"), f32t("mixv")
        sgb = [f32t(f"sgb{i}") for i in range(2)]; bv = [f32t(f"bv{i}") for i in range(2)]; ynT = [f32t(f"ynT{i}") for i in range(2)]
        sd, aa, cum, excl = f32t("sd"), f32t("aa"), f32t("cum"), f32t("excl")
        e_incl, e_excl, e_inv, e_rem = f32t("e_incl"), f32t("e_excl"), f32t("e_inv"), f32t("e_rem")
        kk, sq, rn, tmp, kp, bbv, rkv = (f32t(n) for n in ("kk", "sq", "rn", "tmp", "kp", "bbv", "rkv"))
        ones = f32t("ones")
        nb = P.sb("nb", [128, NT], F32)
        gC = [P.sb(f"gC{i}", [128, NT], F32) for i in range(2)]
        th = P.sb("th", [128, SBW], BF16)
        b16 = lambda n: P.sb(n, [128, SBW], BF16)
        kt, rt, khg, bhg, vbf = ([b16(f"{n}{i}") for i in range(2)] for n in ("kt", "rt", "khg", "bhg", "vbf"))
        kh = [[b16(f"kh{i}_{h}") for h in range(2)] for i in range(2)]
        bh = [[b16(f"bh{i}_{h}") for h in range(2)] for i in range(2)]
        tok3 = [P.sb(f"tok3_{i}", [128, 3, 128], BF16) for i in range(2)]
        Asb = [[P.sb(f"Asb{i}_{h}", [128, 3, 128], BF16) for h in range(2)] for i in range(2)]
        XMS = [[[P.sb(f"XMS{i}_{h}_{j}", [128, 384], BF16) for j in range(2)] for h in range(2)] for i in range(2)]
        TT = [[P.sb(f"TT{i}_{h}", [128, 128], BF16) for h in range(2)] for i in range(2)]
        st6a = P.sb("st6a", [128, 2, 6], F32)
        mva = P.sb("mva", [128, 2], F32)
        sma = P.sb("sma", [128, 2], F32)
        Wsb = P.sb("Wsb", [128, 128], BF16)
        nU = P.sb("nU", [128, 128], BF16)
        ynb = P.sb("ynb", [128, 128], BF16)
        rstd2 = P.sb("rstd2", [128, 2], F32)
        pzs = [P.ps(f"pz{i}", [128, 512], F32) for i in range(2)]
        pA = P.ps("pA", [128, 512], F32)
        pI = [P.ps(f"pI{h}", [128, 512], F32) for h in range(2)]
        pS = P.ps("pS", [128, 512], F32)
        pm = P.ps("pm", [128, 512], F32)
        ptr = P.ps("ptr", [128, 1024], BF16)

        ident_b = lambda n: cmb[:n, 0, :n]
        m_incl = lambda n: cmat[:n, 1, :n]
        m_strict = lambda n: cmat[:n, 2, :n]
        m_low = lambda n: cmat[:n, 3, :n]
        blockones = cmat[:, 4, :]
        pv = lambda base, j: pvec[:, base + j:base + j + 1]

        cl = lambda out, in_: O.dma(out, in_, "const", wait_all=True)
        cl(normg[:], normg_d); cl(cmat[:], cmat_d); cl(lnbc[:], lnbc_d); cl(fing[:], fing_d); cl(pvec[:], pvec_d)
        wsTf = gv[:, 0, :].rearrange("p (h i) -> p h i", i=128)
        w2a2f = gv[:, 1, :]
        O.dma(wsTf, wsT_d, "const", wait_all=True, wk=(gv, 0)); O.dma(w2a2f, w2a2_d, "const", wait_all=True, wk=(gv, 1)); cl(shT[:], shT_d)
        O.copy(cmb[:], cmat[:])
        O.copy(w2a2b[:], w2a2f, eng="act", rk=[(gv, 1)])
        O.ts(omka[:], pvec[:, PV_KA:PV_KA + 8], -1.0, 1.0, op0=ALU.mult, op1=ALU.add)
        O.ts(hbias[:], pvec[:, PV_W0:PV_W0 + 16], -1.0, None, op0=ALU.mult)
        O.memset(mhalf[:], -0.5)
        for h in range(8):
            O.tt(wsTb[:, h, :], wsTf[:, h, :], cmat[:, 1, :], ALU.mult, rk=[(gv, 0), cmat])
        O.memset(HF[:], 0.0); O.memset(HB[:], 0.0); O.memset(lastcol[:], 0.0); O.memset(ones[:], 1.0)
        O.memset(zraw[:, 0:1], 0.0)

        pieces = [(7168, 128), (3072, 2048), (5120, 2048), (0, 2048), (2048, 1024)]
        stg_f = [xt[:, 0, :], xt[:, 1, :]]
        stg_b = [junk, xn]
        k = 0
        for pi, (c0, w) in enumerate(pieces):
            for c in range(16):
                sf = stg_f[k % 2]; sbf = stg_b[k % 2]
                O.dma(sf[:, 0:w], win_d[c * 128:(c + 1) * 128, c0:c0 + w], f"stgf{k % 2}", wk=(xt, k % 2))
                if k % 2 == 0:
                    O.ts(sbf[:, 0:w], sf[:, 0:w], normg[:, c:c + 1], None, op0=ALU.mult, rk=[(xt, k % 2)])
                else:
                    O.act(sbf[:, 0:w], sf[:, 0:w], AF.Copy, scale=normg[:, c:c + 1], rk=[(xt, k % 2)])
                if c0 < 3072:
                    g0 = c0 // 512; ng = w // 512
                    for gg in range(ng):
                        O.dma(WbA[g0 + gg, :, c, :], sbf[:, gg * 512:(gg + 1) * 512], f"stgb{k % 2}_{gg}", wk=(WbA, (g0 + gg, c)))
                else:
                    q0 = (c0 - 3072) // 128; nq = w // 128
                    for qq in range(0, nq, 4):
                        nn = min(4, nq - qq)
                        O.dma(WbB[q0 + qq:q0 + qq + nn, :, c, :].rearrange("q p n -> p q n"),
                              sbf[:, qq * 128:(qq + nn) * 128].rearrange("p (q n) -> p q n", n=128), f"stgb{k % 2}_{qq // 4}",
                              wk=(WbB, ((q0 + qq) // 4, c)))
                k += 1
        for c in range(16):
            sf = stg_f[k % 2]; sbf = stg_b[k % 2]
            O.dma(sf[:, 0:2048], wout_d[c * 128:(c + 1) * 128, :], f"stgf{k % 2}", wk=(xt, k % 2))
            O.copy(sbf[:, 0:2048], sf[:, 0:2048], eng="act" if k % 2 else "dve", rk=[(xt, k % 2)])
            for gg in range(4):
                O.dma(WbO[gg, :, c, :], sbf[:, gg * 512:(gg + 1) * 512], f"stgb{k % 2}_{gg}", wk=(WbO, (gg, c)))
            k += 1

        wslotB = [0]
        wslotA = [0]

        def load_wB(q):
            s = wslotB[0] % 3; wslotB[0] += 1
            O.dma(wB[s][:], WbB[q], f"wB{s}", rk=[(WbB, (q // 4, c)) for c in range(16)])
            return wB[s]

        def load_wA(T, g):
            s = wslotA[0] % 2; wslotA[0] += 1
            O.dma(wA[s][:], T[g], f"wA{s}", rk=[(T, (g, c)) for c in range(16)])
            return wA[s]

        def rsqrt(out, in_, n, scale, eps):
            O.ts(out, in_, scale, eps, op0=ALU.mult, op1=ALU.add)
            O.act(out, out, AF.Ln)
            O.act(out, out, AF.Exp, scale=-0.5)

        def run_merged(gens):
            act = [[g, float(w), 0] for g, w in gens if g is not None]
            while act:
                a = min(act, key=lambda t: t[2] / t[1])
                try:
                    next(a[0]); a[2] += 1
                except StopIteration:
                    act.remove(a)

        def run_seq(g):
            for _ in g:
                pass

        def superblock(x_src, y_dst, tiles, W, sample, seq0=0, sbp=0):
            nt = len(tiles)
            xi = lambda ti: sbp * NT + ti

            def stage0():
                for ti, (off, Pt) in enumerate(tiles):
                    O.dma(xt[:Pt, xi(ti), :], x_src[off:off + Pt, :], f"x{xi(ti)}", wk=(xt, xi(ti)))
                    O.act(junk[:Pt, :], xt[:Pt, xi(ti), :], AF.Square, accum=sm[:Pt, 0:1], rk=[(xt, xi(ti))])
                    rsqrt(sm[:Pt, 2:3], sm[:Pt, 0:1], 1, 1.0 / D, NORM_EPS)
                    O.ts(xn[:Pt, :], xt[:Pt, xi(ti), :], sm[:Pt, 2:3], None, op0=ALU.mult, rk=[(xt, xi(ti))])
                    for c4 in range(4):
                        for j in range(4):
                            c = c4 * 4 + j
                            O.tr(ptr[:, j * Pt:(j + 1) * Pt], xn[:Pt, c * 128:(c + 1) * 128], ident_b(Pt))
                        src = ptr[:, 0:4 * Pt].rearrange("p (j t) -> p j t", t=Pt)
                        O.copy(hT[:, c4 * 4:c4 * 4 + 4, off:off + Pt], src, eng="act" if c4 % 2 else "dve", wk=(hT, ti))
                        yield

            def proj_B(q, half):
                w = load_wB(q)
                for c in range(16):
                    O.mm(pzs[half][:, 0:W], w[:, c, :], hT[:, c, 0:W], start=(c == 0), stop=(c == 15))
                return pzs[half][:, 0:W]

            def shifted(q, half, dst):
                src = proj_B(q, half)
                O.copy(zraw[:, 1:W + 1], src, eng="act")
                if not sample:
                    O.copy(zraw[:, 0:1], lastcol[:, q:q + 1])
                O.tt(diff[:, 0:W], zraw[:, 0:W], zraw[:, 1:W + 1], ALU.subtract)
                if sample:
                    for ti, (off, Pt) in enumerate(tiles):
                        O.tt(diff[:, off:off + 1], shT[:, seq0 + ti, q:q + 1], zraw[:, off + 1:off + 2], ALU.subtract)
                        O.copy(shs[:, seq0 + ti, q:q + 1], zraw[:, off + Pt:off + Pt + 1])
                else:
                    O.copy(lastcol[:, q:q + 1], zraw[:, W:W + 1])
                O.stt(dst[:, 0:W], diff[:, 0:W], pv(PV_MU, q), zraw[:, 1:W + 1], ALU.mult, ALU.add)

            def lora():
                shifted(24, 0, mixl)
                O.act(tg[0:64, 0:W], mixl[0:64, 0:W], AF.Exp, scale=-2.0)
                O.act(tg[0:64, 0:W], tg[0:64, 0:W], AF.Ln, bias=1.0)
                O.act(tg[0:64, 0:W], tg[0:64, 0:W], AF.Exp, scale=-1.0)
                O.ts(th[0:64, 0:W], tg[0:64, 0:W], 2.0, -1.0, op0=ALU.mult, op1=ALU.add)
                O.copy(th[64:128, 0:W], mixl[64:128, 0:W])
                yield

            def prep(p):
                s = p % 2
                shifted(p, 1, mixr); yield
                shifted(8 + p, 0, mixk); yield
                shifted(16 + p, 1, mixv); yield
                src = proj_B(25 + p, 0)
                O.act(tg[:, 0:W], src, AF.Exp, scale=-1.0)
                O.act(tg[:, 0:W], tg[:, 0:W], AF.Ln, bias=1.0)
                O.act(tg[:, 0:W], tg[:, 0:W], AF.Exp, scale=-1.0)
                O.tt(sgb[s][:, 0:W], tg[:, 0:W], src, ALU.mult); yield
                O.mm(pm[:, 0:W], w2a2b[0:64, p * 128:(p + 1) * 128], th[0:64, 0:W])
                O.act(sd[:, 0:W], pm[:, 0:W], AF.Exp, bias=hbias[:, p:p + 1], scale=-1.0)
                O.act(sd[:, 0:W], sd[:, 0:W], AF.Ln, bias=1.0)
                O.act(sd[:, 0:W], sd[:, 0:W], AF.Exp, scale=-1.0)
                O.mm(pm[:, 256:256 + W], w2a2b[64:128, p * 128:(p + 1) * 128], th[64:128, 0:W])
                O.act(aa[:, 0:W], pm[:, 256:256 + W], AF.Exp, bias=hbias[:, 8 + p:9 + p], scale=-1.0)
                O.act(aa[:, 0:W], aa[:, 0:W], AF.Ln, bias=1.0)
                O.act(aa[:, 0:W], aa[:, 0:W], AF.Exp, scale=-1.0); yield
                for ti, (off, Pt) in enumerate(tiles):
                    O.scan(cum[:, off:off + Pt], ones[:, off:off + Pt], sd[:, off:off + Pt], 0.0, ALU.mult, ALU.add)
                    O.ts(nb[:, ti:ti + 1], cum[:, off + Pt - 1:off + Pt], -C0, None, op0=ALU.mult)
                    O.act(gC[s][:, ti:ti + 1], cum[:, off + Pt - 1:off + Pt], AF.Exp, scale=-C0)
                    O.act(e_rem[:, off:off + Pt], cum[:, off:off + Pt], AF.Exp, scale=C0, bias=nb[:, ti:ti + 1])
                yield
                O.act(e_incl[:, 0:W], cum[:, 0:W], AF.Exp, scale=-C0)
                O.tt(excl[:, 0:W], cum[:, 0:W], sd[:, 0:W], ALU.subtract, eng="pool")
                O.act(e_excl[:, 0:W], excl[:, 0:W], AF.Exp, scale=-C0)
                O.act(e_inv[:, 0:W], cum[:, 0:W], AF.Exp, scale=C0); yield
                O.ts(kk[:, 0:W], mixk[:, 0:W], pv(PV_KK, p), None, op0=ALU.mult)
                O.tt(sq[:, 0:W], kk[:, 0:W], kk[:, 0:W], ALU.mult, eng="pool")
                O.mm(pm[:, 0:W], blockones, sq[:, 0:W])
                O.ts(rn[:, 0:W], pm[:, 0:W], 1e-24, None, op0=ALU.max)
                O.act(rn[:, 0:W], rn[:, 0:W], AF.Ln)
                O.act(rn[:, 0:W], rn[:, 0:W], AF.Exp, scale=-0.5); yield
                O.tt(kk[:, 0:W], kk[:, 0:W], rn[:, 0:W], ALU.mult)
                O.ts(tmp[:, 0:W], aa[:, 0:W], pv(PV_KA, p), omka[:, p:p + 1], op0=ALU.mult, op1=ALU.add, eng="pool")
                O.tt(kp[:, 0:W], mixk[:, 0:W], tmp[:, 0:W], ALU.mult)
                O.tt(bbv[:, 0:W], kk[:, 0:W], aa[:, 0:W], ALU.mult); yield
                O.tt(kt[s][:, 0:W], kk[:, 0:W], e_excl[:, 0:W], ALU.mult, eng="pool")
                O.tt(rt[s][:, 0:W], mixr[:, 0:W], e_incl[:, 0:W], ALU.mult)
                for hd in range(2):
                    O.stt(kh[s][hd][:, 0:W], kp[:, 0:W], pv(PV_HM, hd), e_inv[:, 0:W], ALU.mult, ALU.mult)
                yield
                for hd in range(2):
                    O.stt(bh[s][hd][:, 0:W], bbv[:, 0:W], pv(PV_HM, hd), e_inv[:, 0:W], ALU.mult, ALU.mult)
                O.tt(khg[s][:, 0:W], kp[:, 0:W], e_rem[:, 0:W], ALU.mult, eng="pool")
                O.tt(bhg[s][:, 0:W], bbv[:, 0:W], e_rem[:, 0:W], ALU.mult); yield
                O.copy(vbf[s][:, 0:W], mixv[:, 0:W], eng="act")
                O.stt(rkv[:, 0:W], mixr[:, 0:W], pv(PV_RK, p), kp[:, 0:W], ALU.mult, ALU.mult)
                O.mm(pm[:, 256:256 + W], blockones, rkv[:, 0:W])
                O.tt(bv[s][:, 0:W], pm[:, 256:256 + W], mixv[:, 0:W], ALU.mult); yield

            def inv_unit(u):
                p, ti = u // nt, u % nt
                s, par = p % 2, u % 2
                off, Pt = tiles[ti]
                sl = slice(off, off + Pt)
                w = Pt
                for j, srcb in enumerate((vbf[s], khg[s], bhg[s])):
                    O.tr(ptr[:Pt, j * 128:(j + 1) * 128], srcb[:, sl], ident_b(128))
                O.copy(tok3[par][:Pt, :, :], ptr[:Pt, 0:384].rearrange("p (j f) -> p j f", f=128), eng="act")
                yield
                for hd in range(2):
                    hs = slice(hd * 64, (hd + 1) * 64)
                    O.mm(pA[:Pt, 0:Pt], kh[s][hd][:, sl], kt[s][:, sl])
                    O.mm(pA[:Pt, 128:128 + Pt], kh[s][hd][:, sl], rt[s][:, sl])
                    O.mm(pA[:Pt, 256:256 + Pt], bh[s][hd][:, sl], kt[s][:, sl])
                    O.mm(pA[:Pt, 384:384 + Pt], bh[s][hd][:, sl], rt[s][:, sl])
                    O.mm(pI[hd][:Pt, 0:Pt], kt[s][:, sl], bh[s][hd][:, sl])
                    yield
                    X0 = XMS[par][hd][0]
                    O.tt(Asb[par][hd][:Pt, 0, :Pt], pA[:Pt, 0:Pt], m_strict(Pt), ALU.mult)
                    O.tt(Asb[par][hd][:Pt, 1, :Pt], pA[:Pt, 128:128 + Pt], m_incl(Pt), ALU.mult)
                    O.tt(Asb[par][hd][:Pt, 2, :Pt], pA[:Pt, 384:384 + Pt], m_incl(Pt), ALU.mult)
                    O.stt(X0[:Pt, w:2 * w], pA[:Pt, 256:256 + Pt], -1.0, m_strict(Pt), ALU.mult, ALU.mult, wk=(X0, "x"))
                    O.stt(X0[:Pt, 0:w], pI[hd][:Pt, 0:Pt], -1.0, m_low(Pt), ALU.mult, ALU.mult, wk=(X0, "m"))
                    yield
                nlev = int(math.log2(Pt))
                for k in range(nlev):
                    for hd in range(2):
                        cur = XMS[par][hd][k % 2]; nxt = XMS[par][hd][(k + 1) % 2]
                        Mk, Xk, Sk = cur[:Pt, 0:w], cur[:Pt, w:2 * w], cur[:Pt, 2 * w:3 * w]
                        if k == 0:
                            O.mm(pI[hd][:Pt, w:2 * w], Mk, Xk, rk=[(cur, "m"), (cur, "x")])
                            O.mm(pI[hd][:Pt, 0:w], Xk, Mk, rk=[(cur, "m"), (cur, "x")])
                            O.tt(nxt[:Pt, 2 * w:3 * w], Xk, cmat[:Pt, 0, :Pt], ALU.add, rk=[(cur, "x"), cmat], wk=(nxt, "s"))
                            O.copy(nxt[:Pt, 0:2 * w], pI[hd][:Pt, 0:2 * w], eng="act", wk=(nxt, "mx"))
                        elif k < nlev - 1:
                            O.mm(pI[hd][:Pt, w:3 * w], Mk, cur[:Pt, w:3 * w], rk=[(cur, "mx"), (cur, "s")])
                            O.mm(pI[hd][:Pt, 0:w], Xk, Mk, rk=[(cur, "mx")])
                            O.copy(nxt[:Pt, 0:2 * w], pI[hd][:Pt, 0:2 * w], eng="act", wk=(nxt, "mx"))
                            O.tt(nxt[:Pt, 2 * w:3 * w], pI[hd][:Pt, 2 * w:3 * w], Sk, ALU.add, rk=[pI[hd], (cur, "s")],
                                 wk=(nxt, "s"))
                        else:
                            O.mm(pI[hd][:Pt, 2 * w:3 * w], Mk, Sk, rk=[(cur, "mx"), (cur, "s")])
                            O.tt(TT[par][hd][:Pt, :Pt], pI[hd][:Pt, 2 * w:3 * w], Sk, ALU.add, rk=[pI[hd], (cur, "s")])
                        yield

            def state_unit(u):
                p, ti = u // nt, u % nt
                s, par = p % 2, u % 2
                off, Pt = tiles[ti]
                sl = slice(off, off + Pt)
                if sample:
                    O.dma(HsF[:], hs0_d[seq0 + ti, p], "hsin")
                    O.copy(HsB[:], HsF[:])
                    Hf, Hb = HsF, HsB
                    Hfv = lambda a, b: HsF[a, b]
                    Hbv = HsB[:, :]
                else:
                    Hf, Hb = HF, HB
                    Hfv = lambda a, b: HF[a, p, b]
                    Hbv = HB[:, p, :]
                hkey = (Hf, None if sample else p)
                hbkey = (Hb, None if sample else p)
                T3 = tok3[par]
                Vt = lambda hd: T3[:Pt, 0, hd * 64:(hd + 1) * 64]
                A_ = Asb[par]
                O.mm(pS[:Pt, 0:128], kt[s][:, sl], Hbv, start=True, stop=False, rk=[kt[s], hbkey])
                for hd in range(2):
                    O.mm(pS[:Pt, hd * 64:(hd + 1) * 64], A_[hd][:Pt, 0, :Pt], Vt(hd), start=False, stop=(hd == 1))
                O.copy(Wsb[:Pt, :], pS[:Pt, 0:128], eng="act")
                yield
                for hd in range(2):
                    O.mm(pS[:Pt, 128 + hd * 64:128 + (hd + 1) * 64], TT[par][hd][:Pt, :Pt], Wsb[:Pt, hd * 64:(hd + 1) * 64])
                O.act(nU[:Pt, :], pS[:Pt, 128:256], AF.Copy, scale=-1.0)
                yield
                O.mm(pS[:Pt, 256:384], rt[s][:, sl], Hbv, start=True, stop=False, rk=[rt[s], hbkey])
                for hd in range(2):
                    O.mm(pS[:Pt, 256 + hd * 64:256 + (hd + 1) * 64], A_[hd][:Pt, 1, :Pt], Vt(hd), start=False, stop=False)
                for hd in range(2):
                    O.mm(pS[:Pt, 256 + hd * 64:256 + (hd + 1) * 64], A_[hd][:Pt, 2, :Pt], nU[:Pt, hd * 64:(hd + 1) * 64],
                         start=False, stop=(hd == 1))
                O.mm(pS[:, 384:512], T3[:Pt, 1, :], T3[:Pt, 0, :], start=True, stop=False)
                O.mm(pS[:, 384:512], T3[:Pt, 2, :], nU[:Pt, :], start=False, stop=True)
                yield
                for hd in range(2):
                    hs = slice(hd * 64, (hd + 1) * 64)
                    O.stt(Hfv(hs, hs), Hfv(hs, hs), gC[s][hs, ti:ti + 1], pS[hs, 384 + hd * 64:384 + (hd + 1) * 64],
                          ALU.mult, ALU.add, rk=[hkey, pS, gC[s]], wk=hkey)
                if sample:
                    O.dma(hs_d[seq0 + ti, p], HsF[:], "hsout")
                else:
                    O.copy(Hbv, HF[:, p, :], eng="act", rk=[hkey], wk=hbkey)
                yield
                for hd in range(2):
                    O.bn_stats(st6[:Pt, hd, :], pS[:Pt, 256 + hd * 64:256 + (hd + 1) * 64])
                    O.bn_aggr(mv[:Pt, hd, :], st6[:Pt, hd, :])
                rsqrt(rstd2[:Pt, :], mv[:Pt, :, 1], 2, 1.0, GN_EPS)
                for hd in range(2):
                    O.ts(ynb[:Pt, hd * 64:(hd + 1) * 64], pS[:Pt, 256 + hd * 64:256 + (hd + 1) * 64], mv[:Pt, hd, 0:1],
                         rstd2[:Pt, hd:hd + 1], op0=ALU.subtract, op1=ALU.mult)
                yield
                O.tr(ptr[:, 512:512 + Pt], ynb[:Pt, :], ident_b(Pt))
                O.act(ynT[s][:, sl], ptr[:, 512:512 + Pt], AF.Identity, scale=pv(PV_GNG, p), bias=pv(PV_GNB, p))
                yield
                if ti == nt - 1:
                    O.tt(ynT[s][:, 0:W], ynT[s][:, 0:W], bv[s][:, 0:W], ALU.add)
                    O.tt(outT[:, 8 + p, 0:W], ynT[s][:, 0:W], sgb[s][:, 0:W], ALU.mult, wk=(outT, ("b", p)))
                    yield

            def a_branch():
                for gi, g in enumerate((2, 3, 0, 1, 4, 5)):
                    w = load_wA(WbA, g)
                    col = (g % 2) * 512
                    for ti, (off, Pt) in enumerate(tiles):
                        half = (gi * nt + ti) % 2
                        acc = pzs[half][:Pt, :]
                        for c in range(16):
                            O.mm(acc, hT[:, c, off:off + Pt], w[:, c, :], start=(c == 0), stop=(c == 15), rk=[(hT, ti), w])
                        if g in (2, 3):
                            O.act(gv[:Pt, ti, col:col + 512], acc, AF.Gelu, wk=(gv, ti))
                        elif g in (0, 1):
                            O.act(gus[:Pt, ti, col:col + 512], acc, AF.Gelu, wk=(gus, ti))
                        else:
                            O.act(gat[:Pt, :], acc, AF.Tanh, scale=0.5)
                            O.stt(gat2[:Pt, :], gat[:Pt, :], 1.0, acc, ALU.add, ALU.mult)
                            O.stt(gus[:Pt, ti, col:col + 512], gat2[:Pt, :], 0.5, gus[:Pt, ti, col:col + 512], ALU.mult,
                                  ALU.mult, rk=[(gus, ti), gat2], wk=(gus, ti))
                        yield
                    if g == 3:
                        for ti, (off, Pt) in enumerate(tiles):
                            for j in range(2):
                                O.bn_stats(st6a[:Pt, j, :], gv[:Pt, ti, j * 512:(j + 1) * 512], rk=[(gv, ti)])
                            O.bn_aggr(mva[:Pt, :], st6a[:Pt, :, :].rearrange("p a b -> p (a b)"))
                            rsqrt(sma[:Pt, 1:2], mva[:Pt, 1:2], 1, 1.0, LN_EPS)
                            O.ts(gv[:Pt, ti, :], gv[:Pt, ti, :], mva[:Pt, 0:1], sma[:Pt, 1:2], op0=ALU.subtract, op1=ALU.mult,
                                 rk=[(gv, ti)], wk=(gv, ti))
                            O.tt(gv[:Pt, ti, :], gv[:Pt, ti, :], lnbc[:Pt, 0, :], ALU.mult, rk=[(gv, ti), lnbc], wk=(gv, ti))
                            yield
                            if sample:
                                O.tt(gv[:Pt, ti, :], gv[:Pt, ti, :], lnbc[:Pt, 1, :], ALU.add, rk=[(gv, ti), lnbc], wk=(gv, ti))
                                O.dma(vn_d[seq0 * 64 + off:seq0 * 64 + off + Pt, :], gv[:Pt, ti, :], f"vn{ti}",
                                      rk=[(gv, ti)])
                                O.copy(vnb[:Pt, ti, :], gv[:Pt, ti, :], eng="act", rk=[(gv, ti)], wk=(vnb, ti))
                            else:
                                O.tt(vnb[:Pt, ti, :], gv[:Pt, ti, :], lnbc[:Pt, 1, :], ALU.add, rk=[(gv, ti), lnbc],
                                     wk=(vnb, ti))
                            yield
                for ti, (off, Pt) in enumerate(tiles):
                    for h in range(8):
                        O.mm(pzs[h // 4][:Pt, (h % 4) * 128:(h % 4 + 1) * 128], wsTb[:Pt, h, :Pt],
                             vnb[:Pt, ti, h * 128:(h + 1) * 128], rk=[wsTb, (vnb, ti)])
                    for h in range(8):
                        O.stt(oa[:Pt, h * 128:(h + 1) * 128], pzs[h // 4][:Pt, (h % 4) * 128:(h % 4 + 1) * 128],
                              pv(PV_SGUB, h)[:Pt, :], gus[:Pt, ti, h * 128:(h + 1) * 128], ALU.add, ALU.mult,
                              rk=[pzs[h // 4], (gus, ti), pvec])
                    yield
                    for c4 in range(2):
                        for j in range(4):
                            c = c4 * 4 + j
                            O.tr(ptr[:, j * Pt:(j + 1) * Pt], oa[:Pt, c * 128:(c + 1) * 128], ident_b(Pt))
                        src = ptr[:, 0:4 * Pt].rearrange("p (j t) -> p j t", t=Pt)
                        O.copy(outT[:, c4 * 4:c4 * 4 + 4, off:off + Pt], src, eng="act" if c4 % 2 else "dve",
                               wk=(outT, ("a", ti, c4)))
                        yield

            def out_proj():
                for g in range(4):
                    w = load_wA(WbO, g)
                    for ti, (off, Pt) in enumerate(tiles):
                        half = (g * nt + ti) % 2
                        acc = pzs[half][:Pt, :]
                        for c in range(16):
                            O.mm(acc, outT[:, c, off:off + Pt], w[:, c, :], start=(c == 0), stop=(c == 15), rk=[outT, w])
                        O.tt(xt[:Pt, xi(ti), g * 512:(g + 1) * 512], acc, xt[:Pt, xi(ti), g * 512:(g + 1) * 512], ALU.add,
                             rk=[pzs[half], (xt, xi(ti))], wk=(xt, xi(ti)))
                        yield
                for ti, (off, Pt) in enumerate(tiles):
                    O.act(junk[:Pt, :], xt[:Pt, xi(ti), :], AF.Square, accum=sm[:Pt, 5:6], rk=[(xt, xi(ti))])
                    rsqrt(sm[:Pt, 7:8], sm[:Pt, 5:6], 1, 1.0 / D, NORM_EPS)
                    O.stt(xt[:Pt, xi(ti), :], xt[:Pt, xi(ti), :], sm[:Pt, 7:8], fing[:Pt, :], ALU.mult, ALU.mult,
                          rk=[(xt, xi(ti)), fing], wk=(xt, xi(ti)))
                    O.dma(y_dst[off:off + Pt, :], xt[:Pt, xi(ti), :], f"y{xi(ti)}", rk=[(xt, xi(ti))])
                    yield

            run_seq(stage0())
            run_seq(lora())
            run_seq(prep(0))
            NU = 8 * nt
            ab = a_branch()
            n_inv = 3 + 2 * int(math.log2(tiles[0][1]))
            if nt == 2:
                run_merged([(inv_unit(0), n_inv), (ab_slice(ab, 4), 4)])
                for u in range(NU):
                    p = u // nt
                    gens = [(state_unit(u), 7)]
                    if u + 1 < NU:
                        gens.append((inv_unit(u + 1), n_inv))
                    if u % nt == 0 and p + 1 < 8:
                        gens.append((prep(p + 1), 13))
                    else:
                        gens.append((ab_slice(ab, 5), 5))
                    run_merged(gens)
            else:
                for u in range(NU):
                    if u > 0 and u % nt == 0:
                        run_seq(prep(u // nt))
                    run_merged([(inv_unit(u), n_inv), (ab_slice(ab, 4), 4)])
                    run_seq(state_unit(u))
            run_seq(ab)
            run_seq(out_proj())

        def ab_slice(g, n):
            for _ in range(n):
                try:
                    next(g)
                except StopIteration:
                    return
                yield

        for sbi in range(NP // SBW):
            superblock(xp_d[sbi * SBW:(sbi + 1) * SBW, :], yp_d[sbi * SBW:(sbi + 1) * SBW, :],
                       [(i * 128, 128) for i in range(SBW // 128)], SBW, False, 0, sbi % 2)
        O.dma(shp_d, lastcol[:], "misc_out")
        O.dma(hp_d.rearrange("q p f -> p q f"), HF[:], "misc_out")
        for s0 in range(0, NS, NT):
            n = min(NT, NS - s0)
            superblock(xs_d[s0 * 64:(s0 + n) * 64, :], ys_d[s0 * 64:(s0 + n) * 64, :], [(i * 64, 64) for i in range(n)], n * 64, True, s0,
                       (NP // SBW + s0 // NT) % 2)
        if NS > 0:
            O.dma(shs_d, shs[:], "misc_out")
        out_tags = ["misc_out", "hsout"] + [f"y{i}" for i in range(2 * NT)] + [f"vn{i}" for i in range(NT)]
        P.emit_all(out_tags, eng="sp")
        n_ops = len(P.ops)
    return nc, n_ops
N_CORES = 8
_NC_CACHE = {}


def _consts():
    i = np.arange(128)
    ident = np.eye(128, dtype=np.float32)
    incl = (i[:, None] <= i[None, :]).astype(np.float32)
    strict = (i[:, None] < i[None, :]).astype(np.float32)
    low = (i[:, None] > i[None, :]).astype(np.float32)
    blk = ((i[:, None] // 64) == (i[None, :] // 64)).astype(np.float32)
    return np.ascontiguousarray(np.stack([ident, incl, strict, low, blk], axis=1))


def _cols(v, n):
    return np.ascontiguousarray(np.asarray(v, np.float32).reshape(n, 128).T)


def _shared_inputs(norm_g, w_in, w_out, sgu_ln_g, sgu_ln_b, sgu_w, sgu_b, shift_mu, w0, w2, a0, a2, k_k, k_a, r_k,
                   gn_g, gn_b, final_g):
    pvec = np.concatenate([
        _cols(shift_mu[0], 25), _cols(w0[0], 8), _cols(a0[0], 8), _cols(k_k[0], 8), _cols(k_a[0], 8),
        _cols(np.asarray(r_k[0]).reshape(-1), 8), _cols(gn_g[0], 8), _cols(gn_b[0], 8),
        np.ascontiguousarray(np.asarray(sgu_b[0], np.float32).T),
        np.stack([(np.arange(128) < 64), (np.arange(128) >= 64)], axis=1).astype(np.float32)], axis=1)
    return {
        "w_in": np.ascontiguousarray(np.asarray(w_in[0], np.float32)),
        "w_out": np.ascontiguousarray(np.asarray(w_out[0], np.float32)),
        "i_normg": _cols(norm_g[0], 16),
        "i_cmat": _consts(),
        "i_lnbc": np.ascontiguousarray(np.broadcast_to(
            np.stack([np.asarray(sgu_ln_g[0], np.float32), np.asarray(sgu_ln_b[0], np.float32)])[None], (128, 2, DA))),
        "i_fing": np.ascontiguousarray(np.broadcast_to(np.asarray(final_g, np.float32)[None], (128, D))),
        "i_pvec": np.ascontiguousarray(pvec.astype(np.float32)),
        "i_wsT": np.ascontiguousarray(np.transpose(np.asarray(sgu_w[0], np.float32), (2, 0, 1))),
        "i_w2a2": np.ascontiguousarray(np.concatenate([np.asarray(w2[0], np.float32), np.asarray(a2[0], np.float32)], 0)),
    }


def _h_blockdiag(S):
    n = S.shape[0]
    out = np.zeros((n, 8, 128, 128), np.float32)
    Ht = np.transpose(S, (0, 1, 3, 2))
    out[:, :, 0:64, 0:64] = Ht[:, 0::2]
    out[:, :, 64:128, 64:128] = Ht[:, 1::2]
    return out


def _h_unblock(Hbd):
    n = Hbd.shape[0]
    S = np.zeros((n, 16, 64, 64), np.float32)
    S[:, 0::2] = np.transpose(Hbd[:, :, 0:64, 0:64], (0, 1, 3, 2))
    S[:, 1::2] = np.transpose(Hbd[:, :, 64:128, 64:128], (0, 1, 3, 2))
    return S


def run_layout(x_prompt, x_sample, state_b_wkv, state_b_shift, shared, n_cores, NP, NS):
    key = (NP, NS)
    if key not in _NC_CACHE:
        _NC_CACHE[key] = build(NP, NS)[0]
    nc = _NC_CACHE[key]
    B = x_prompt.shape[0]
    in_maps = []
    for c in range(n_cores):
        m = dict(shared)
        m["xp"] = np.ascontiguousarray(x_prompt[c], np.float32) if c < B else np.zeros((NP, D), np.float32)
        sl = slice(c * NS, (c + 1) * NS)
        m["xs"] = np.ascontiguousarray(np.asarray(x_sample[sl], np.float32).reshape(NS * 64, D))
        m["hs0"] = _h_blockdiag(np.asarray(state_b_wkv[0, sl], np.float32))
        sh = np.asarray(state_b_shift[0, sl, 0, :], np.float32).reshape(NS, 25, 128)
        m["shiftT"] = np.ascontiguousarray(np.transpose(sh, (2, 0, 1)))
        in_maps.append(m)
    res = run_bass_kernel_spmd(nc, in_maps, core_ids=list(range(n_cores)))
    R = res.results
    y_prompt = np.stack([R[c]["yp"] for c in range(B)]).astype(np.float32)
    y_sample = np.concatenate([R[c]["ys"].reshape(NS, 64, D) for c in range(n_cores)]).astype(np.float32)
    wkv_p = np.concatenate([_h_unblock(R[c]["hp_out"][None]) for c in range(B)])[None]
    shp = np.stack([R[c]["shp_out"].T.reshape(1, NSH) for c in range(B)])[None]
    wkv_s = np.concatenate([_h_unblock(R[c]["hs_out"]) for c in range(n_cores)])[None]
    shs = np.concatenate([np.transpose(R[c]["shs_out"], (1, 2, 0)).reshape(NS, 1, NSH) for c in range(n_cores)])[None]
    vn = np.concatenate([R[c]["vn_out"].reshape(NS, 64, DA) for c in range(n_cores)])[None]
    return (y_prompt, y_sample, wkv_p.astype(np.float32), shp.astype(np.float32), wkv_s.astype(np.float32),
            shs.astype(np.float32), vn.astype(np.float32))


def kernel(x_prompt, x_sample, state_b_wkv, state_b_shift, norm_g, w_in, w_out, sgu_ln_g, sgu_ln_b,
           sgu_w, sgu_b, shift_mu, w0, w2, a0, a2, k_k, k_a, r_k, gn_g, gn_b, final_g):
    shared = _shared_inputs(norm_g, w_in, w_out, sgu_ln_g, sgu_ln_b, sgu_w, sgu_b, shift_mu, w0, w2, a0, a2,
                            k_k, k_a, r_k, gn_g, gn_b, final_g)
    x_prompt = np.asarray(x_prompt); x_sample = np.asarray(x_sample)
    return run_layout(x_prompt, x_sample, np.asarray(state_b_wkv), np.asarray(state_b_shift), shared,
                      N_CORES, x_prompt.shape[1], x_sample.shape[0] // N_CORES)
```

```python
from concourse.bass_utils import run_bass_kernel_spmd
import sys
import numpy as np
import concourse.bass as bass
import concourse.mybir as mybir

F32 = mybir.dt.float32
BF16 = mybir.dt.bfloat16
AF = mybir.ActivationFunctionType
ALU = mybir.AluOpType

SCHED_SEED = None
SCHED_JITTER_NS = 400.0
SEM_CAP = 16384


def _key(x):
    sub = None
    if isinstance(x, tuple):
        x, sub = x
    t = getattr(x, "tensor", x)
    return (t.name, sub)


class Prog:
    COMPUTE = ("pe", "act", "dve", "pool")

    def __init__(self, nc, stack):
        self.nc = nc
        self.stack = stack
        self.ops = []
        self.state = {}
        self.eng = {"pe": nc.tensor, "act": nc.scalar, "dve": nc.vector, "pool": nc.gpsimd, "sp": nc.sync}
        self.psum_names = set()
        self.bank_last = {}

    def sb(self, name, shape, dt):
        return self.stack.enter_context(self.nc.sbuf_tensor(name, list(shape), dt))

    def ps(self, name, shape, dt):
        self.psum_names.add(name)
        return self.stack.enter_context(self.nc.psum_tensor(name, list(shape), dt))

    def _states(self, key, create=True):
        name, sub = key
        d = self.state.setdefault(name, {})
        if sub is None:
            if None not in d:
                d[None] = [None, []]
            return list(d.values())
        out = []
        if sub not in d:
            d[sub] = [None, []]
        out.append(d[sub])
        if None in d:
            out.append(d[None])
        return out

    def op(self, eng, fn, reads=(), writes=(), dma_tag=None, wait_all=False, cost=100.0, lat=0.0):
        idx = len(self.ops)
        deps = {}
        rk = [_key(r) for r in reads if r is not None and not isinstance(r, (int, float))]
        wk = [_key(w) for w in writes if w is not None]
        for k in rk:
            for st in self._states(k):
                if st[0] is not None:
                    deps.setdefault(st[0], "raw")
        for k in wk:
            for st in self._states(k):
                if st[0] is not None:
                    deps.setdefault(st[0], "waw")
                for r in st[1]:
                    if r != idx:
                        deps.setdefault(r, "war")
        for name in {k[0] for k in rk + wk if k[0] in self.psum_names}:
            bl = self.bank_last.setdefault(name, {})
            for e2, i2 in bl.items():
                if e2 != eng:
                    deps.setdefault(i2, "bank")
                else:
                    deps.setdefault(i2, "order")
            bl[eng] = idx
        for k in rk:
            name, sub = k
            if sub is None:
                for st in self._states(k):
                    st[1].append(idx)
            else:
                self.state[name][sub][1].append(idx)
        for k in wk:
            name, sub = k
            if sub is None:
                d = self.state[name]
                for s in list(d.keys()):
                    d[s] = [idx, []]
            else:
                self.state[name][sub] = [idx, []]
        o = dict(eng=eng, fn=fn, deps=deps, signal=False, dma_tag=dma_tag, wait_all=wait_all, val=None,
                 cost=float(cost), lat=float(lat), line=sys._getframe(2).f_lineno)
        need = []
        for d, kind in deps.items():
            p = self.ops[d]
            if p["dma_tag"] is None and dma_tag is None and p["eng"] == eng:
                if eng == "pe":
                    continue
                if kind == "order":
                    continue
            need.append(d)
            p["signal"] = True
        o["need"] = need
        if dma_tag is not None:
            o["signal"] = True
        self.ops.append(o)
        return idx


    def schedule(self, window=64, slack=0.0, use_cp=True):
        import bisect
        ops = self.ops
        n = len(ops)
        succ = [[] for _ in range(n)]
        ndeps = [0] * n
        for i, o in enumerate(ops):
            ndeps[i] = len(o["deps"])
            for d in o["deps"]:
                succ[d].append(i)
        ready = [0.0] * n
        blv = [0.0] * n
        for i in range(n - 1, -1, -1):
            m = 0.0
            for sidx in succ[i]:
                if blv[sidx] > m:
                    m = blv[sidx]
            blv[i] = m + ops[i]["cost"] + ops[i]["lat"] + 100.0
        if SCHED_SEED is not None:
            rs = np.random.RandomState(SCHED_SEED)
            jit = rs.standard_normal(n) * SCHED_JITTER_NS
            prio = [(-(blv[i] + jit[i]), i) for i in range(n)]
        else:
            prio = [(-blv[i], i) for i in range(n)] if use_cp else [(i, i) for i in range(n)]
        engs = sorted({o["eng"] for o in ops})
        free = {e: 0.0 for e in engs}
        rel = {e: [] for e in engs}
        for i, o in enumerate(ops):
            if ndeps[i] == 0:
                bisect.insort(rel[o["eng"]], (prio[i], i))
        order = []
        done = 0
        last_on = {}
        rdy_src = {}
        cur_tbl = [None]
        TBL = 1300.0

        def eff(e, t, i):
            st_ = max(t, ready[i])
            if e == "act":
                tb = ops[i].get("tbl")
                if tb is not None and tb != cur_tbl[0]:
                    st_ += TBL
            return st_
        while done < n:
            best = None
            for e in engs:
                cand = rel[e]
                if not cand:
                    continue
                t = free[e]
                lim = [c_[1] for c_ in cand[:window]]
                tmin = min(eff(e, t, i) for i in lim)
                for i in lim:
                    if eff(e, t, i) <= tmin + slack:
                        pick = i
                        break
                stt = eff(e, t, pick)
                if best is None or stt < best[0]:
                    best = (stt, e, pick)
            stt, e, i = best
            rel[e].remove((prio[i], i))
            o = ops[i]
            o["t0"] = stt
            o["bind"] = ("res", last_on.get(e)) if free[e] >= ready[i] else ("dep", rdy_src.get(i))
            last_on[e] = i
            if e == "act" and o.get("tbl") is not None:
                cur_tbl[0] = o["tbl"]
            fin = stt + o["cost"] + o["lat"]
            free[e] = stt + o["cost"]
            order.append(i)
            done += 1
            for sidx in succ[i]:
                so = ops[sidx]
                if o["dma_tag"] is not None or so["eng"] != e:
                    l = 180.0
                elif e == "pe":
                    l = 0.0
                else:
                    l = 60.0 if o["deps"] and ops[sidx]["deps"].get(i) in ("raw", "waw") else 0.0
                r = fin + l
                if r > ready[sidx]:
                    ready[sidx] = r
                    rdy_src[sidx] = i
                ndeps[sidx] -= 1
                if ndeps[sidx] == 0:
                    bisect.insort(rel[so["eng"]], (prio[sidx], sidx))
        self.est_ns = max(free.values())
        self._sched_dbg = (ops, order)
        remap = {old: new for new, old in enumerate(order)}
        new_ops = []
        for old in order:
            o = ops[old]
            o["deps"] = {remap[d]: k for d, k in o["deps"].items()}
            o["need"] = [remap[d] for d in o["need"]]
            new_ops.append(o)
        self.ops = new_ops

    def emit(self):
        nc = self.nc
        ops = self.ops
        seqc = {}
        for o in ops:
            key = ("dma", o["dma_tag"]) if o["dma_tag"] is not None else ("eng", o["eng"])
            seqc[key] = seqc.get(key, 0) + 1
            o["key"] = key
            o["seq"] = seqc[key]
        needed = set()
        for o in ops:
            tgt = {}
            for d in o["need"]:
                p = ops[d]
                k = p["key"]
                if k not in tgt or p["seq"] > ops[tgt[k]]["seq"]:
                    tgt[k] = d
            o["tgt"] = tgt
            needed.update(tgt.values())
        counters = {}
        for i, o in enumerate(ops):
            o["signal"] = (o["dma_tag"] is not None) or (i in needed)
            if not o["signal"]:
                continue
            inc = 16 if o["dma_tag"] is not None else 1
            counters[o["key"]] = counters.get(o["key"], 0) + inc
            o["val"] = counters[o["key"]]
        totals = dict(counters)
        hw = {}

        def hwsem(key, k):
            if (key, k) not in hw:
                hw[(key, k)] = self.stack.enter_context(nc.semaphore(f"s_{key[0]}_{key[1]}_{k}"))
            return hw[(key, k)]

        waited = {}
        n_wait = 0
        for o in ops:
            e = self.eng[o["eng"]]
            for key, d in o["tgt"].items():
                p = ops[d]
                v = totals[key] if p["wait_all"] else p["val"]
                if waited.get((o["eng"], key), 0) >= v:
                    continue
                waited[(o["eng"], key)] = v
                k = (v - 1) // SEM_CAP
                e.wait_ge(hwsem(key, k), v - k * SEM_CAP)
                n_wait += 1
            ins = o["fn"](e)
            if o["signal"]:
                v = o["val"]
                k = (v - 1) // SEM_CAP
                inc = 16 if o["dma_tag"] is not None else 1
                ins.then_inc(hwsem(o["key"], k), inc)
        self.n_wait = n_wait
        self.n_signal = sum(1 for o in ops if o["signal"])
        return totals, hwsem

    def finish(self, out_tags, eng="sp"):
        totals, hwsem = self._fin
        e = self.eng[eng]
        for t in out_tags:
            key = ("dma", t)
            if key in totals:
                v = totals[key]
                k = (v - 1) // SEM_CAP
                e.wait_ge(hwsem(key, k), v - k * SEM_CAP)

    def emit_all(self, out_tags, eng="sp", sched=True):
        if sched:
            self.schedule()
        self._fin = self.emit()
        self.finish(out_tags, eng)


def _nfree(ap):
    n = 1
    for d in list(ap.shape)[1:]:
        n *= int(d)
    return n


class Ops:
    def __init__(self, prog):
        self.p = prog

    def _ew(self, out, *ins):
        n = _nfree(out)
        ps = any((getattr(getattr(x, "tensor", None), "name", None) in self.p.psum_names) for x in ins if x is not None
                 and not isinstance(x, (int, float)))
        return (160.0 + 1.05 * n) if ps else (100.0 + 0.8 * n)

    @staticmethod
    def _aps(*xs):
        return [x for x in xs if x is not None and not isinstance(x, (int, float))]

    def mm(self, out, lhsT, rhs, start=True, stop=True, wk=None, rk=()):
        self.p.op("pe", lambda e: e.matmul(out, lhsT, rhs, start=start, stop=stop),
                  reads=[lhsT, rhs] if not rk else list(rk), writes=[wk if wk is not None else out],
                  cost=8.0 + 0.43 * max(_nfree(rhs), 64), lat=120.0)

    def tr(self, out, in_, ident, wk=None, rk=()):
        self.p.op("pe", lambda e: e.transpose(out, in_, ident),
                  reads=[in_, ident] if not rk else list(rk) + [ident], writes=[wk if wk is not None else out],
                  cost=8.0 + 0.43 * max(_nfree(in_), 64), lat=120.0)

    def act(self, out, in_, func, bias=None, scale=None, accum=None, eng="act", wk=None, rk=None):
        kw = {}
        if bias is not None:
            kw["bias"] = bias
        if scale is not None:
            kw["scale"] = scale
        if accum is not None:
            kw["accum_out"] = accum
        reads = self._aps(in_, bias, scale) if rk is None else list(rk) + self._aps(bias, scale)
        writes = [wk if wk is not None else out] + ([accum] if accum is not None else [])
        tbl = {AF.Exp: "ln_exp", AF.Ln: "ln_exp", AF.Gelu: "gelu", AF.Tanh: "gelu", AF.Silu: "silu", AF.Sigmoid: "sig",
               AF.Sqrt: "sqrt"}.get(func)
        i = self.p.op("act", lambda e: e.activation(out, in_, func, **kw), reads=reads, writes=writes,
                      cost=self._ew(out, in_) + (90.0 if accum is not None else 0.0))
        self.p.ops[i]["tbl"] = tbl

    def tt(self, out, a, b, op, eng="dve", wk=None, rk=None):
        self.p.op(eng, lambda e: e.tensor_tensor(out, a, b, op),
                  reads=[a, b] if rk is None else list(rk), writes=[wk if wk is not None else out],
                  cost=self._ew(out, a, b) * (2.0 if eng == "pool" else 1.0))

    def ts(self, out, a, s1, s2=None, op0=ALU.mult, op1=None, eng="dve", wk=None, rk=None, accum=None):
        kw = {}
        if op1 is not None:
            kw["op1"] = op1
        if accum is not None:
            kw["accum_out"] = accum
        reads = self._aps(a, s1, s2) if rk is None else list(rk) + self._aps(s1, s2)
        writes = [wk if wk is not None else out] + ([accum] if accum is not None else [])
        self.p.op(eng, lambda e: e.tensor_scalar(out, a, s1, s2, op0, **kw), reads=reads, writes=writes,
                  cost=self._ew(out, a) * (2.0 if eng == "pool" else 1.0))

    def stt(self, out, in0, scalar, in1, op0, op1, eng="dve", wk=None, rk=None):
        reads = self._aps(in0, scalar, in1) if rk is None else list(rk) + self._aps(scalar)
        self.p.op(eng, lambda e: e.scalar_tensor_tensor(out, in0, scalar, in1, op0, op1),
                  reads=reads, writes=[wk if wk is not None else out],
                  cost=self._ew(out, in0, in1) * (2.0 if eng == "pool" else 1.0))

    def copy(self, out, in_, eng="dve", wk=None, rk=None):
        if eng == "act":
            self.p.op("act", lambda e: e.copy(out, in_), reads=[in_] if rk is None else list(rk),
                      writes=[wk if wk is not None else out], cost=self._ew(out, in_))
        else:
            self.p.op(eng, lambda e: e.tensor_copy(out, in_), reads=[in_] if rk is None else list(rk),
                      writes=[wk if wk is not None else out], cost=self._ew(out, in_) * (2.0 if eng == "pool" else 1.0))

    def memset(self, out, val, eng="dve", wk=None):
        self.p.op(eng, lambda e: e.memset(out, val), reads=[], writes=[wk if wk is not None else out],
                  cost=60.0 + 0.5 * _nfree(out))

    def recip(self, out, in_, wk=None):
        self.p.op("dve", lambda e: e.reciprocal(out, in_), reads=[in_], writes=[wk if wk is not None else out],
                  cost=self._ew(out, in_) * 1.5)

    def scan(self, out, d0, d1, initial, op0, op1, wk=None, rk=None):
        reads = self._aps(d0, d1, initial) if rk is None else list(rk)
        self.p.op("dve", lambda e: e.tensor_tensor_scan(out, d0, d1, initial, op0, op1),
                  reads=reads, writes=[wk if wk is not None else out], cost=100.0 + 2.1 * _nfree(out))

    def bn_stats(self, out, in_, wk=None, rk=None):
        self.p.op("dve", lambda e: e.bn_stats(out, in_), reads=[in_] if rk is None else list(rk),
                  writes=[wk if wk is not None else out], cost=self._ew(in_, in_))

    def bn_aggr(self, out, in_, wk=None):
        self.p.op("dve", lambda e: e.bn_aggr(out, in_), reads=[in_], writes=[wk if wk is not None else out])

    def dma(self, out, in_, tag, q="sp", wk=None, rk=None, wait_all=False):
        nbytes = 128 * _nfree(out) * (2 if out.dtype == BF16 else 4)
        self.p.op(q, lambda e: e.dma_start(out=out, in_=in_), reads=[in_] if rk is None else list(rk),
                  writes=[wk if wk is not None else out], dma_tag=tag, wait_all=wait_all,
                  cost=70.0, lat=2000.0 + nbytes / 150.0)
import contextlib
import math

D = 2048
DA = 1024
NSH = 3200
DP = 7296
NORM_EPS = 1e-6
LN_EPS = 1e-5
GN_EPS = 64e-5
C0 = math.exp(-0.5)
NQ = 25
PV_MU, PV_W0, PV_A0, PV_KK, PV_KA, PV_RK, PV_GNG, PV_GNB, PV_SGUB = 0, 25, 33, 41, 49, 57, 65, 73, 81
PV_HM = 89
PV_N = 91


def build(NP, NS, SBW=256):
    nc = bass.Bass("TRN2", target_bir_lowering=False)
    st = contextlib.ExitStack()
    with st:
        P = Prog(nc, st)
        O = Ops(P)
        NST = NS * 64
        assert NP % SBW == 0
        din = lambda n, s, dt=F32: nc.dram_tensor(n, list(s), dt, kind="ExternalInput").ap()
        dout = lambda n, s, dt=F32: nc.dram_tensor(n, list(s), dt, kind="ExternalOutput").ap()
        xp_d = din("xp", [NP, D]); xs_d = din("xs", [NST, D])
        hs0_d = din("hs0", [NS, 8, 128, 128]); shT_d = din("shiftT", [128, NS, NQ])
        win_d = din("w_in", [D, DP]); wout_d = din("w_out", [D, D])
        normg_d = din("i_normg", [128, 16]); cmat_d = din("i_cmat", [128, 5, 128])
        lnbc_d = din("i_lnbc", [128, 2, DA]); fing_d = din("i_fing", [128, D]); pvec_d = din("i_pvec", [128, PV_N])
        wsT_d = din("i_wsT", [128, 8, 128]); w2a2_d = din("i_w2a2", [128, DA])
        yp_d = dout("yp", [NP, D]); ys_d = dout("ys", [NST, D])
        hp_d = dout("hp_out", [8, 128, 128]); hs_d = dout("hs_out", [NS, 8, 128, 128])
        shp_d = dout("shp_out", [128, NQ]); shs_d = dout("shs_out", [128, NS, NQ])
        vn_d = dout("vn_out", [NST, DA])
        WbA = nc.dram_tensor("WbA", [6, 128, 16, 512], BF16, kind="Internal").ap()
        WbB = nc.dram_tensor("WbB", [33, 128, 16, 128], BF16, kind="Internal").ap()
        WbO = nc.dram_tensor("WbO", [4, 128, 16, 512], BF16, kind="Internal").ap()

        NT = SBW // 128
        normg = P.sb("normg", [128, 16], F32)
        cmat = P.sb("cmat", [128, 5, 128], F32)
        cmb = P.sb("cmb", [128, 5, 128], BF16)
        lnbc = P.sb("lnbc", [128, 2, DA], F32)
        fing = P.sb("fing", [128, D], F32)
        pvec = P.sb("pvec", [128, PV_N], F32)
        omka = P.sb("omka", [128, 8], F32)
        hbias = P.sb("hbias", [128, 16], F32)
        mhalf = P.sb("mhalf", [128, SBW], F32)
        tg = P.sb("tg", [128, SBW], F32)
        wsTb = P.sb("wsTb", [128, 8, 128], BF16)
        w2a2b = P.sb("w2a2b", [128, DA], BF16)
        HF = P.sb("HF", [128, 8, 128], F32)
        HB = P.sb("HB", [128, 8, 128], BF16)
        HsF = P.sb("HsF", [128, 128], F32)
        HsB = P.sb("HsB", [128, 128], BF16)
        lastcol = P.sb("lastcol", [128, NQ], F32)
        shT = P.sb("shT", [128, NS, NQ], F32)
        shs = P.sb("shs", [128, NS, NQ], F32)
        xt = P.sb("xt", [128, 2 * NT, D], F32)
        junk = P.sb("junk", [128, D], BF16)
        xn = P.sb("xn", [128, D], BF16)
        hT = P.sb("hT", [128, 16, SBW], BF16)
        outT = P.sb("outT", [128, 16, SBW], BF16)
        wB = [P.sb(f"wB{i}", [128, 16, 128], BF16) for i in range(3)]
        wA = [P.sb(f"wA{i}", [128, 16, 512], BF16) for i in range(2)]
        gv = P.sb("gv", [128, NT, DA], F32)
        gus = P.sb("gus", [128, NT, DA], BF16)
        gat = P.sb("gat", [128, 512], F32)
        gat2 = P.sb("gat2", [128, 512], BF16)
        vnb = P.sb("vnb", [128, NT, DA], BF16)
        oa = P.sb("oa", [128, DA], BF16)
        st6 = P.sb("st6", [128, 2, 6], F32)
        mv = P.sb("mv", [128, 2, 2], F32)
        sm = P.sb("sm", [128, 8], F32)
        f32t = lambda n: P.sb(n, [128, SBW], F32)
        zraw = P.sb("zraw", [128, SBW + 1], F32)
        diff = f32t("diff")
        mixl, mixr, mixk, mixv = f32t("mixl"), f32t("mixr"), f32t("mixk"), f32t("mixv")
        sgb = [f32t(f"sgb{i}") for i in range(2)]; bv = [f32t(f"bv{i}") for i in range(2)]; ynT = [f32t(f"ynT{i}") for i in range(2)]
        sd, aa, cum, excl = f32t("sd"), f32t("aa"), f32t("cum"), f32t("excl")
        e_incl, e_excl, e_inv, e_rem = f32t("e_incl"), f32t("e_excl"), f32t("e_inv"), f32t("e_rem")
        kk, sq, rn, tmp, kp, bbv, rkv = (f32t(n) for n in ("kk", "sq", "rn", "tmp", "kp", "bbv", "rkv"))
        ones = f32t("ones")
        nb = P.sb("nb", [128, NT], F32)
        gC = [P.sb(f"gC{i}", [128, NT], F32) for i in range(2)]
        th = P.sb("th", [128, SBW], BF16)
        b16 = lambda n: P.sb(n, [128, SBW], BF16)
        kt, rt, khg, bhg, vbf = ([b16(f"{n}{i}") for i in range(2)] for n in ("kt", "rt", "khg", "bhg", "vbf"))
        kh = [[b16(f"kh{i}_{h}") for h in range(2)] for i in range(2)]
        bh = [[b16(f"bh{i}_{h}") for h in range(2)] for i in range(2)]
        tok3 = [P.sb(f"tok3_{i}", [128, 3, 128], BF16) for i in range(2)]
        Asb = [[P.sb(f"Asb{i}_{h}", [128, 3, 128], BF16) for h in range(2)] for i in range(2)]
        XMS = [[[P.sb(f"XMS{i}_{h}_{j}", [128, 384], BF16) for j in range(2)] for h in range(2)] for i in range(2)]
        TT = [[P.sb(f"TT{i}_{h}", [128, 128], BF16) for h in range(2)] for i in range(2)]
        st6a = P.sb("st6a", [128, 2, 6], F32)
        mva = P.sb("mva", [128, 2], F32)
        sma = P.sb("sma", [128, 2], F32)
        Wsb = P.sb("Wsb", [128, 128], BF16)
        nU = P.sb("nU", [128, 128], BF16)
        ynb = P.sb("ynb", [128, 128], BF16)
        rstd2 = P.sb("rstd2", [128, 2], F32)
        pzs = [P.ps(f"pz{i}", [128, 512], F32) for i in range(2)]
        pA = P.ps("pA", [128, 512], F32)
        pI = [P.ps(f"pI{h}", [128, 512], F32) for h in range(2)]
        pS = P.ps("pS", [128, 512], F32)
        pm = P.ps("pm", [128, 512], F32)
        ptr = P.ps("ptr", [128, 1024], BF16)

        ident_b = lambda n: cmb[:n, 0, :n]
        m_incl = lambda n: cmat[:n, 1, :n]
        m_strict = lambda n: cmat[:n, 2, :n]
        m_low = lambda n: cmat[:n, 3, :n]
        blockones = cmat[:, 4, :]
        pv = lambda base, j: pvec[:, base + j:base + j + 1]

        cl = lambda out, in_: O.dma(out, in_, "const", wait_all=True)
        cl(normg[:], normg_d); cl(cmat[:], cmat_d); cl(lnbc[:], lnbc_d); cl(fing[:], fing_d); cl(pvec[:], pvec_d)
        wsTf = gv[:, 0, :].rearrange("p (h i) -> p h i", i=128)
        w2a2f = gv[:, 1, :]
        O.dma(wsTf, wsT_d, "const", wait_all=True, wk=(gv, 0)); O.dma(w2a2f, w2a2_d, "const", wait_all=True, wk=(gv, 1)); cl(shT[:], shT_d)
        O.copy(cmb[:], cmat[:])
        O.copy(w2a2b[:], w2a2f, eng="act", rk=[(gv, 1)])
        O.ts(omka[:], pvec[:, PV_KA:PV_KA + 8], -1.0, 1.0, op0=ALU.mult, op1=ALU.add)
        O.ts(hbias[:], pvec[:, PV_W0:PV_W0 + 16], -1.0, None, op0=ALU.mult)
        O.memset(mhalf[:], -0.5)
        for h in range(8):
            O.tt(wsTb[:, h, :], wsTf[:, h, :], cmat[:, 1, :], ALU.mult, rk=[(gv, 0), cmat])
        O.memset(HF[:], 0.0); O.memset(HB[:], 0.0); O.memset(lastcol[:], 0.0); O.memset(ones[:], 1.0)
        O.memset(zraw[:, 0:1], 0.0)

        pieces = [(7168, 128), (3072, 2048), (5120, 2048), (0, 2048), (2048, 1024)]
        stg_f = [xt[:, 0, :], xt[:, 1, :]]
        stg_b = [junk, xn]
        k = 0
        for pi, (c0, w) in enumerate(pieces):
            for c in range(16):
                sf = stg_f[k % 2]; sbf = stg_b[k % 2]
                O.dma(sf[:, 0:w], win_d[c * 128:(c + 1) * 128, c0:c0 + w], f"stgf{k % 2}", wk=(xt, k % 2))
                if k % 2 == 0:
                    O.ts(sbf[:, 0:w], sf[:, 0:w], normg[:, c:c + 1], None, op0=ALU.mult, rk=[(xt, k % 2)])
                else:
                    O.act(sbf[:, 0:w], sf[:, 0:w], AF.Copy, scale=normg[:, c:c + 1], rk=[(xt, k % 2)])
                if c0 < 3072:
                    g0 = c0 // 512; ng = w // 512
                    for gg in range(ng):
                        O.dma(WbA[g0 + gg, :, c, :], sbf[:, gg * 512:(gg + 1) * 512], f"stgb{k % 2}_{gg}", wk=(WbA, (g0 + gg, c)))
                else:
                    q0 = (c0 - 3072) // 128; nq = w // 128
                    for qq in range(0, nq, 4):
                        nn = min(4, nq - qq)
                        O.dma(WbB[q0 + qq:q0 + qq + nn, :, c, :].rearrange("q p n -> p q n"),
                              sbf[:, qq * 128:(qq + nn) * 128].rearrange("p (q n) -> p q n", n=128), f"stgb{k % 2}_{qq // 4}",
                              wk=(WbB, ((q0 + qq) // 4, c)))
                k += 1
        for c in range(16):
            sf = stg_f[k % 2]; sbf = stg_b[k % 2]
            O.dma(sf[:, 0:2048], wout_d[c * 128:(c + 1) * 128, :], f"stgf{k % 2}", wk=(xt, k % 2))
            O.copy(sbf[:, 0:2048], sf[:, 0:2048], eng="act" if k % 2 else "dve", rk=[(xt, k % 2)])
            for gg in range(4):
                O.dma(WbO[gg, :, c, :], sbf[:, gg * 512:(gg + 1) * 512], f"stgb{k % 2}_{gg}", wk=(WbO, (gg, c)))
            k += 1

        wslotB = [0]
        wslotA = [0]

        def load_wB(q):
            s = wslotB[0] % 3; wslotB[0] += 1
            O.dma(wB[s][:], WbB[q], f"wB{s}", rk=[(WbB, (q // 4, c)) for c in range(16)])
            return wB[s]

        def load_wA(T, g):
            s = wslotA[0] % 2; wslotA[0] += 1
            O.dma(wA[s][:], T[g], f"wA{s}", rk=[(T, (g, c)) for c in range(16)])
            return wA[s]

        def rsqrt(out, in_, n, scale, eps):
            O.ts(out, in_, scale, eps, op0=ALU.mult, op1=ALU.add)
            O.act(out, out, AF.Ln)
            O.act(out, out, AF.Exp, scale=-0.5)

        def run_merged(gens):
            act = [[g, float(w), 0] for g, w in gens if g is not None]
            while act:
                a = min(act, key=lambda t: t[2] / t[1])
                try:
                    next(a[0]); a[2] += 1
                except StopIteration:
                    act.remove(a)

        def run_seq(g):
            for _ in g:
                pass

        def superblock(x_src, y_dst, tiles, W, sample, seq0=0, sbp=0):
            nt = len(tiles)
            xi = lambda ti: sbp * NT + ti

            def stage0():
                for ti, (off, Pt) in enumerate(tiles):
                    O.dma(xt[:Pt, xi(ti), :], x_src[off:off + Pt, :], f"x{xi(ti)}", wk=(xt, xi(ti)))
                    O.act(junk[:Pt, :], xt[:Pt, xi(ti), :], AF.Square, accum=sm[:Pt, 0:1], rk=[(xt, xi(ti))])
                    rsqrt(sm[:Pt, 2:3], sm[:Pt, 0:1], 1, 1.0 / D, NORM_EPS)
                    O.ts(xn[:Pt, :], xt[:Pt, xi(ti), :], sm[:Pt, 2:3], None, op0=ALU.mult, rk=[(xt, xi(ti))])
                    for c4 in range(4):
                        for j in range(4):
                            c = c4 * 4 + j
                            O.tr(ptr[:, j * Pt:(j + 1) * Pt], xn[:Pt, c * 128:(c + 1) * 128], ident_b(Pt))
                        src = ptr[:, 0:4 * Pt].rearrange("p (j t) -> p j t", t=Pt)
                        O.copy(hT[:, c4 * 4:c4 * 4 + 4, off:off + Pt], src, eng="act" if c4 % 2 else "dve", wk=(hT, ti))
                        yield

            def proj_B(q, half):
                w = load_wB(q)
                for c in range(16):
                    O.mm(pzs[half][:, 0:W], w[:, c, :], hT[:, c, 0:W], start=(c == 0), stop=(c == 15))
                return pzs[half][:, 0:W]

            def shifted(q, half, dst):
                src = proj_B(q, half)
                O.copy(zraw[:, 1:W + 1], src, eng="act")
                if not sample:
                    O.copy(zraw[:, 0:1], lastcol[:, q:q + 1])
                O.tt(diff[:, 0:W], zraw[:, 0:W], zraw[:, 1:W + 1], ALU.subtract)
                if sample:
                    for ti, (off, Pt) in enumerate(tiles):
                        O.tt(diff[:, off:off + 1], shT[:, seq0 + ti, q:q + 1], zraw[:, off + 1:off + 2], ALU.subtract)
                        O.copy(shs[:, seq0 + ti, q:q + 1], zraw[:, off + Pt:off + Pt + 1])
                else:
                    O.copy(lastcol[:, q:q + 1], zraw[:, W:W + 1])
                O.stt(dst[:, 0:W], diff[:, 0:W], pv(PV_MU, q), zraw[:, 1:W + 1], ALU.mult, ALU.add)

            def lora():
                shifted(24, 0, mixl)
                O.act(tg[0:64, 0:W], mixl[0:64, 0:W], AF.Exp, scale=-2.0)
                O.act(tg[0:64, 0:W], tg[0:64, 0:W], AF.Ln, bias=1.0)
                O.act(tg[0:64, 0:W], tg[0:64, 0:W], AF.Exp, scale=-1.0)
                O.ts(th[0:64, 0:W], tg[0:64, 0:W], 2.0, -1.0, op0=ALU.mult, op1=ALU.add)
                O.copy(th[64:128, 0:W], mixl[64:128, 0:W])
                yield

            def prep(p):
                s = p % 2
                shifted(p, 1, mixr); yield
                shifted(8 + p, 0, mixk); yield
                shifted(16 + p, 1, mixv); yield
                src = proj_B(25 + p, 0)
                O.act(tg[:, 0:W], src, AF.Exp, scale=-1.0)
                O.act(tg[:, 0:W], tg[:, 0:W], AF.Ln, bias=1.0)
                O.act(tg[:, 0:W], tg[:, 0:W], AF.Exp, scale=-1.0)
                O.tt(sgb[s][:, 0:W], tg[:, 0:W], src, ALU.mult); yield
                O.mm(pm[:, 0:W], w2a2b[0:64, p * 128:(p + 1) * 128], th[0:64, 0:W])
                O.act(sd[:, 0:W], pm[:, 0:W], AF.Exp, bias=hbias[:, p:p + 1], scale=-1.0)
                O.act(sd[:, 0:W], sd[:, 0:W], AF.Ln, bias=1.0)
                O.act(sd[:, 0:W], sd[:, 0:W], AF.Exp, scale=-1.0)
                O.mm(pm[:, 256:256 + W], w2a2b[64:128, p * 128:(p + 1) * 128], th[64:128, 0:W])
                O.act(aa[:, 0:W], pm[:, 256:256 + W], AF.Exp, bias=hbias[:, 8 + p:9 + p], scale=-1.0)
                O.act(aa[:, 0:W], aa[:, 0:W], AF.Ln, bias=1.0)
                O.act(aa[:, 0:W], aa[:, 0:W], AF.Exp, scale=-1.0); yield
                for ti, (off, Pt) in enumerate(tiles):
                    O.scan(cum[:, off:off + Pt], ones[:, off:off + Pt], sd[:, off:off + Pt], 0.0, ALU.mult, ALU.add)
                    O.ts(nb[:, ti:ti + 1], cum[:, off + Pt - 1:off + Pt], -C0, None, op0=ALU.mult)
                    O.act(gC[s][:, ti:ti + 1], cum[:, off + Pt - 1:off + Pt], AF.Exp, scale=-C0)
                    O.act(e_rem[:, off:off + Pt], cum[:, off:off + Pt], AF.Exp, scale=C0, bias=nb[:, ti:ti + 1])
                yield
                O.act(e_incl[:, 0:W], cum[:, 0:W], AF.Exp, scale=-C0)
                O.tt(excl[:, 0:W], cum[:, 0:W], sd[:, 0:W], ALU.subtract, eng="pool")
                O.act(e_excl[:, 0:W], excl[:, 0:W], AF.Exp, scale=-C0)
                O.act(e_inv[:, 0:W], cum[:, 0:W], AF.Exp, scale=C0); yield
                O.ts(kk[:, 0:W], mixk[:, 0:W], pv(PV_KK, p), None, op0=ALU.mult)
                O.tt(sq[:, 0:W], kk[:, 0:W], kk[:, 0:W], ALU.mult, eng="pool")
                O.mm(pm[:, 0:W], blockones, sq[:, 0:W])
                O.ts(rn[:, 0:W], pm[:, 0:W], 1e-24, None, op0=ALU.max)
                O.act(rn[:, 0:W], rn[:, 0:W], AF.Ln)
                O.act(rn[:, 0:W], rn[:, 0:W], AF.Exp, scale=-0.5); yield
                O.tt(kk[:, 0:W], kk[:, 0:W], rn[:, 0:W], ALU.mult)
                O.ts(tmp[:, 0:W], aa[:, 0:W], pv(PV_KA, p), omka[:, p:p + 1], op0=ALU.mult, op1=ALU.add, eng="pool")
                O.tt(kp[:, 0:W], mixk[:, 0:W], tmp[:, 0:W], ALU.mult)
                O.tt(bbv[:, 0:W], kk[:, 0:W], aa[:, 0:W], ALU.mult); yield
                O.tt(kt[s][:, 0:W], kk[:, 0:W], e_excl[:, 0:W], ALU.mult, eng="pool")
                O.tt(rt[s][:, 0:W], mixr[:, 0:W], e_incl[:, 0:W], ALU.mult)
                for hd in range(2):
                    O.stt(kh[s][hd][:, 0:W], kp[:, 0:W], pv(PV_HM, hd), e_inv[:, 0:W], ALU.mult, ALU.mult)
                yield
                for hd in range(2):
                    O.stt(bh[s][hd][:, 0:W], bbv[:, 0:W], pv(PV_HM, hd), e_inv[:, 0:W], ALU.mult, ALU.mult)
                O.tt(khg[s][:, 0:W], kp[:, 0:W], e_rem[:, 0:W], ALU.mult, eng="pool")
                O.tt(bhg[s][:, 0:W], bbv[:, 0:W], e_rem[:, 0:W], ALU.mult); yield
                O.copy(vbf[s][:, 0:W], mixv[:, 0:W], eng="act")
                O.stt(rkv[:, 0:W], mixr[:, 0:W], pv(PV_RK, p), kp[:, 0:W], ALU.mult, ALU.mult)
                O.mm(pm[:, 256:256 + W], blockones, rkv[:, 0:W])
                O.tt(bv[s][:, 0:W], pm[:, 256:256 + W], mixv[:, 0:W], ALU.mult); yield

            def inv_unit(u):
                p, ti = u // nt, u % nt
                s, par = p % 2, u % 2
                off, Pt = tiles[ti]
                sl = slice(off, off + Pt)
                w = Pt
                for j, srcb in enumerate((vbf[s], khg[s], bhg[s])):
                    O.tr(ptr[:Pt, j * 128:(j + 1) * 128], srcb[:, sl], ident_b(128))
                O.copy(tok3[par][:Pt, :, :], ptr[:Pt, 0:384].rearrange("p (j f) -> p j f", f=128), eng="act")
                yield
                for hd in range(2):
                    hs = slice(hd * 64, (hd + 1) * 64)
                    O.mm(pA[:Pt, 0:Pt], kh[s][hd][:, sl], kt[s][:, sl])
                    O.mm(pA[:Pt, 128:128 + Pt], kh[s][hd][:, sl], rt[s][:, sl])
                    O.mm(pA[:Pt, 256:256 + Pt], bh[s][hd][:, sl], kt[s][:, sl])
                    O.mm(pA[:Pt, 384:384 + Pt], bh[s][hd][:, sl], rt[s][:, sl])
                    O.mm(pI[hd][:Pt, 0:Pt], kt[s][:, sl], bh[s][hd][:, sl])
                    yield
                    X0 = XMS[par][hd][0]
                    O.tt(Asb[par][hd][:Pt, 0, :Pt], pA[:Pt, 0:Pt], m_strict(Pt), ALU.mult)
                    O.tt(Asb[par][hd][:Pt, 1, :Pt], pA[:Pt, 128:128 + Pt], m_incl(Pt), ALU.mult)
                    O.tt(Asb[par][hd][:Pt, 2, :Pt], pA[:Pt, 384:384 + Pt], m_incl(Pt), ALU.mult)
                    O.stt(X0[:Pt, w:2 * w], pA[:Pt, 256:256 + Pt], -1.0, m_strict(Pt), ALU.mult, ALU.mult, wk=(X0, "x"))
                    O.stt(X0[:Pt, 0:w], pI[hd][:Pt, 0:Pt], -1.0, m_low(Pt), ALU.mult, ALU.mult, wk=(X0, "m"))
                    yield
                nlev = int(math.log2(Pt))
                for k in range(nlev):
                    for hd in range(2):
                        cur = XMS[par][hd][k % 2]; nxt = XMS[par][hd][(k + 1) % 2]
                        Mk, Xk, Sk = cur[:Pt, 0:w], cur[:Pt, w:2 * w], cur[:Pt, 2 * w:3 * w]
                        if k == 0:
                            O.mm(pI[hd][:Pt, w:2 * w], Mk, Xk, rk=[(cur, "m"), (cur, "x")])
                            O.mm(pI[hd][:Pt, 0:w], Xk, Mk, rk=[(cur, "m"), (cur, "x")])
                            O.tt(nxt[:Pt, 2 * w:3 * w], Xk, cmat[:Pt, 0, :Pt], ALU.add, rk=[(cur, "x"), cmat], wk=(nxt, "s"))
                            O.copy(nxt[:Pt, 0:2 * w], pI[hd][:Pt, 0:2 * w], eng="act", wk=(nxt, "mx"))
                        elif k < nlev - 1:
                            O.mm(pI[hd][:Pt, w:3 * w], Mk, cur[:Pt, w:3 * w], rk=[(cur, "mx"), (cur, "s")])
                            O.mm(pI[hd][:Pt, 0:w], Xk, Mk, rk=[(cur, "mx")])
                            O.copy(nxt[:Pt, 0:2 * w], pI[hd][:Pt, 0:2 * w], eng="act", wk=(nxt, "mx"))
                            O.tt(nxt[:Pt, 2 * w:3 * w], pI[hd][:Pt, 2 * w:3 * w], Sk, ALU.add, rk=[pI[hd], (cur, "s")],
                                 wk=(nxt, "s"))
                        else:
                            O.mm(pI[hd][:Pt, 2 * w:3 * w], Mk, Sk, rk=[(cur, "mx"), (cur, "s")])
                            O.tt(TT[par][hd][:Pt, :Pt], pI[hd][:Pt, 2 * w:3 * w], Sk, ALU.add, rk=[pI[hd], (cur, "s")])
                        yield

            def state_unit(u):
                p, ti = u // nt, u % nt
                s, par = p % 2, u % 2
                off, Pt = tiles[ti]
                sl = slice(off, off + Pt)
                if sample:
                    O.dma(HsF[:], hs0_d[seq0 + ti, p], "hsin")
                    O.copy(HsB[:], HsF[:])
                    Hf, Hb = HsF, HsB
                    Hfv = lambda a, b: HsF[a, b]
                    Hbv = HsB[:, :]
                else:
                    Hf, Hb = HF, HB
                    Hfv = lambda a, b: HF[a, p, b]
                    Hbv = HB[:, p, :]
                hkey = (Hf, None if sample else p)
                hbkey = (Hb, None if sample else p)
                T3 = tok3[par]
                Vt = lambda hd: T3[:Pt, 0, hd * 64:(hd + 1) * 64]
                A_ = Asb[par]
                O.mm(pS[:Pt, 0:128], kt[s][:, sl], Hbv, start=True, stop=False, rk=[kt[s], hbkey])
                for hd in range(2):
                    O.mm(pS[:Pt, hd * 64:(hd + 1) * 64], A_[hd][:Pt, 0, :Pt], Vt(hd), start=False, stop=(hd == 1))
                O.copy(Wsb[:Pt, :], pS[:Pt, 0:128], eng="act")
                yield
                for hd in range(2):
                    O.mm(pS[:Pt, 128 + hd * 64:128 + (hd + 1) * 64], TT[par][hd][:Pt, :Pt], Wsb[:Pt, hd * 64:(hd + 1) * 64])
                O.act(nU[:Pt, :], pS[:Pt, 128:256], AF.Copy, scale=-1.0)
                yield
                O.mm(pS[:Pt, 256:384], rt[s][:, sl], Hbv, start=True, stop=False, rk=[rt[s], hbkey])
                for hd in range(2):
                    O.mm(pS[:Pt, 256 + hd * 64:256 + (hd + 1) * 64], A_[hd][:Pt, 1, :Pt], Vt(hd), start=False, stop=False)
                for hd in range(2):
                    O.mm(pS[:Pt, 256 + hd * 64:256 + (hd + 1) * 64], A_[hd][:Pt, 2, :Pt], nU[:Pt, hd * 64:(hd + 1) * 64],
                         start=False, stop=(hd == 1))
                O.mm(pS[:, 384:512], T3[:Pt, 1, :], T3[:Pt, 0, :], start=True, stop=False)
                O.mm(pS[:, 384:512], T3[:Pt, 2, :], nU[:Pt, :], start=False, stop=True)
                yield
                for hd in range(2):
                    hs = slice(hd * 64, (hd + 1) * 64)
                    O.stt(Hfv(hs, hs), Hfv(hs, hs), gC[s][hs, ti:ti + 1], pS[hs, 384 + hd * 64:384 + (hd + 1) * 64],
                          ALU.mult, ALU.add, rk=[hkey, pS, gC[s]], wk=hkey)
                if sample:
                    O.dma(hs_d[seq0 + ti, p], HsF[:], "hsout")
                else:
                    O.copy(Hbv, HF[:, p, :], eng="act", rk=[hkey], wk=hbkey)
                yield
                for hd in range(2):
                    O.bn_stats(st6[:Pt, hd, :], pS[:Pt, 256 + hd * 64:256 + (hd + 1) * 64])
                    O.bn_aggr(mv[:Pt, hd, :], st6[:Pt, hd, :])
                rsqrt(rstd2[:Pt, :], mv[:Pt, :, 1], 2, 1.0, GN_EPS)
                for hd in range(2):
                    O.ts(ynb[:Pt, hd * 64:(hd + 1) * 64], pS[:Pt, 256 + hd * 64:256 + (hd + 1) * 64], mv[:Pt, hd, 0:1],
                         rstd2[:Pt, hd:hd + 1], op0=ALU.subtract, op1=ALU.mult)
                yield
                O.tr(ptr[:, 512:512 + Pt], ynb[:Pt, :], ident_b(Pt))
                O.act(ynT[s][:, sl], ptr[:, 512:512 + Pt], AF.Identity, scale=pv(PV_GNG, p), bias=pv(PV_GNB, p))
                yield
                if ti == nt - 1:
                    O.tt(ynT[s][:, 0:W], ynT[s][:, 0:W], bv[s][:, 0:W], ALU.add)
                    O.tt(outT[:, 8 + p, 0:W], ynT[s][:, 0:W], sgb[s][:, 0:W], ALU.mult, wk=(outT, ("b", p)))
                    yield

            def a_branch():
                for gi, g in enumerate((2, 3, 0, 1, 4, 5)):
                    w = load_wA(WbA, g)
                    col = (g % 2) * 512
                    for ti, (off, Pt) in enumerate(tiles):
                        half = (gi * nt + ti) % 2
                        acc = pzs[half][:Pt, :]
                        for c in range(16):
                            O.mm(acc, hT[:, c, off:off + Pt], w[:, c, :], start=(c == 0), stop=(c == 15), rk=[(hT, ti), w])
                        if g in (2, 3):
                            O.act(gv[:Pt, ti, col:col + 512], acc, AF.Gelu, wk=(gv, ti))
                        elif g in (0, 1):
                            O.act(gus[:Pt, ti, col:col + 512], acc, AF.Gelu, wk=(gus, ti))
                        else:
                            O.act(gat[:Pt, :], acc, AF.Tanh, scale=0.5)
                            O.stt(gat2[:Pt, :], gat[:Pt, :], 1.0, acc, ALU.add, ALU.mult)
                            O.stt(gus[:Pt, ti, col:col + 512], gat2[:Pt, :], 0.5, gus[:Pt, ti, col:col + 512], ALU.mult,
                                  ALU.mult, rk=[(gus, ti), gat2], wk=(gus, ti))
                        yield
                    if g == 3:
                        for ti, (off, Pt) in enumerate(tiles):
                            for j in range(2):
                                O.bn_stats(st6a[:Pt, j, :], gv[:Pt, ti, j * 512:(j + 1) * 512], rk=[(gv, ti)])
                            O.bn_aggr(mva[:Pt, :], st6a[:Pt, :, :].rearrange("p a b -> p (a b)"))
                            rsqrt(sma[:Pt, 1:2], mva[:Pt, 1:2], 1, 1.0, LN_EPS)
                            O.ts(gv[:Pt, ti, :], gv[:Pt, ti, :], mva[:Pt, 0:1], sma[:Pt, 1:2], op0=ALU.subtract, op1=ALU.mult,
                                 rk=[(gv, ti)], wk=(gv, ti))
                            O.tt(gv[:Pt, ti, :], gv[:Pt, ti, :], lnbc[:Pt, 0, :], ALU.mult, rk=[(gv, ti), lnbc], wk=(gv, ti))
                            yield
                            if sample:
                                O.tt(gv[:Pt, ti, :], gv[:Pt, ti, :], lnbc[:Pt, 1, :], ALU.add, rk=[(gv, ti), lnbc], wk=(gv, ti))
                                O.dma(vn_d[seq0 * 64 + off:seq0 * 64 + off + Pt, :], gv[:Pt, ti, :], f"vn{ti}",
                                      rk=[(gv, ti)])
                                O.copy(vnb[:Pt, ti, :], gv[:Pt, ti, :], eng="act", rk=[(gv, ti)], wk=(vnb, ti))
                            else:
                                O.tt(vnb[:Pt, ti, :], gv[:Pt, ti, :], lnbc[:Pt, 1, :], ALU.add, rk=[(gv, ti), lnbc],
                                     wk=(vnb, ti))
                            yield
                for ti, (off, Pt) in enumerate(tiles):
                    for h in range(8):
                        O.mm(pzs[h // 4][:Pt, (h % 4) * 128:(h % 4 + 1) * 128], wsTb[:Pt, h, :Pt],
                             vnb[:Pt, ti, h * 128:(h + 1) * 128], rk=[wsTb, (vnb, ti)])
                    for h in range(8):
                        O.stt(oa[:Pt, h * 128:(h + 1) * 128], pzs[h // 4][:Pt, (h % 4) * 128:(h % 4 + 1) * 128],
                              pv(PV_SGUB, h)[:Pt, :], gus[:Pt, ti, h * 128:(h + 1) * 128], ALU.add, ALU.mult,
                              rk=[pzs[h // 4], (gus, ti), pvec])
                    yield
                    for c4 in range(2):
                        for j in range(4):
                            c = c4 * 4 + j
                            O.tr(ptr[:, j * Pt:(j + 1) * Pt], oa[:Pt, c * 128:(c + 1) * 128], ident_b(Pt))
                        src = ptr[:, 0:4 * Pt].rearrange("p (j t) -> p j t", t=Pt)
                        O.copy(outT[:, c4 * 4:c4 * 4 + 4, off:off + Pt], src, eng="act" if c4 % 2 else "dve",
                               wk=(outT, ("a", ti, c4)))
                        yield

            def out_proj():
                for g in range(4):
                    w = load_wA(WbO, g)
                    for ti, (off, Pt) in enumerate(tiles):
                        half = (g * nt + ti) % 2
                        acc = pzs[half][:Pt, :]
                        for c in range(16):
                            O.mm(acc, outT[:, c, off:off + Pt], w[:, c, :], start=(c == 0), stop=(c == 15), rk=[outT, w])
                        O.tt(xt[:Pt, xi(ti), g * 512:(g + 1) * 512], acc, xt[:Pt, xi(ti), g * 512:(g + 1) * 512], ALU.add,
                             rk=[pzs[half], (xt, xi(ti))], wk=(xt, xi(ti)))
                        yield
                for ti, (off, Pt) in enumerate(tiles):
                    O.act(junk[:Pt, :], xt[:Pt, xi(ti), :], AF.Square, accum=sm[:Pt, 5:6], rk=[(xt, xi(ti))])
                    rsqrt(sm[:Pt, 7:8], sm[:Pt, 5:6], 1, 1.0 / D, NORM_EPS)
                    O.stt(xt[:Pt, xi(ti), :], xt[:Pt, xi(ti), :], sm[:Pt, 7:8], fing[:Pt, :], ALU.mult, ALU.mult,
                          rk=[(xt, xi(ti)), fing], wk=(xt, xi(ti)))
                    O.dma(y_dst[off:off + Pt, :], xt[:Pt, xi(ti), :], f"y{xi(ti)}", rk=[(xt, xi(ti))])
                    yield

            run_seq(stage0())
            run_seq(lora())
            run_seq(prep(0))
            NU = 8 * nt
            ab = a_branch()
            n_inv = 3 + 2 * int(math.log2(tiles[0][1]))
            if nt == 2:
                run_merged([(inv_unit(0), n_inv), (ab_slice(ab, 4), 4)])
                for u in range(NU):
                    p = u // nt
                    gens = [(state_unit(u), 7)]
                    if u + 1 < NU:
                        gens.append((inv_unit(u + 1), n_inv))
                    if u % nt == 0 and p + 1 < 8:
                        gens.append((prep(p + 1), 13))
                    else:
                        gens.append((ab_slice(ab, 5), 5))
                    run_merged(gens)
            else:
                for u in range(NU):
                    if u > 0 and u % nt == 0:
                        run_seq(prep(u // nt))
                    run_merged([(inv_unit(u), n_inv), (ab_slice(ab, 4), 4)])
                    run_seq(state_unit(u))
            run_seq(ab)
            run_seq(out_proj())

        def ab_slice(g, n):
            for _ in range(n):
                try:
                    next(g)
                except StopIteration:
                    return
                yield

        for sbi in range(NP // SBW):
            superblock(xp_d[sbi * SBW:(sbi + 1) * SBW, :], yp_d[sbi * SBW:(sbi + 1) * SBW, :],
                       [(i * 128, 128) for i in range(SBW // 128)], SBW, False, 0, sbi % 2)
        O.dma(shp_d, lastcol[:], "misc_out")
        O.dma(hp_d.rearrange("q p f -> p q f"), HF[:], "misc_out")
        for s0 in range(0, NS, NT):
            n = min(NT, NS - s0)
            superblock(xs_d[s0 * 64:(s0 + n) * 64, :], ys_d[s0 * 64:(s0 + n) * 64, :], [(i * 64, 64) for i in range(n)], n * 64, True, s0,
                       (NP // SBW + s0 // NT) % 2)
        if NS > 0:
            O.dma(shs_d, shs[:], "misc_out")
        out_tags = ["misc_out", "hsout"] + [f"y{i}" for i in range(2 * NT)] + [f"vn{i}" for i in range(NT)]
        P.emit_all(out_tags, eng="sp")
        n_ops = len(P.ops)
    return nc, n_ops
N_CORES = 8
_NC_CACHE = {}


def _consts():
    i = np.arange(128)
    ident = np.eye(128, dtype=np.float32)
    incl = (i[:, None] <= i[None, :]).astype(np.float32)
    strict = (i[:, None] < i[None, :]).astype(np.float32)
    low = (i[:, None] > i[None, :]).astype(np.float32)
    blk = ((i[:, None] // 64) == (i[None, :] // 64)).astype(np.float32)
    return np.ascontiguousarray(np.stack([ident, incl, strict, low, blk], axis=1))


def _cols(v, n):
    return np.ascontiguousarray(np.asarray(v, np.float32).reshape(n, 128).T)


def _shared_inputs(norm_g, w_in, w_out, sgu_ln_g, sgu_ln_b, sgu_w, sgu_b, shift_mu, w0, w2, a0, a2, k_k, k_a, r_k,
                   gn_g, gn_b, final_g):
    pvec = np.concatenate([
        _cols(shift_mu[0], 25), _cols(w0[0], 8), _cols(a0[0], 8), _cols(k_k[0], 8), _cols(k_a[0], 8),
        _cols(np.asarray(r_k[0]).reshape(-1), 8), _cols(gn_g[0], 8), _cols(gn_b[0], 8),
        np.ascontiguousarray(np.asarray(sgu_b[0], np.float32).T),
        np.stack([(np.arange(128) < 64), (np.arange(128) >= 64)], axis=1).astype(np.float32)], axis=1)
    return {
        "w_in": np.ascontiguousarray(np.asarray(w_in[0], np.float32)),
        "w_out": np.ascontiguousarray(np.asarray(w_out[0], np.float32)),
        "i_normg": _cols(norm_g[0], 16),
        "i_cmat": _consts(),
        "i_lnbc": np.ascontiguousarray(np.broadcast_to(
            np.stack([np.asarray(sgu_ln_g[0], np.float32), np.asarray(sgu_ln_b[0], np.float32)])[None], (128, 2, DA))),
        "i_fing": np.ascontiguousarray(np.broadcast_to(np.asarray(final_g, np.float32)[None], (128, D))),
        "i_pvec": np.ascontiguousarray(pvec.astype(np.float32)),
        "i_wsT": np.ascontiguousarray(np.transpose(np.asarray(sgu_w[0], np.float32), (2, 0, 1))),
        "i_w2a2": np.ascontiguousarray(np.concatenate([np.asarray(w2[0], np.float32), np.asarray(a2[0], np.float32)], 0)),
    }


def _h_blockdiag(S):
    n = S.shape[0]
    out = np.zeros((n, 8, 128, 128), np.float32)
    Ht = np.transpose(S, (0, 1, 3, 2))
    out[:, :, 0:64, 0:64] = Ht[:, 0::2]
    out[:, :, 64:128, 64:128] = Ht[:, 1::2]
    return out


def _h_unblock(Hbd):
    n = Hbd.shape[0]
    S = np.zeros((n, 16, 64, 64), np.float32)
    S[:, 0::2] = np.transpose(Hbd[:, :, 0:64, 0:64], (0, 1, 3, 2))
    S[:, 1::2] = np.transpose(Hbd[:, :, 64:128, 64:128], (0, 1, 3, 2))
    return S


def run_layout(x_prompt, x_sample, state_b_wkv, state_b_shift, shared, n_cores, NP, NS):
    key = (NP, NS)
    if key not in _NC_CACHE:
        _NC_CACHE[key] = build(NP, NS)[0]
    nc = _NC_CACHE[key]
    B = x_prompt.shape[0]
    in_maps = []
    for c in range(n_cores):
        m = dict(shared)
        m["xp"] = np.ascontiguousarray(x_prompt[c], np.float32) if c < B else np.zeros((NP, D), np.float32)
        sl = slice(c * NS, (c + 1) * NS)
        m["xs"] = np.ascontiguousarray(np.asarray(x_sample[sl], np.float32).reshape(NS * 64, D))
        m["hs0"] = _h_blockdiag(np.asarray(state_b_wkv[0, sl], np.float32))
        sh = np.asarray(state_b_shift[0, sl, 0, :], np.float32).reshape(NS, 25, 128)
        m["shiftT"] = np.ascontiguousarray(np.transpose(sh, (2, 0, 1)))
        in_maps.append(m)
    res = run_bass_kernel_spmd(nc, in_maps, core_ids=list(range(n_cores)))
    R = res.results
    y_prompt = np.stack([R[c]["yp"] for c in range(B)]).astype(np.float32)
    y_sample = np.concatenate([R[c]["ys"].reshape(NS, 64, D) for c in range(n_cores)]).astype(np.float32)
    wkv_p = np.concatenate([_h_unblock(R[c]["hp_out"][None]) for c in range(B)])[None]
    shp = np.stack([R[c]["shp_out"].T.reshape(1, NSH) for c in range(B)])[None]
    wkv_s = np.concatenate([_h_unblock(R[c]["hs_out"]) for c in range(n_cores)])[None]
    shs = np.concatenate([np.transpose(R[c]["shs_out"], (1, 2, 0)).reshape(NS, 1, NSH) for c in range(n_cores)])[None]
    vn = np.concatenate([R[c]["vn_out"].reshape(NS, 64, DA) for c in range(n_cores)])[None]
    return (y_prompt, y_sample, wkv_p.astype(np.float32), shp.astype(np.float32), wkv_s.astype(np.float32),
            shs.astype(np.float32), vn.astype(np.float32))


def kernel(x_prompt, x_sample, state_b_wkv, state_b_shift, norm_g, w_in, w_out, sgu_ln_g, sgu_ln_b,
           sgu_w, sgu_b, shift_mu, w0, w2, a0, a2, k_k, k_a, r_k, gn_g, gn_b, final_g):
    shared = _shared_inputs(norm_g, w_in, w_out, sgu_ln_g, sgu_ln_b, sgu_w, sgu_b, shift_mu, w0, w2, a0, a2,
                            k_k, k_a, r_k, gn_g, gn_b, final_g)
    x_prompt = np.asarray(x_prompt); x_sample = np.asarray(x_sample)
    return run_layout(x_prompt, x_sample, np.asarray(state_b_wkv), np.asarray(state_b_shift), shared,
                      N_CORES, x_prompt.shape[1], x_sample.shape[0] // N_CORES)
```

```python
from concourse.bass_utils import run_bass_kernel_spmd
import sys
import numpy as np
import concourse.bass as bass
import concourse.mybir as mybir

F32 = mybir.dt.float32
BF16 = mybir.dt.bfloat16
AF = mybir.ActivationFunctionType
ALU = mybir.AluOpType

SCHED_SEED = None
SCHED_JITTER_NS = 400.0
SEM_CAP = 16384


def _key(x):
    sub = None
    if isinstance(x, tuple):
        x, sub = x
    t = getattr(x, "tensor", x)
    return (t.name, sub)


class Prog:
    COMPUTE = ("pe", "act", "dve", "pool")

    def __init__(self, nc, stack):
        self.nc = nc
        self.stack = stack
        self.ops = []
        self.state = {}
        self.eng = {"pe": nc.tensor, "act": nc.scalar, "dve": nc.vector, "pool": nc.gpsimd, "sp": nc.sync}
        self.psum_names = set()
        self.bank_last = {}

    def sb(self, name, shape, dt):
        return self.stack.enter_context(self.nc.sbuf_tensor(name, list(shape), dt))

    def ps(self, name, shape, dt):
        self.psum_names.add(name)
        return self.stack.enter_context(self.nc.psum_tensor(name, list(shape), dt))

    def _states(self, key, create=True):
        name, sub = key
        d = self.state.setdefault(name, {})
        if sub is None:
            if None not in d:
                d[None] = [None, []]
            return list(d.values())
        out = []
        if sub not in d:
            d[sub] = [None, []]
        out.append(d[sub])
        if None in d:
            out.append(d[None])
        return out

    def op(self, eng, fn, reads=(), writes=(), dma_tag=None, wait_all=False, cost=100.0, lat=0.0):
        idx = len(self.ops)
        deps = {}
        rk = [_key(r) for r in reads if r is not None and not isinstance(r, (int, float))]
        wk = [_key(w) for w in writes if w is not None]
        for k in rk:
            for st in self._states(k):
                if st[0] is not None:
                    deps.setdefault(st[0], "raw")
        for k in wk:
            for st in self._states(k):
                if st[0] is not None:
                    deps.setdefault(st[0], "waw")
                for r in st[1]:
                    if r != idx:
                        deps.setdefault(r, "war")
        for name in {k[0] for k in rk + wk if k[0] in self.psum_names}:
            bl = self.bank_last.setdefault(name, {})
            for e2, i2 in bl.items():
                if e2 != eng:
                    deps.setdefault(i2, "bank")
                else:
                    deps.setdefault(i2, "order")
            bl[eng] = idx
        for k in rk:
            name, sub = k
            if sub is None:
                for st in self._states(k):
                    st[1].append(idx)
            else:
                self.state[name][sub][1].append(idx)
        for k in wk:
            name, sub = k
            if sub is None:
                d = self.state[name]
                for s in list(d.keys()):
                    d[s] = [idx, []]
            else:
                self.state[name][sub] = [idx, []]
        o = dict(eng=eng, fn=fn, deps=deps, signal=False, dma_tag=dma_tag, wait_all=wait_all, val=None,
                 cost=float(cost), lat=float(lat), line=sys._getframe(2).f_lineno)
        need = []
        for d, kind in deps.items():
            p = self.ops[d]
            if p["dma_tag"] is None and dma_tag is None and p["eng"] == eng:
                if eng == "pe":
                    continue
                if kind == "order":
                    continue
            need.append(d)
            p["signal"] = True
        o["need"] = need
        if dma_tag is not None:
            o["signal"] = True
        self.ops.append(o)
        return idx


    def schedule(self, window=64, slack=0.0, use_cp=True):
        import bisect
        ops = self.ops
        n = len(ops)
        succ = [[] for _ in range(n)]
        ndeps = [0] * n
        for i, o in enumerate(ops):
            ndeps[i] = len(o["deps"])
            for d in o["deps"]:
                succ[d].append(i)
        ready = [0.0] * n
        blv = [0.0] * n
        for i in range(n - 1, -1, -1):
            m = 0.0
            for sidx in succ[i]:
                if blv[sidx] > m:
                    m = blv[sidx]
            blv[i] = m + ops[i]["cost"] + ops[i]["lat"] + 100.0
        if SCHED_SEED is not None:
            rs = np.random.RandomState(SCHED_SEED)
            jit = rs.standard_normal(n) * SCHED_JITTER_NS
            prio = [(-(blv[i] + jit[i]), i) for i in range(n)]
        else:
            prio = [(-blv[i], i) for i in range(n)] if use_cp else [(i, i) for i in range(n)]
        engs = sorted({o["eng"] for o in ops})
        free = {e: 0.0 for e in engs}
        rel = {e: [] for e in engs}
        for i, o in enumerate(ops):
            if ndeps[i] == 0:
                bisect.insort(rel[o["eng"]], (prio[i], i))
        order = []
        done = 0
        last_on = {}
        rdy_src = {}
        cur_tbl = [None]
        TBL = 1300.0

        def eff(e, t, i):
            st_ = max(t, ready[i])
            if e == "act":
                tb = ops[i].get("tbl")
                if tb is not None and tb != cur_tbl[0]:
                    st_ += TBL
            return st_
        while done < n:
            best = None
            for e in engs:
                cand = rel[e]
                if not cand:
                    continue
                t = free[e]
                lim = [c_[1] for c_ in cand[:window]]
                tmin = min(eff(e, t, i) for i in lim)
                for i in lim:
                    if eff(e, t, i) <= tmin + slack:
                        pick = i
                        break
                stt = eff(e, t, pick)
                if best is None or stt < best[0]:
                    best = (stt, e, pick)
            stt, e, i = best
            rel[e].remove((prio[i], i))
            o = ops[i]
            o["t0"] = stt
            o["bind"] = ("res", last_on.get(e)) if free[e] >= ready[i] else ("dep", rdy_src.get(i))
            last_on[e] = i
            if e == "act" and o.get("tbl") is not None:
                cur_tbl[0] = o["tbl"]
            fin = stt + o["cost"] + o["lat"]
            free[e] = stt + o["cost"]
            order.append(i)
            done += 1
            for sidx in succ[i]:
                so = ops[sidx]
                if o["dma_tag"] is not None or so["eng"] != e:
                    l = 180.0
                elif e == "pe":
                    l = 0.0
                else:
                    l = 60.0 if o["deps"] and ops[sidx]["deps"].get(i) in ("raw", "waw") else 0.0
                r = fin + l
                if r > ready[sidx]:
                    ready[sidx] = r
                    rdy_src[sidx] = i
                ndeps[sidx] -= 1
                if ndeps[sidx] == 0:
                    bisect.insort(rel[so["eng"]], (prio[sidx], sidx))
        self.est_ns = max(free.values())
        self._sched_dbg = (ops, order)
        remap = {old: new for new, old in enumerate(order)}
        new_ops = []
        for old in order:
            o = ops[old]
            o["deps"] = {remap[d]: k for d, k in o["deps"].items()}
            o["need"] = [remap[d] for d in o["need"]]
            new_ops.append(o)
        self.ops = new_ops

    def emit(self):
        nc = self.nc
        ops = self.ops
        seqc = {}
        for o in ops:
            key = ("dma", o["dma_tag"]) if o["dma_tag"] is not None else ("eng", o["eng"])
            seqc[key] = seqc.get(key, 0) + 1
            o["key"] = key
            o["seq"] = seqc[key]
        needed = set()
        for o in ops:
            tgt = {}
            for d in o["need"]:
                p = ops[d]
                k = p["key"]
                if k not in tgt or p["seq"] > ops[tgt[k]]["seq"]:
                    tgt[k] = d
            o["tgt"] = tgt
            needed.update(tgt.values())
        counters = {}
        for i, o in enumerate(ops):
            o["signal"] = (o["dma_tag"] is not None) or (i in needed)
            if not o["signal"]:
                continue
            inc = 16 if o["dma_tag"] is not None else 1
            counters[o["key"]] = counters.get(o["key"], 0) + inc
            o["val"] = counters[o["key"]]
        totals = dict(counters)
        hw = {}

        def hwsem(key, k):
            if (key, k) not in hw:
                hw[(key, k)] = self.stack.enter_context(nc.semaphore(f"s_{key[0]}_{key[1]}_{k}"))
            return hw[(key, k)]

        waited = {}
        n_wait = 0
        for o in ops:
            e = self.eng[o["eng"]]
            ws = []
            for key, d in o["tgt"].items():
                p = ops[d]
                v = totals[key] if p["wait_all"] else p["val"]
                if waited.get((o["eng"], key), 0) >= v:
                    continue
                waited[(o["eng"], key)] = v
                k = (v - 1) // SEM_CAP
                ws.append((hwsem(key, k), v - k * SEM_CAP))
            embed = ws.pop() if (ws and o["dma_tag"] is None) else None
            for (sm_, lv) in ws:
                e.wait_ge(sm_, lv)
                n_wait += 1
            ins = o["fn"](e)
            if embed is not None:
                ins._wait_ge(embed[0], embed[1])
            if o["signal"]:
                v = o["val"]
                k = (v - 1) // SEM_CAP
                inc = 16 if o["dma_tag"] is not None else 1
                ins.then_inc(hwsem(o["key"], k), inc)
        self.n_wait = n_wait
        self.n_signal = sum(1 for o in ops if o["signal"])
        return totals, hwsem

    def finish(self, out_tags, eng="sp"):
        totals, hwsem = self._fin
        e = self.eng[eng]
        for t in out_tags:
            key = ("dma", t)
            if key in totals:
                v = totals[key]
                k = (v - 1) // SEM_CAP
                e.wait_ge(hwsem(key, k), v - k * SEM_CAP)

    def emit_all(self, out_tags, eng="sp", sched=True):
        if sched:
            self.schedule()
        self._fin = self.emit()
        self.finish(out_tags, eng)


def _nfree(ap):
    n = 1
    for d in list(ap.shape)[1:]:
        n *= int(d)
    return n


class Ops:
    def __init__(self, prog):
        self.p = prog

    def _ew(self, out, *ins):
        n = _nfree(out)
        ps = any((getattr(getattr(x, "tensor", None), "name", None) in self.p.psum_names) for x in ins if x is not None
                 and not isinstance(x, (int, float)))
        return (160.0 + 1.05 * n) if ps else (100.0 + 0.8 * n)

    @staticmethod
    def _aps(*xs):
        return [x for x in xs if x is not None and not isinstance(x, (int, float))]

    def mm(self, out, lhsT, rhs, start=True, stop=True, wk=None, rk=()):
        self.p.op("pe", lambda e: e.matmul(out, lhsT, rhs, start=start, stop=stop),
                  reads=[lhsT, rhs] if not rk else list(rk), writes=[wk if wk is not None else out],
                  cost=8.0 + 0.43 * max(_nfree(rhs), 64), lat=120.0)

    def tr(self, out, in_, ident, wk=None, rk=()):
        self.p.op("pe", lambda e: e.transpose(out, in_, ident),
                  reads=[in_, ident] if not rk else list(rk) + [ident], writes=[wk if wk is not None else out],
                  cost=8.0 + 0.43 * max(_nfree(in_), 64), lat=120.0)

    def act(self, out, in_, func, bias=None, scale=None, accum=None, eng="act", wk=None, rk=None):
        kw = {}
        if bias is not None:
            kw["bias"] = bias
        if scale is not None:
            kw["scale"] = scale
        if accum is not None:
            kw["accum_out"] = accum
        reads = self._aps(in_, bias, scale) if rk is None else list(rk) + self._aps(bias, scale)
        writes = [wk if wk is not None else out] + ([accum] if accum is not None else [])
        tbl = {AF.Exp: "ln_exp", AF.Ln: "ln_exp", AF.Gelu: "gelu", AF.Tanh: "gelu", AF.Silu: "silu", AF.Sigmoid: "sig",
               AF.Sqrt: "sqrt"}.get(func)
        i = self.p.op("act", lambda e: e.activation(out, in_, func, **kw), reads=reads, writes=writes,
                      cost=self._ew(out, in_) + (90.0 if accum is not None else 0.0))
        self.p.ops[i]["tbl"] = tbl

    def tt(self, out, a, b, op, eng="dve", wk=None, rk=None):
        self.p.op(eng, lambda e: e.tensor_tensor(out, a, b, op),
                  reads=[a, b] if rk is None else list(rk), writes=[wk if wk is not None else out],
                  cost=self._ew(out, a, b) * (2.0 if eng == "pool" else 1.0))

    def ts(self, out, a, s1, s2=None, op0=ALU.mult, op1=None, eng="dve", wk=None, rk=None, accum=None):
        kw = {}
        if op1 is not None:
            kw["op1"] = op1
        if accum is not None:
            kw["accum_out"] = accum
        reads = self._aps(a, s1, s2) if rk is None else list(rk) + self._aps(s1, s2)
        writes = [wk if wk is not None else out] + ([accum] if accum is not None else [])
        self.p.op(eng, lambda e: e.tensor_scalar(out, a, s1, s2, op0, **kw), reads=reads, writes=writes,
                  cost=self._ew(out, a) * (2.0 if eng == "pool" else 1.0))

    def stt(self, out, in0, scalar, in1, op0, op1, eng="dve", wk=None, rk=None):
        reads = self._aps(in0, scalar, in1) if rk is None else list(rk) + self._aps(scalar)
        self.p.op(eng, lambda e: e.scalar_tensor_tensor(out, in0, scalar, in1, op0, op1),
                  reads=reads, writes=[wk if wk is not None else out],
                  cost=self._ew(out, in0, in1) * (2.0 if eng == "pool" else 1.0))

    def copy(self, out, in_, eng="dve", wk=None, rk=None):
        if eng == "act":
            self.p.op("act", lambda e: e.copy(out, in_), reads=[in_] if rk is None else list(rk),
                      writes=[wk if wk is not None else out], cost=self._ew(out, in_))
        else:
            self.p.op(eng, lambda e: e.tensor_copy(out, in_), reads=[in_] if rk is None else list(rk),
                      writes=[wk if wk is not None else out], cost=self._ew(out, in_) * (2.0 if eng == "pool" else 1.0))

    def memset(self, out, val, eng="dve", wk=None):
        self.p.op(eng, lambda e: e.memset(out, val), reads=[], writes=[wk if wk is not None else out],
                  cost=60.0 + 0.5 * _nfree(out))

    def recip(self, out, in_, wk=None):
        self.p.op("dve", lambda e: e.reciprocal(out, in_), reads=[in_], writes=[wk if wk is not None else out],
                  cost=self._ew(out, in_) * 1.5)

    def scan(self, out, d0, d1, initial, op0, op1, wk=None, rk=None):
        reads = self._aps(d0, d1, initial) if rk is None else list(rk)
        self.p.op("dve", lambda e: e.tensor_tensor_scan(out, d0, d1, initial, op0, op1),
                  reads=reads, writes=[wk if wk is not None else out], cost=100.0 + 2.1 * _nfree(out))

    def bn_stats(self, out, in_, wk=None, rk=None):
        self.p.op("dve", lambda e: e.bn_stats(out, in_), reads=[in_] if rk is None else list(rk),
                  writes=[wk if wk is not None else out], cost=self._ew(in_, in_))

    def bn_aggr(self, out, in_, wk=None):
        self.p.op("dve", lambda e: e.bn_aggr(out, in_), reads=[in_], writes=[wk if wk is not None else out])

    def dma(self, out, in_, tag, q="sp", wk=None, rk=None, wait_all=False):
        nbytes = 128 * _nfree(out) * (2 if out.dtype == BF16 else 4)
        self.p.op(q, lambda e: e.dma_start(out=out, in_=in_), reads=[in_] if rk is None else list(rk),
                  writes=[wk if wk is not None else out], dma_tag=tag, wait_all=wait_all,
                  cost=70.0, lat=2000.0 + nbytes / 150.0)
import contextlib
import math

D = 2048
DA = 1024
NSH = 3200
DP = 7296
NORM_EPS = 1e-6
LN_EPS = 1e-5
GN_EPS = 64e-5
C0 = math.exp(-0.5)
NQ = 25
PV_MU, PV_W0, PV_A0, PV_KK, PV_KA, PV_RK, PV_GNG, PV_GNB, PV_SGUB = 0, 25, 33, 41, 49, 57, 65, 73, 81
PV_HM = 89
PV_N = 91


def build(NP, NS, SBW=256):
    nc = bass.Bass("TRN2", target_bir_lowering=False)
    st = contextlib.ExitStack()
    with st:
        P = Prog(nc, st)
        O = Ops(P)
        NST = NS * 64
        assert NP % SBW == 0
        din = lambda n, s, dt=F32: nc.dram_tensor(n, list(s), dt, kind="ExternalInput").ap()
        dout = lambda n, s, dt=F32: nc.dram_tensor(n, list(s), dt, kind="ExternalOutput").ap()
        xp_d = din("xp", [NP, D]); xs_d = din("xs", [NST, D])
        hs0_d = din("hs0", [NS, 8, 128, 128]); shT_d = din("shiftT", [128, NS, NQ])
        win_d = din("w_in", [D, DP]); wout_d = din("w_out", [D, D])
        normg_d = din("i_normg", [128, 16]); cmat_d = din("i_cmat", [128, 5, 128])
        lnbc_d = din("i_lnbc", [128, 2, DA]); fing_d = din("i_fing", [128, D]); pvec_d = din("i_pvec", [128, PV_N])
        wsT_d = din("i_wsT", [128, 8, 128]); w2a2_d = din("i_w2a2", [128, DA])
        yp_d = dout("yp", [NP, D]); ys_d = dout("ys", [NST, D])
        hp_d = dout("hp_out", [8, 128, 128]); hs_d = dout("hs_out", [NS, 8, 128, 128])
        shp_d = dout("shp_out", [128, NQ]); shs_d = dout("shs_out", [128, NS, NQ])
        vn_d = dout("vn_out", [NST, DA])
        WbA = nc.dram_tensor("WbA", [6, 128, 16, 512], BF16, kind="Internal").ap()
        WbB = nc.dram_tensor("WbB", [33, 128, 16, 128], BF16, kind="Internal").ap()
        WbO = nc.dram_tensor("WbO", [4, 128, 16, 512], BF16, kind="Internal").ap()

        NT = SBW // 128
        normg = P.sb("normg", [128, 16], F32)
        cmat = P.sb("cmat", [128, 5, 128], F32)
        cmb = P.sb("cmb", [128, 5, 128], BF16)
        lnbc = P.sb("lnbc", [128, 2, DA], F32)
        fing = P.sb("fing", [128, D], F32)
        pvec = P.sb("pvec", [128, PV_N], F32)
        omka = P.sb("omka", [128, 8], F32)
        hbias = P.sb("hbias", [128, 16], F32)
        mhalf = P.sb("mhalf", [128, SBW], F32)
        tg = P.sb("tg", [128, SBW], F32)
        wsTb = P.sb("wsTb", [128, 8, 128], BF16)
        w2a2b = P.sb("w2a2b", [128, DA], BF16)
        HF = P.sb("HF", [128, 8, 128], F32)
        HB = P.sb("HB", [128, 8, 128], BF16)
        HsF = P.sb("HsF", [128, 128], F32)
        HsB = P.sb("HsB", [128, 128], BF16)
        lastcol = P.sb("lastcol", [128, NQ], F32)
        shT = P.sb("shT", [128, NS, NQ], F32)
        shs = P.sb("shs", [128, NS, NQ], F32)
        xt = P.sb("xt", [128, 2 * NT, D], F32)
        junk = P.sb("junk", [128, D], BF16)
        xn = P.sb("xn", [128, D], BF16)
        hT = P.sb("hT", [128, 16, SBW], BF16)
        outT = P.sb("outT", [128, 16, SBW], BF16)
        wB = [P.sb(f"wB{i}", [128, 16, 128], BF16) for i in range(3)]
        wA = [P.sb(f"wA{i}", [128, 16, 512], BF16) for i in range(2)]
        gv = P.sb("gv", [128, NT, DA], F32)
        gus = P.sb("gus", [128, NT, DA], BF16)
        gat = P.sb("gat", [128, 512], F32)
        gat2 = P.sb("gat2", [128, 512], BF16)
        vnb = P.sb("vnb", [128, NT, DA], BF16)
        oa = P.sb("oa", [128, DA], BF16)
        st6 = P.sb("st6", [128, 2, 6], F32)
        mv = P.sb("mv", [128, 2, 2], F32)
        sm = P.sb("sm", [128, 8], F32)
        f32t = lambda n: P.sb(n, [128, SBW], F32)
        zraw = P.sb("zraw", [128, SBW + 1], F32)
        diff = f32t("diff")
        mixl, mixr, mixk, mixv = f32t("mixl"), f32t("mixr"), f32t("mixk"), f32t("mixv")
        sgb = [f32t(f"sgb{i}") for i in range(2)]; bv = [f32t(f"bv{i}") for i in range(2)]; ynT = [f32t(f"ynT{i}") for i in range(2)]
        sd, aa, cum, excl = f32t("sd"), f32t("aa"), f32t("cum"), f32t("excl")
        e_incl, e_excl, e_inv, e_rem = f32t("e_incl"), f32t("e_excl"), f32t("e_inv"), f32t("e_rem")
        kk, sq, rn, tmp, kp, bbv, rkv = (f32t(n) for n in ("kk", "sq", "rn", "tmp", "kp", "bbv", "rkv"))
        ones = f32t("ones")
        nb = P.sb("nb", [128, NT], F32)
        gC = [P.sb(f"gC{i}", [128, NT], F32) for i in range(2)]
        th = P.sb("th", [128, SBW], BF16)
        b16 = lambda n: P.sb(n, [128, SBW], BF16)
        kt, rt, khg, bhg, vbf = ([b16(f"{n}{i}") for i in range(2)] for n in ("kt", "rt", "khg", "bhg", "vbf"))
        kh = [[b16(f"kh{i}_{h}") for h in range(2)] for i in range(2)]
        bh = [[b16(f"bh{i}_{h}") for h in range(2)] for i in range(2)]
        tok3 = [P.sb(f"tok3_{i}", [128, 3, 128], BF16) for i in range(2)]
        Asb = [[P.sb(f"Asb{i}_{h}", [128, 3, 128], BF16) for h in range(2)] for i in range(2)]
        XMS = [[[P.sb(f"XMS{i}_{h}_{j}", [128, 384], BF16) for j in range(2)] for h in range(2)] for i in range(2)]
        TT = [[P.sb(f"TT{i}_{h}", [128, 128], BF16) for h in range(2)] for i in range(2)]
        st6a = P.sb("st6a", [128, 2, 6], F32)
        mva = P.sb("mva", [128, 2], F32)
        sma = P.sb("sma", [128, 2], F32)
        Wsb = P.sb("Wsb", [128, 128], BF16)
        nU = P.sb("nU", [128, 128], BF16)
        ynb = P.sb("ynb", [128, 128], BF16)
        rstd2 = P.sb("rstd2", [128, 2], F32)
        pzs = [P.ps(f"pz{i}", [128, 512], F32) for i in range(2)]
        pA = P.ps("pA", [128, 512], F32)
        pI = [P.ps(f"pI{h}", [128, 512], F32) for h in range(2)]
        pS = P.ps("pS", [128, 512], F32)
        pm = P.ps("pm", [128, 512], F32)
        ptr = P.ps("ptr", [128, 1024], BF16)

        ident_b = lambda n: cmb[:n, 0, :n]
        m_incl = lambda n: cmat[:n, 1, :n]
        m_strict = lambda n: cmat[:n, 2, :n]
        m_low = lambda n: cmat[:n, 3, :n]
        blockones = cmat[:, 4, :]
        pv = lambda base, j: pvec[:, base + j:base + j + 1]

        cl = lambda out, in_: O.dma(out, in_, "const", wait_all=True)
        cl(normg[:], normg_d); cl(cmat[:], cmat_d); cl(lnbc[:], lnbc_d); cl(fing[:], fing_d); cl(pvec[:], pvec_d)
        wsTf = gv[:, 0, :].rearrange("p (h i) -> p h i", i=128)
        w2a2f = gv[:, 1, :]
        O.dma(wsTf, wsT_d, "const", wait_all=True, wk=(gv, 0)); O.dma(w2a2f, w2a2_d, "const", wait_all=True, wk=(gv, 1)); cl(shT[:], shT_d)
        O.copy(cmb[:], cmat[:])
        O.copy(w2a2b[:], w2a2f, eng="act", rk=[(gv, 1)])
        O.ts(omka[:], pvec[:, PV_KA:PV_KA + 8], -1.0, 1.0, op0=ALU.mult, op1=ALU.add)
        O.ts(hbias[:], pvec[:, PV_W0:PV_W0 + 16], -1.0, None, op0=ALU.mult)
        O.memset(mhalf[:], -0.5)
        for h in range(8):
            O.tt(wsTb[:, h, :], wsTf[:, h, :], cmat[:, 1, :], ALU.mult, rk=[(gv, 0), cmat])
        O.memset(HF[:], 0.0); O.memset(HB[:], 0.0); O.memset(lastcol[:], 0.0); O.memset(ones[:], 1.0)
        O.memset(zraw[:, 0:1], 0.0)

        pieces = [(7168, 128), (3072, 2048), (5120, 2048), (0, 2048), (2048, 1024)]
        stg_f = [xt[:, 0, :], xt[:, 1, :]]
        stg_b = [junk, xn]
        k = 0
        for pi, (c0, w) in enumerate(pieces):
            for c in range(16):
                sf = stg_f[k % 2]; sbf = stg_b[k % 2]
                O.dma(sf[:, 0:w], win_d[c * 128:(c + 1) * 128, c0:c0 + w], f"stgf{k % 2}", wk=(xt, k % 2))
                if k % 2 == 0:
                    O.ts(sbf[:, 0:w], sf[:, 0:w], normg[:, c:c + 1], None, op0=ALU.mult, rk=[(xt, k % 2)])
                else:
                    O.act(sbf[:, 0:w], sf[:, 0:w], AF.Copy, scale=normg[:, c:c + 1], rk=[(xt, k % 2)])
                if c0 < 3072:
                    g0 = c0 // 512; ng = w // 512
                    for gg in range(ng):
                        O.dma(WbA[g0 + gg, :, c, :], sbf[:, gg * 512:(gg + 1) * 512], f"stgb{k % 2}_{gg}", wk=(WbA, (g0 + gg, c)))
                else:
                    q0 = (c0 - 3072) // 128; nq = w // 128
                    for qq in range(0, nq, 4):
                        nn = min(4, nq - qq)
                        O.dma(WbB[q0 + qq:q0 + qq + nn, :, c, :].rearrange("q p n -> p q n"),
                              sbf[:, qq * 128:(qq + nn) * 128].rearrange("p (q n) -> p q n", n=128), f"stgb{k % 2}_{qq // 4}",
                              wk=(WbB, ((q0 + qq) // 4, c)))
                k += 1
        for c in range(16):
            sf = stg_f[k % 2]; sbf = stg_b[k % 2]
            O.dma(sf[:, 0:2048], wout_d[c * 128:(c + 1) * 128, :], f"stgf{k % 2}", wk=(xt, k % 2))
            O.copy(sbf[:, 0:2048], sf[:, 0:2048], eng="act" if k % 2 else "dve", rk=[(xt, k % 2)])
            for gg in range(4):
                O.dma(WbO[gg, :, c, :], sbf[:, gg * 512:(gg + 1) * 512], f"stgb{k % 2}_{gg}", wk=(WbO, (gg, c)))
            k += 1

        wslotB = [0]
        wslotA = [0]

        def load_wB(q):
            s = wslotB[0] % 3; wslotB[0] += 1
            O.dma(wB[s][:], WbB[q], f"wB{s}", rk=[(WbB, (q // 4, c)) for c in range(16)])
            return wB[s]

        def load_wA(T, g):
            s = wslotA[0] % 2; wslotA[0] += 1
            O.dma(wA[s][:], T[g], f"wA{s}", rk=[(T, (g, c)) for c in range(16)])
            return wA[s]

        def rsqrt(out, in_, n, scale, eps):
            O.ts(out, in_, scale, eps, op0=ALU.mult, op1=ALU.add)
            O.act(out, out, AF.Ln)
            O.act(out, out, AF.Exp, scale=-0.5)

        def run_merged(gens):
            act = [[g, float(w), 0] for g, w in gens if g is not None]
            while act:
                a = min(act, key=lambda t: t[2] / t[1])
                try:
                    next(a[0]); a[2] += 1
                except StopIteration:
                    act.remove(a)

        def run_seq(g):
            for _ in g:
                pass

        def superblock(x_src, y_dst, tiles, W, sample, seq0=0, sbp=0):
            nt = len(tiles)
            xi = lambda ti: sbp * NT + ti

            def stage0():
                for ti, (off, Pt) in enumerate(tiles):
                    O.dma(xt[:Pt, xi(ti), :], x_src[off:off + Pt, :], f"x{xi(ti)}", wk=(xt, xi(ti)))
                    O.act(junk[:Pt, :], xt[:Pt, xi(ti), :], AF.Square, accum=sm[:Pt, 0:1], rk=[(xt, xi(ti))])
                    rsqrt(sm[:Pt, 2:3], sm[:Pt, 0:1], 1, 1.0 / D, NORM_EPS)
                    O.ts(xn[:Pt, :], xt[:Pt, xi(ti), :], sm[:Pt, 2:3], None, op0=ALU.mult, rk=[(xt, xi(ti))])
                    for c4 in range(4):
                        for j in range(4):
                            c = c4 * 4 + j
                            O.tr(ptr[:, j * Pt:(j + 1) * Pt], xn[:Pt, c * 128:(c + 1) * 128], ident_b(Pt))
                        src = ptr[:, 0:4 * Pt].rearrange("p (j t) -> p j t", t=Pt)
                        O.copy(hT[:, c4 * 4:c4 * 4 + 4, off:off + Pt], src, eng="act" if c4 % 2 else "dve", wk=(hT, ti))
                        yield

            def proj_B(q, half):
                w = load_wB(q)
                for c in range(16):
                    O.mm(pzs[half][:, 0:W], w[:, c, :], hT[:, c, 0:W], start=(c == 0), stop=(c == 15))
                return pzs[half][:, 0:W]

            def shifted(q, half, dst):
                src = proj_B(q, half)
                O.copy(zraw[:, 1:W + 1], src, eng="act")
                if not sample:
                    O.copy(zraw[:, 0:1], lastcol[:, q:q + 1])
                O.tt(diff[:, 0:W], zraw[:, 0:W], zraw[:, 1:W + 1], ALU.subtract)
                if sample:
                    for ti, (off, Pt) in enumerate(tiles):
                        O.tt(diff[:, off:off + 1], shT[:, seq0 + ti, q:q + 1], zraw[:, off + 1:off + 2], ALU.subtract)
                        O.copy(shs[:, seq0 + ti, q:q + 1], zraw[:, off + Pt:off + Pt + 1])
                else:
                    O.copy(lastcol[:, q:q + 1], zraw[:, W:W + 1])
                O.stt(dst[:, 0:W], diff[:, 0:W], pv(PV_MU, q), zraw[:, 1:W + 1], ALU.mult, ALU.add)

            def lora():
                shifted(24, 0, mixl)
                O.act(tg[0:64, 0:W], mixl[0:64, 0:W], AF.Exp, scale=-2.0)
                O.act(tg[0:64, 0:W], tg[0:64, 0:W], AF.Ln, bias=1.0)
                O.act(tg[0:64, 0:W], tg[0:64, 0:W], AF.Exp, scale=-1.0)
                O.ts(th[0:64, 0:W], tg[0:64, 0:W], 2.0, -1.0, op0=ALU.mult, op1=ALU.add)
                O.copy(th[64:128, 0:W], mixl[64:128, 0:W])
                yield

            def prep(p):
                s = p % 2
                shifted(p, 1, mixr); yield
                shifted(8 + p, 0, mixk); yield
                shifted(16 + p, 1, mixv); yield
                src = proj_B(25 + p, 0)
                O.act(tg[:, 0:W], src, AF.Exp, scale=-1.0)
                O.act(tg[:, 0:W], tg[:, 0:W], AF.Ln, bias=1.0)
                O.act(tg[:, 0:W], tg[:, 0:W], AF.Exp, scale=-1.0)
                O.tt(sgb[s][:, 0:W], tg[:, 0:W], src, ALU.mult); yield
                O.mm(pm[:, 0:W], w2a2b[0:64, p * 128:(p + 1) * 128], th[0:64, 0:W])
                O.act(sd[:, 0:W], pm[:, 0:W], AF.Exp, bias=hbias[:, p:p + 1], scale=-1.0)
                O.act(sd[:, 0:W], sd[:, 0:W], AF.Ln, bias=1.0)
                O.act(sd[:, 0:W], sd[:, 0:W], AF.Exp, scale=-1.0)
                O.mm(pm[:, 256:256 + W], w2a2b[64:128, p * 128:(p + 1) * 128], th[64:128, 0:W])
                O.act(aa[:, 0:W], pm[:, 256:256 + W], AF.Exp, bias=hbias[:, 8 + p:9 + p], scale=-1.0)
                O.act(aa[:, 0:W], aa[:, 0:W], AF.Ln, bias=1.0)
                O.act(aa[:, 0:W], aa[:, 0:W], AF.Exp, scale=-1.0); yield
                for ti, (off, Pt) in enumerate(tiles):
                    O.scan(cum[:, off:off + Pt], ones[:, off:off + Pt], sd[:, off:off + Pt], 0.0, ALU.mult, ALU.add)
                    O.ts(nb[:, ti:ti + 1], cum[:, off + Pt - 1:off + Pt], -C0, None, op0=ALU.mult)
                    O.act(gC[s][:, ti:ti + 1], cum[:, off + Pt - 1:off + Pt], AF.Exp, scale=-C0)
                    O.act(e_rem[:, off:off + Pt], cum[:, off:off + Pt], AF.Exp, scale=C0, bias=nb[:, ti:ti + 1])
                yield
                O.act(e_incl[:, 0:W], cum[:, 0:W], AF.Exp, scale=-C0)
                O.tt(excl[:, 0:W], cum[:, 0:W], sd[:, 0:W], ALU.subtract, eng="pool")
                O.act(e_excl[:, 0:W], excl[:, 0:W], AF.Exp, scale=-C0)
                O.act(e_inv[:, 0:W], cum[:, 0:W], AF.Exp, scale=C0); yield
                O.ts(kk[:, 0:W], mixk[:, 0:W], pv(PV_KK, p), None, op0=ALU.mult)
                O.tt(sq[:, 0:W], kk[:, 0:W], kk[:, 0:W], ALU.mult, eng="pool")
                O.mm(pm[:, 0:W], blockones, sq[:, 0:W])
                O.ts(rn[:, 0:W], pm[:, 0:W], 1e-24, None, op0=ALU.max)
                O.act(rn[:, 0:W], rn[:, 0:W], AF.Ln)
                O.act(rn[:, 0:W], rn[:, 0:W], AF.Exp, scale=-0.5); yield
                O.tt(kk[:, 0:W], kk[:, 0:W], rn[:, 0:W], ALU.mult)
                O.ts(tmp[:, 0:W], aa[:, 0:W], pv(PV_KA, p), omka[:, p:p + 1], op0=ALU.mult, op1=ALU.add, eng="pool")
                O.tt(kp[:, 0:W], mixk[:, 0:W], tmp[:, 0:W], ALU.mult)
                O.tt(bbv[:, 0:W], kk[:, 0:W], aa[:, 0:W], ALU.mult); yield
                O.tt(kt[s][:, 0:W], kk[:, 0:W], e_excl[:, 0:W], ALU.mult, eng="pool")
                O.tt(rt[s][:, 0:W], mixr[:, 0:W], e_incl[:, 0:W], ALU.mult)
                for hd in range(2):
                    O.stt(kh[s][hd][:, 0:W], kp[:, 0:W], pv(PV_HM, hd), e_inv[:, 0:W], ALU.mult, ALU.mult)
                yield
                for hd in range(2):
                    O.stt(bh[s][hd][:, 0:W], bbv[:, 0:W], pv(PV_HM, hd), e_inv[:, 0:W], ALU.mult, ALU.mult)
                O.tt(khg[s][:, 0:W], kp[:, 0:W], e_rem[:, 0:W], ALU.mult, eng="pool")
                O.tt(bhg[s][:, 0:W], bbv[:, 0:W], e_rem[:, 0:W], ALU.mult); yield
                O.copy(vbf[s][:, 0:W], mixv[:, 0:W], eng="act")
                O.stt(rkv[:, 0:W], mixr[:, 0:W], pv(PV_RK, p), kp[:, 0:W], ALU.mult, ALU.mult)
                O.mm(pm[:, 256:256 + W], blockones, rkv[:, 0:W])
                O.tt(bv[s][:, 0:W], pm[:, 256:256 + W], mixv[:, 0:W], ALU.mult); yield

            def inv_unit(u):
                p, ti = u // nt, u % nt
                s, par = p % 2, u % 2
                off, Pt = tiles[ti]
                sl = slice(off, off + Pt)
                w = Pt
                for j, srcb in enumerate((vbf[s], khg[s], bhg[s])):
                    O.tr(ptr[:Pt, j * 128:(j + 1) * 128], srcb[:, sl], ident_b(128))
                O.copy(tok3[par][:Pt, :, :], ptr[:Pt, 0:384].rearrange("p (j f) -> p j f", f=128), eng="act")
                yield
                for hd in range(2):
                    hs = slice(hd * 64, (hd + 1) * 64)
                    O.mm(pA[:Pt, 0:Pt], kh[s][hd][:, sl], kt[s][:, sl])
                    O.mm(pA[:Pt, 128:128 + Pt], kh[s][hd][:, sl], rt[s][:, sl])
                    O.mm(pA[:Pt, 256:256 + Pt], bh[s][hd][:, sl], kt[s][:, sl])
                    O.mm(pA[:Pt, 384:384 + Pt], bh[s][hd][:, sl], rt[s][:, sl])
                    O.mm(pI[hd][:Pt, 0:Pt], kt[s][:, sl], bh[s][hd][:, sl])
                    yield
                    X0 = XMS[par][hd][0]
                    O.tt(Asb[par][hd][:Pt, 0, :Pt], pA[:Pt, 0:Pt], m_strict(Pt), ALU.mult)
                    O.tt(Asb[par][hd][:Pt, 1, :Pt], pA[:Pt, 128:128 + Pt], m_incl(Pt), ALU.mult)
                    O.tt(Asb[par][hd][:Pt, 2, :Pt], pA[:Pt, 384:384 + Pt], m_incl(Pt), ALU.mult)
                    O.stt(X0[:Pt, w:2 * w], pA[:Pt, 256:256 + Pt], -1.0, m_strict(Pt), ALU.mult, ALU.mult, wk=(X0, "x"))
                    O.stt(X0[:Pt, 0:w], pI[hd][:Pt, 0:Pt], -1.0, m_low(Pt), ALU.mult, ALU.mult, wk=(X0, "m"))
                    yield
                nlev = int(math.log2(Pt))
                for k in range(nlev):
                    for hd in range(2):
                        cur = XMS[par][hd][k % 2]; nxt = XMS[par][hd][(k + 1) % 2]
                        Mk, Xk, Sk = cur[:Pt, 0:w], cur[:Pt, w:2 * w], cur[:Pt, 2 * w:3 * w]
                        if k == 0:
                            O.mm(pI[hd][:Pt, w:2 * w], Mk, Xk, rk=[(cur, "m"), (cur, "x")])
                            O.mm(pI[hd][:Pt, 0:w], Xk, Mk, rk=[(cur, "m"), (cur, "x")])
                            O.tt(nxt[:Pt, 2 * w:3 * w], Xk, cmat[:Pt, 0, :Pt], ALU.add, rk=[(cur, "x"), cmat], wk=(nxt, "s"))
                            O.copy(nxt[:Pt, 0:2 * w], pI[hd][:Pt, 0:2 * w], eng="act", wk=(nxt, "mx"))
                        elif k < nlev - 1:
                            O.mm(pI[hd][:Pt, w:3 * w], Mk, cur[:Pt, w:3 * w], rk=[(cur, "mx"), (cur, "s")])
                            O.mm(pI[hd][:Pt, 0:w], Xk, Mk, rk=[(cur, "mx")])
                            O.copy(nxt[:Pt, 0:2 * w], pI[hd][:Pt, 0:2 * w], eng="act", wk=(nxt, "mx"))
                            O.tt(nxt[:Pt, 2 * w:3 * w], pI[hd][:Pt, 2 * w:3 * w], Sk, ALU.add, rk=[pI[hd], (cur, "s")],
                                 wk=(nxt, "s"))
                        else:
                            O.mm(pI[hd][:Pt, 2 * w:3 * w], Mk, Sk, rk=[(cur, "mx"), (cur, "s")])
                            O.tt(TT[par][hd][:Pt, :Pt], pI[hd][:Pt, 2 * w:3 * w], Sk, ALU.add, rk=[pI[hd], (cur, "s")])
                        yield

            def state_unit(u):
                p, ti = u // nt, u % nt
                s, par = p % 2, u % 2
                off, Pt = tiles[ti]
                sl = slice(off, off + Pt)
                if sample:
                    O.dma(HsF[:], hs0_d[seq0 + ti, p], "hsin")
                    O.copy(HsB[:], HsF[:])
                    Hf, Hb = HsF, HsB
                    Hfv = lambda a, b: HsF[a, b]
                    Hbv = HsB[:, :]
                else:
                    Hf, Hb = HF, HB
                    Hfv = lambda a, b: HF[a, p, b]
                    Hbv = HB[:, p, :]
                hkey = (Hf, None if sample else p)
                hbkey = (Hb, None if sample else p)
                T3 = tok3[par]
                Vt = lambda hd: T3[:Pt, 0, hd * 64:(hd + 1) * 64]
                A_ = Asb[par]
                O.mm(pS[:Pt, 0:128], kt[s][:, sl], Hbv, start=True, stop=False, rk=[kt[s], hbkey])
                for hd in range(2):
                    O.mm(pS[:Pt, hd * 64:(hd + 1) * 64], A_[hd][:Pt, 0, :Pt], Vt(hd), start=False, stop=(hd == 1))
                O.copy(Wsb[:Pt, :], pS[:Pt, 0:128], eng="act")
                yield
                for hd in range(2):
                    O.mm(pS[:Pt, 128 + hd * 64:128 + (hd + 1) * 64], TT[par][hd][:Pt, :Pt], Wsb[:Pt, hd * 64:(hd + 1) * 64])
                O.act(nU[:Pt, :], pS[:Pt, 128:256], AF.Copy, scale=-1.0)
                yield
                O.mm(pS[:Pt, 256:384], rt[s][:, sl], Hbv, start=True, stop=False, rk=[rt[s], hbkey])
                for hd in range(2):
                    O.mm(pS[:Pt, 256 + hd * 64:256 + (hd + 1) * 64], A_[hd][:Pt, 1, :Pt], Vt(hd), start=False, stop=False)
                for hd in range(2):
                    O.mm(pS[:Pt, 256 + hd * 64:256 + (hd + 1) * 64], A_[hd][:Pt, 2, :Pt], nU[:Pt, hd * 64:(hd + 1) * 64],
                         start=False, stop=(hd == 1))
                O.mm(pS[:, 384:512], T3[:Pt, 1, :], T3[:Pt, 0, :], start=True, stop=False)
                O.mm(pS[:, 384:512], T3[:Pt, 2, :], nU[:Pt, :], start=False, stop=True)
                yield
                for hd in range(2):
                    hs = slice(hd * 64, (hd + 1) * 64)
                    O.stt(Hfv(hs, hs), Hfv(hs, hs), gC[s][hs, ti:ti + 1], pS[hs, 384 + hd * 64:384 + (hd + 1) * 64],
                          ALU.mult, ALU.add, rk=[hkey, pS, gC[s]], wk=hkey)
                if sample:
                    O.dma(hs_d[seq0 + ti, p], HsF[:], "hsout")
                else:
                    O.copy(Hbv, HF[:, p, :], eng="act", rk=[hkey], wk=hbkey)
                yield
                for hd in range(2):
                    O.bn_stats(st6[:Pt, hd, :], pS[:Pt, 256 + hd * 64:256 + (hd + 1) * 64])
                    O.bn_aggr(mv[:Pt, hd, :], st6[:Pt, hd, :])
                rsqrt(rstd2[:Pt, :], mv[:Pt, :, 1], 2, 1.0, GN_EPS)
                for hd in range(2):
                    O.ts(ynb[:Pt, hd * 64:(hd + 1) * 64], pS[:Pt, 256 + hd * 64:256 + (hd + 1) * 64], mv[:Pt, hd, 0:1],
                         rstd2[:Pt, hd:hd + 1], op0=ALU.subtract, op1=ALU.mult)
                yield
                O.tr(ptr[:, 512:512 + Pt], ynb[:Pt, :], ident_b(Pt))
                O.act(ynT[s][:, sl], ptr[:, 512:512 + Pt], AF.Identity, scale=pv(PV_GNG, p), bias=pv(PV_GNB, p))
                yield
                if ti == nt - 1:
                    O.tt(ynT[s][:, 0:W], ynT[s][:, 0:W], bv[s][:, 0:W], ALU.add)
                    O.tt(outT[:, 8 + p, 0:W], ynT[s][:, 0:W], sgb[s][:, 0:W], ALU.mult, wk=(outT, ("b", p)))
                    yield

            def a_branch():
                for gi, g in enumerate((2, 3, 0, 1, 4, 5)):
                    w = load_wA(WbA, g)
                    col = (g % 2) * 512
                    for ti, (off, Pt) in enumerate(tiles):
                        half = (gi * nt + ti) % 2
                        acc = pzs[half][:Pt, :]
                        for c in range(16):
                            O.mm(acc, hT[:, c, off:off + Pt], w[:, c, :], start=(c == 0), stop=(c == 15), rk=[(hT, ti), w])
                        if g in (2, 3):
                            O.act(gv[:Pt, ti, col:col + 512], acc, AF.Gelu, wk=(gv, ti))
                        elif g in (0, 1):
                            O.act(gus[:Pt, ti, col:col + 512], acc, AF.Gelu, wk=(gus, ti))
                        else:
                            O.act(gat[:Pt, :], acc, AF.Tanh, scale=0.5)
                            O.stt(gat2[:Pt, :], gat[:Pt, :], 1.0, acc, ALU.add, ALU.mult)
                            O.stt(gus[:Pt, ti, col:col + 512], gat2[:Pt, :], 0.5, gus[:Pt, ti, col:col + 512], ALU.mult,
                                  ALU.mult, rk=[(gus, ti), gat2], wk=(gus, ti))
                        yield
                    if g == 3:
                        for ti, (off, Pt) in enumerate(tiles):
                            for j in range(2):
                                O.bn_stats(st6a[:Pt, j, :], gv[:Pt, ti, j * 512:(j + 1) * 512], rk=[(gv, ti)])
                            O.bn_aggr(mva[:Pt, :], st6a[:Pt, :, :].rearrange("p a b -> p (a b)"))
                            rsqrt(sma[:Pt, 1:2], mva[:Pt, 1:2], 1, 1.0, LN_EPS)
                            O.ts(gv[:Pt, ti, :], gv[:Pt, ti, :], mva[:Pt, 0:1], sma[:Pt, 1:2], op0=ALU.subtract, op1=ALU.mult,
                                 rk=[(gv, ti)], wk=(gv, ti))
                            O.tt(gv[:Pt, ti, :], gv[:Pt, ti, :], lnbc[:Pt, 0, :], ALU.mult, rk=[(gv, ti), lnbc], wk=(gv, ti))
                            yield
                            if sample:
                                O.tt(gv[:Pt, ti, :], gv[:Pt, ti, :], lnbc[:Pt, 1, :], ALU.add, rk=[(gv, ti), lnbc], wk=(gv, ti))
                                O.dma(vn_d[seq0 * 64 + off:seq0 * 64 + off + Pt, :], gv[:Pt, ti, :], f"vn{ti}",
                                      rk=[(gv, ti)])
                                O.copy(vnb[:Pt, ti, :], gv[:Pt, ti, :], eng="act", rk=[(gv, ti)], wk=(vnb, ti))
                            else:
                                O.tt(vnb[:Pt, ti, :], gv[:Pt, ti, :], lnbc[:Pt, 1, :], ALU.add, rk=[(gv, ti), lnbc],
                                     wk=(vnb, ti))
                            yield
                for ti, (off, Pt) in enumerate(tiles):
                    for h in range(8):
                        O.mm(pzs[h // 4][:Pt, (h % 4) * 128:(h % 4 + 1) * 128], wsTb[:Pt, h, :Pt],
                             vnb[:Pt, ti, h * 128:(h + 1) * 128], rk=[wsTb, (vnb, ti)])
                    for h in range(8):
                        O.stt(oa[:Pt, h * 128:(h + 1) * 128], pzs[h // 4][:Pt, (h % 4) * 128:(h % 4 + 1) * 128],
                              pv(PV_SGUB, h)[:Pt, :], gus[:Pt, ti, h * 128:(h + 1) * 128], ALU.add, ALU.mult,
                              rk=[pzs[h // 4], (gus, ti), pvec])
                    yield
                    for c4 in range(2):
                        for j in range(4):
                            c = c4 * 4 + j
                            O.tr(ptr[:, j * Pt:(j + 1) * Pt], oa[:Pt, c * 128:(c + 1) * 128], ident_b(Pt))
                        src = ptr[:, 0:4 * Pt].rearrange("p (j t) -> p j t", t=Pt)
                        O.copy(outT[:, c4 * 4:c4 * 4 + 4, off:off + Pt], src, eng="act" if c4 % 2 else "dve",
                               wk=(outT, ("a", ti, c4)))
                        yield

            def out_proj():
                for g in range(4):
                    w = load_wA(WbO, g)
                    for ti, (off, Pt) in enumerate(tiles):
                        half = (g * nt + ti) % 2
                        acc = pzs[half][:Pt, :]
                        for c in range(16):
                            O.mm(acc, outT[:, c, off:off + Pt], w[:, c, :], start=(c == 0), stop=(c == 15), rk=[outT, w])
                        O.tt(xt[:Pt, xi(ti), g * 512:(g + 1) * 512], acc, xt[:Pt, xi(ti), g * 512:(g + 1) * 512], ALU.add,
                             rk=[pzs[half], (xt, xi(ti))], wk=(xt, xi(ti)))
                        yield
                for ti, (off, Pt) in enumerate(tiles):
                    O.act(junk[:Pt, :], xt[:Pt, xi(ti), :], AF.Square, accum=sm[:Pt, 5:6], rk=[(xt, xi(ti))])
                    rsqrt(sm[:Pt, 7:8], sm[:Pt, 5:6], 1, 1.0 / D, NORM_EPS)
                    O.stt(xt[:Pt, xi(ti), :], xt[:Pt, xi(ti), :], sm[:Pt, 7:8], fing[:Pt, :], ALU.mult, ALU.mult,
                          rk=[(xt, xi(ti)), fing], wk=(xt, xi(ti)))
                    O.dma(y_dst[off:off + Pt, :], xt[:Pt, xi(ti), :], f"y{xi(ti)}", rk=[(xt, xi(ti))])
                    yield

            run_seq(stage0())
            run_seq(lora())
            run_seq(prep(0))
            NU = 8 * nt
            ab = a_branch()
            n_inv = 3 + 2 * int(math.log2(tiles[0][1]))
            if nt == 2:
                run_merged([(inv_unit(0), n_inv), (ab_slice(ab, 4), 4)])
                for u in range(NU):
                    p = u // nt
                    gens = [(state_unit(u), 7)]
                    if u + 1 < NU:
                        gens.append((inv_unit(u + 1), n_inv))
                    if u % nt == 0 and p + 1 < 8:
                        gens.append((prep(p + 1), 13))
                    else:
                        gens.append((ab_slice(ab, 5), 5))
                    run_merged(gens)
            else:
                for u in range(NU):
                    if u > 0 and u % nt == 0:
                        run_seq(prep(u // nt))
                    run_merged([(inv_unit(u), n_inv), (ab_slice(ab, 4), 4)])
                    run_seq(state_unit(u))
            run_seq(ab)
            run_seq(out_proj())

        def ab_slice(g, n):
            for _ in range(n):
                try:
                    next(g)
                except StopIteration:
                    return
                yield

        for sbi in range(NP // SBW):
            superblock(xp_d[sbi * SBW:(sbi + 1) * SBW, :], yp_d[sbi * SBW:(sbi + 1) * SBW, :],
                       [(i * 128, 128) for i in range(SBW // 128)], SBW, False, 0, sbi % 2)
        O.dma(shp_d, lastcol[:], "misc_out")
        O.dma(hp_d.rearrange("q p f -> p q f"), HF[:], "misc_out")
        for s0 in range(0, NS, NT):
            n = min(NT, NS - s0)
            superblock(xs_d[s0 * 64:(s0 + n) * 64, :], ys_d[s0 * 64:(s0 + n) * 64, :], [(i * 64, 64) for i in range(n)], n * 64, True, s0,
                       (NP // SBW + s0 // NT) % 2)
        if NS > 0:
            O.dma(shs_d, shs[:], "misc_out")
        out_tags = ["misc_out", "hsout"] + [f"y{i}" for i in range(2 * NT)] + [f"vn{i}" for i in range(NT)]
        P.emit_all(out_tags, eng="sp")
        n_ops = len(P.ops)
    return nc, n_ops
N_CORES = 8
_NC_CACHE = {}


def _consts():
    i = np.arange(128)
    ident = np.eye(128, dtype=np.float32)
    incl = (i[:, None] <= i[None, :]).astype(np.float32)
    strict = (i[:, None] < i[None, :]).astype(np.float32)
    low = (i[:, None] > i[None, :]).astype(np.float32)
    blk = ((i[:, None] // 64) == (i[None, :] // 64)).astype(np.float32)
    return np.ascontiguousarray(np.stack([ident, incl, strict, low, blk], axis=1))


def _cols(v, n):
    return np.ascontiguousarray(np.asarray(v, np.float32).reshape(n, 128).T)


def _shared_inputs(norm_g, w_in, w_out, sgu_ln_g, sgu_ln_b, sgu_w, sgu_b, shift_mu, w0, w2, a0, a2, k_k, k_a, r_k,
                   gn_g, gn_b, final_g):
    pvec = np.concatenate([
        _cols(shift_mu[0], 25), _cols(w0[0], 8), _cols(a0[0], 8), _cols(k_k[0], 8), _cols(k_a[0], 8),
        _cols(np.asarray(r_k[0]).reshape(-1), 8), _cols(gn_g[0], 8), _cols(gn_b[0], 8),
        np.ascontiguousarray(np.asarray(sgu_b[0], np.float32).T),
        np.stack([(np.arange(128) < 64), (np.arange(128) >= 64)], axis=1).astype(np.float32)], axis=1)
    return {
        "w_in": np.ascontiguousarray(np.asarray(w_in[0], np.float32)),
        "w_out": np.ascontiguousarray(np.asarray(w_out[0], np.float32)),
        "i_normg": _cols(norm_g[0], 16),
        "i_cmat": _consts(),
        "i_lnbc": np.ascontiguousarray(np.broadcast_to(
            np.stack([np.asarray(sgu_ln_g[0], np.float32), np.asarray(sgu_ln_b[0], np.float32)])[None], (128, 2, DA))),
        "i_fing": np.ascontiguousarray(np.broadcast_to(np.asarray(final_g, np.float32)[None], (128, D))),
        "i_pvec": np.ascontiguousarray(pvec.astype(np.float32)),
        "i_wsT": np.ascontiguousarray(np.transpose(np.asarray(sgu_w[0], np.float32), (2, 0, 1))),
        "i_w2a2": np.ascontiguousarray(np.concatenate([np.asarray(w2[0], np.float32), np.asarray(a2[0], np.float32)], 0)),
    }


def _h_blockdiag(S):
    n = S.shape[0]
    out = np.zeros((n, 8, 128, 128), np.float32)
    Ht = np.transpose(S, (0, 1, 3, 2))
    out[:, :, 0:64, 0:64] = Ht[:, 0::2]
    out[:, :, 64:128, 64:128] = Ht[:, 1::2]
    return out


def _h_unblock(Hbd):
    n = Hbd.shape[0]
    S = np.zeros((n, 16, 64, 64), np.float32)
    S[:, 0::2] = np.transpose(Hbd[:, :, 0:64, 0:64], (0, 1, 3, 2))
    S[:, 1::2] = np.transpose(Hbd[:, :, 64:128, 64:128], (0, 1, 3, 2))
    return S


def run_layout(x_prompt, x_sample, state_b_wkv, state_b_shift, shared, n_cores, NP, NS):
    key = (NP, NS)
    if key not in _NC_CACHE:
        _NC_CACHE[key] = build(NP, NS)[0]
    nc = _NC_CACHE[key]
    B = x_prompt.shape[0]
    in_maps = []
    for c in range(n_cores):
        m = dict(shared)
        m["xp"] = np.ascontiguousarray(x_prompt[c], np.float32) if c < B else np.zeros((NP, D), np.float32)
        sl = slice(c * NS, (c + 1) * NS)
        m["xs"] = np.ascontiguousarray(np.asarray(x_sample[sl], np.float32).reshape(NS * 64, D))
        m["hs0"] = _h_blockdiag(np.asarray(state_b_wkv[0, sl], np.float32))
        sh = np.asarray(state_b_shift[0, sl, 0, :], np.float32).reshape(NS, 25, 128)
        m["shiftT"] = np.ascontiguousarray(np.transpose(sh, (2, 0, 1)))
        in_maps.append(m)
    res = run_bass_kernel_spmd(nc, in_maps, core_ids=list(range(n_cores)))
    R = res.results
    y_prompt = np.stack([R[c]["yp"] for c in range(B)]).astype(np.float32)
    y_sample = np.concatenate([R[c]["ys"].reshape(NS, 64, D) for c in range(n_cores)]).astype(np.float32)
    wkv_p = np.concatenate([_h_unblock(R[c]["hp_out"][None]) for c in range(B)])[None]
    shp = np.stack([R[c]["shp_out"].T.reshape(1, NSH) for c in range(B)])[None]
    wkv_s = np.concatenate([_h_unblock(R[c]["hs_out"]) for c in range(n_cores)])[None]
    shs = np.concatenate([np.transpose(R[c]["shs_out"], (1, 2, 0)).reshape(NS, 1, NSH) for c in range(n_cores)])[None]
    vn = np.concatenate([R[c]["vn_out"].reshape(NS, 64, DA) for c in range(n_cores)])[None]
    return (y_prompt, y_sample, wkv_p.astype(np.float32), shp.astype(np.float32), wkv_s.astype(np.float32),
            shs.astype(np.float32), vn.astype(np.float32))


def kernel(x_prompt, x_sample, state_b_wkv, state_b_shift, norm_g, w_in, w_out, sgu_ln_g, sgu_ln_b,
           sgu_w, sgu_b, shift_mu, w0, w2, a0, a2, k_k, k_a, r_k, gn_g, gn_b, final_g):
    shared = _shared_inputs(norm_g, w_in, w_out, sgu_ln_g, sgu_ln_b, sgu_w, sgu_b, shift_mu, w0, w2, a0, a2,
                            k_k, k_a, r_k, gn_g, gn_b, final_g)
    x_prompt = np.asarray(x_prompt); x_sample = np.asarray(x_sample)
    return run_layout(x_prompt, x_sample, np.asarray(state_b_wkv), np.asarray(state_b_shift), shared,
                      N_CORES, x_prompt.shape[1], x_sample.shape[0] // N_CORES)
```
